# Optimizing a Trainium2 kernel written in Bass

```python
import jax, jax.numpy as jnp
from jax import lax
import numpy as np

D_MODEL = 1024
BATCH = 2
SEQ = 16384
DEPTH = 1
DEC_BATCH = 2
DEC_SEQ = 8192
PAST_LEN = 128

HEAD_DIM = 64
N_HEADS_A = 8
N_KV_A = 2
N_HEADS_B = 8
N_KV_B = 2
WIDTH_A = N_HEADS_A * HEAD_DIM
WIDTH_B = N_HEADS_B * HEAD_DIM
MIX_WIDTH = WIDTH_A + WIDTH_B
KV_A = N_KV_A * HEAD_DIM
KV_B = N_KV_B * HEAD_DIM
IN_WIDTH = WIDTH_A + 2 * KV_A + WIDTH_B + 2 * KV_B
BLOCK = 128
WINDOW = 128
GRID_W = 64
ROPE_THETA = 10000.0
D_FF = 2816
CONV_WIDTH = 3
EPS = 1e-6

kernel_name = "hymba_axial_rope_window_sink_convffn_encoder"


def rmsnorm(x, g):
    xf = x.astype(jnp.float32)
    y = xf * lax.rsqrt(jnp.mean(xf * xf, axis=-1, keepdims=True) + EPS)
    return (y * g.astype(jnp.float32)).astype(x.dtype)


def axial_rope_tables(S):
    rows = S // GRID_W
    row = jnp.repeat(jnp.arange(rows, dtype=jnp.float32), GRID_W)
    col = jnp.tile(jnp.arange(GRID_W, dtype=jnp.float32), rows)
    axis_dim = HEAD_DIM // 2
    inv = ROPE_THETA ** (-jnp.arange(0, axis_dim, 2, dtype=jnp.float32) / axis_dim)
    ang = jnp.stack([row[:, None] * inv, col[:, None] * inv], axis=1)
    return jnp.cos(ang), jnp.sin(ang)


def apply_axial_rope(x, cos, sin):
    B, S, H, Dh = x.shape
    xr = x.astype(jnp.float32).reshape(B, S, H, 2, 2, Dh // 4)
    x1, x2 = xr[..., 0, :], xr[..., 1, :]
    c = cos[None, :, None]
    s = sin[None, :, None]
    out = jnp.stack([x1 * c - x2 * s, x2 * c + x1 * s], axis=-2)
    return out.reshape(B, S, H, Dh).astype(x.dtype)


def global_attention(q, k, v):
    B, S, HQ, Dh = q.shape
    HKV = k.shape[2]
    G = HQ // HKV
    nb = S // BLOCK
    qb = (q * (Dh ** -0.5)).reshape(B, nb, BLOCK, HKV, G, Dh).transpose(1, 0, 2, 3, 4, 5)

    def one_block(qi):
        s = jnp.einsum('bqkgd,bskd->bkgqs', qi, k).astype(jnp.float32)
        p = jax.nn.softmax(s, axis=-1)
        return jnp.einsum('bkgqs,bskd->bqkgd', p.astype(v.dtype), v)

    o = lax.map(one_block, qb)
    return o.transpose(1, 0, 2, 3, 4, 5).reshape(B, S, HQ * Dh)


def window_attention(q, k, v, sink, slopes):
    B, S, HQ, Dh = q.shape
    HKV = k.shape[2]
    G = HQ // HKV
    nb = S // BLOCK
    qb = (q * (Dh ** -0.5)).reshape(B, nb, BLOCK, HKV, G, Dh)
    pad = ((0, 0), (BLOCK, BLOCK), (0, 0), (0, 0))
    kp = jnp.pad(k, pad).reshape(B, nb + 2, BLOCK, HKV, Dh)
    vp = jnp.pad(v, pad).reshape(B, nb + 2, BLOCK, HKV, Dh)
    kb = jnp.concatenate([kp[:, :-2], kp[:, 1:-1], kp[:, 2:]], axis=2)
    vb = jnp.concatenate([vp[:, :-2], vp[:, 1:-1], vp[:, 2:]], axis=2)
    s = jnp.einsum('bnqkgd,bnskd->bnkgqs', qb, kb).astype(jnp.float32)
    blk = jnp.arange(nb)[:, None]
    tpos = blk * BLOCK + jnp.arange(BLOCK)[None, :]
    spos = (blk - 1) * BLOCK + jnp.arange(3 * BLOCK)[None, :]
    dist = jnp.abs(tpos[:, :, None] - spos[:, None, :])
    valid = (dist <= WINDOW) & ((spos >= 0) & (spos < S))[:, None, :]
    distf = dist.astype(jnp.float32)
    bias = -slopes.reshape(HKV, G)[None, :, :, None, None] * distf[:, None, None]
    s = jnp.where(valid[:, None, None], s + bias, -jnp.inf)
    sink_l = sink.astype(jnp.float32).reshape(HKV, G)[None, None, :, :, None, None]
    m = jnp.maximum(jnp.max(s, axis=-1, keepdims=True), sink_l)
    p = jnp.exp(s - m)
    p = p / (jnp.sum(p, axis=-1, keepdims=True) + jnp.exp(sink_l - m))
    o = jnp.einsum('bnkgqs,bnskd->bnqkgd', p.astype(v.dtype), vb)
    return o.reshape(B, S, HQ * Dh)


def depthwise_conv_centred(u, w, b):
    up = jnp.pad(u, ((0, 0), (1, 1), (0, 0)))
    return up[:, :-2] * w[0] + up[:, 1:-1] * w[1] + up[:, 2:] * w[2] + b


def encoder_layer(x, norm_mix_g, w_in, qnorm_a_g, knorm_a_g, qnorm_b_g, knorm_b_g, sink_b,
                  out_norm_a_g, out_norm_b_g, w_out, norm_ffn_g, w_up, conv_w, conv_b, w_down):
    B, S, _ = x.shape
    cos, sin = axial_rope_tables(S)
    slopes = jnp.exp2(-8.0 * jnp.arange(1, N_HEADS_B + 1, dtype=jnp.float32) / N_HEADS_B)

    h = rmsnorm(x, norm_mix_g)
    proj = h @ w_in
    o1 = WIDTH_A
    o2 = o1 + KV_A
    o3 = o2 + KV_A
    o4 = o3 + WIDTH_B
    o5 = o4 + KV_B
    qa = proj[..., :o1].reshape(B, S, N_HEADS_A, HEAD_DIM)
    ka = proj[..., o1:o2].reshape(B, S, N_KV_A, HEAD_DIM)
    va = proj[..., o2:o3].reshape(B, S, N_KV_A, HEAD_DIM)
    qb = proj[..., o3:o4].reshape(B, S, N_HEADS_B, HEAD_DIM)
    kb = proj[..., o4:o5].reshape(B, S, N_KV_B, HEAD_DIM)
    vb = proj[..., o5:].reshape(B, S, N_KV_B, HEAD_DIM)

    qa = apply_axial_rope(rmsnorm(qa, qnorm_a_g), cos, sin)
    ka = apply_axial_rope(rmsnorm(ka, knorm_a_g), cos, sin)
    ya = global_attention(qa, ka, va)

    qb = rmsnorm(qb, qnorm_b_g)
    kb = rmsnorm(kb, knorm_b_g)
    yb = window_attention(qb, kb, vb, sink_b, slopes)

    y = jnp.concatenate([rmsnorm(ya, out_norm_a_g), rmsnorm(yb, out_norm_b_g)], axis=-1)
    x = x + y @ w_out

    h2 = rmsnorm(x, norm_ffn_g)
    u = depthwise_conv_centred(h2 @ w_up, conv_w, conv_b)
    gate, up = u[..., :D_FF], u[..., D_FF:]
    x = x + (jax.nn.gelu(gate, approximate=True) * up) @ w_down
    return x


def setup_inputs(seed: int = 0) -> dict:
    key = jax.random.key(seed)
    ks = jax.random.split(key, 20)
    f32 = jnp.float32

    def nrm(k, shape, scale):
        return jax.random.normal(k, shape, f32) * scale

    def gain(k, shape):
        return 1.0 + 0.02 * jax.random.normal(k, shape, f32)

    return {
        "x_prompt": jax.random.normal(ks[0], (BATCH, SEQ, D_MODEL), f32),
        "x_sample": jax.random.normal(ks[1], (DEC_BATCH, DEC_SEQ, D_MODEL), f32),
        "norm_mix_g": gain(ks[2], (DEPTH, D_MODEL)),
        "w_in": nrm(ks[3], (DEPTH, D_MODEL, IN_WIDTH), D_MODEL ** -0.5),
        "qnorm_a_g": gain(ks[4], (DEPTH, HEAD_DIM)),
        "knorm_a_g": gain(ks[5], (DEPTH, HEAD_DIM)),
        "qnorm_b_g": gain(ks[6], (DEPTH, HEAD_DIM)),
        "knorm_b_g": gain(ks[7], (DEPTH, HEAD_DIM)),
        "sink_b": nrm(ks[8], (DEPTH, N_HEADS_B), 0.5),
        "out_norm_a_g": gain(ks[9], (DEPTH, WIDTH_A)),
        "out_norm_b_g": gain(ks[10], (DEPTH, WIDTH_B)),
        "w_out": nrm(ks[11], (DEPTH, MIX_WIDTH, D_MODEL), MIX_WIDTH ** -0.5),
        "norm_ffn_g": gain(ks[12], (DEPTH, D_MODEL)),
        "w_up": nrm(ks[13], (DEPTH, D_MODEL, 2 * D_FF), D_MODEL ** -0.5),
        "conv_w": nrm(ks[14], (DEPTH, CONV_WIDTH, 2 * D_FF), CONV_WIDTH ** -0.5),
        "conv_b": nrm(ks[15], (DEPTH, 2 * D_FF), 0.01),
        "w_down": nrm(ks[16], (DEPTH, D_FF, D_MODEL), D_FF ** -0.5),
    }


def reference(x_prompt, x_sample, norm_mix_g, w_in, qnorm_a_g, knorm_a_g, qnorm_b_g, knorm_b_g,
              sink_b, out_norm_a_g, out_norm_b_g, w_out, norm_ffn_g, w_up, conv_w, conv_b, w_down):
    y_prompt = x_prompt
    y_sample = x_sample
    for l in range(DEPTH):
        params = (norm_mix_g[l], w_in[l], qnorm_a_g[l], knorm_a_g[l], qnorm_b_g[l], knorm_b_g[l],
                  sink_b[l], out_norm_a_g[l], out_norm_b_g[l], w_out[l], norm_ffn_g[l],
                  w_up[l], conv_w[l], conv_b[l], w_down[l])
        y_prompt = encoder_layer(y_prompt, *params)
        y_sample = encoder_layer(y_sample, *params)
    return (y_prompt, y_sample)
```

```python
import math
import os
from contextlib import ExitStack
import numpy as np
import concourse.bass as bass
import concourse.mybir as mybir
from concourse.bass_utils import run_bass_kernel_spmd

F32 = mybir.dt.float32
BF16 = mybir.dt.bfloat16
AF = mybir.ActivationFunctionType
ALU = mybir.AluOpType
AX = mybir.AxisListType

D = 1024
DC = 8
INW = 1536
DFF = 2816
FC = 22
EPS = 1e-6
DEBUG_TAGS = os.environ.get("K_TAGS", "0") == "1"
XW = 1028
NEG = -30000.0


def I(name, *args, **kw):
    return (name, args, kw)


class Sched:
    def __init__(self):
        self.ops = []
        self.lw = {}
        self.rd = {}

    @staticmethod
    def _key(op):
        return ("dma", op["dma"]) if op["dma"] is not None else op["eng"]

    def add(self, eng, fn, reads=(), writes=(), dma=None):
        i = len(self.ops)
        deps = {}

        def dep(j):
            k = self._key(self.ops[j])
            if k not in deps or deps[k] < j:
                deps[k] = j

        for b in reads:
            w = self.lw.get(b)
            if w is not None:
                dep(w)
        for b in writes:
            w = self.lw.get(b)
            if w is not None:
                dep(w)
            for j in self.rd.get(b, {}).values():
                dep(j)
        op = dict(eng=eng, fn=fn, deps=[], dma=dma, inc=False, done=None, tag=f"#{i} {fn[0]} r={list(reads)} w={list(writes)}")
        for k, j in deps.items():
            dop = self.ops[j]
            if dop["dma"] is None and dop["eng"] == "pe" and eng == "pe" and dma is None:
                continue
            dop["inc"] = True
            op["deps"].append(j)
        self.ops.append(op)
        me = self._key(op)
        for b in writes:
            self.lw[b] = i
            self.rd[b] = {}
        for b in reads:
            self.rd.setdefault(b, {})[me] = i
        return i

    def emit(self, nc, tag):
        with ExitStack() as es:
            sems = {}
            cnt = {}
            for op in self.ops:
                k = self._key(op)
                if op["dma"] is not None:
                    cnt[k] = cnt.get(k, 0) + 16
                    op["done"] = (k, cnt[k])
                elif op["inc"]:
                    cnt[k] = cnt.get(k, 0) + 1
                    op["done"] = (k, cnt[k])
            for n, k in enumerate(cnt.keys()):
                sems[k] = es.enter_context(nc.semaphore(f"{tag}_s{n}"))
            block = es.enter_context(nc.Block())
            ops = self.ops

            def run(engname):
                def body(eng):
                    waited = {}
                    for op in ops:
                        if op["eng"] != engname:
                            continue
                        for j in op["deps"]:
                            k, v = ops[j]["done"]
                            if waited.get(k, 0) >= v:
                                continue
                            eng.wait_ge(sems[k], v)
                            waited[k] = v
                        name, args, kw = op["fn"]
                        inst = getattr(eng, name)(*args, **kw)
                        if DEBUG_TAGS:
                            inst.annotate(op["tag"])
                        if op["done"] is not None:
                            k, v = op["done"]
                            inst.then_inc(sems[k], 16 if op["dma"] is not None else 1)
                return body

            block.sync(run("sp"))
            block.tensor(run("pe"))
            block.vector(run("dve"))
            block.scalar(run("act"))
            block.gpsimd(run("pool"))


def build(groups):
    nc = bass.Bass("TRN2", target_bir_lowering=False)

    def din(name, shape):
        return nc.dram_tensor(name, list(shape), F32, kind="ExternalInput").ap()

    def dout(name, shape):
        return nc.dram_tensor(name, list(shape), F32, kind="ExternalOutput").ap()

    dr = {}
    for g in groups:
        n = g["name"]
        S, NB = g["S"], g["NB"]
        NE = NB + 4
        dr["xseq" + n] = din("xseq" + n, [S, D])
        dr["xext" + n] = din("xext" + n, [NE * 128, D])
        dr["rseq" + n] = din("rseq" + n, [S, 64])
        dr["rext" + n] = din("rext" + n, [NE * 128, 64])
        dr["kval" + n] = din("kval" + n, [128, NE])
        dr["bval" + n] = din("bval" + n, [128, NE])
        dr["y" + n] = dout("y" + n, [NB * 128, D])
        dr["x1" + n] = nc.dram_tensor("x1" + n, [(NB + 2) * 128, XW], F32).ap()
    w_in = din("w_in", [D, INW])
    w_out = din("w_out", [D, D])
    w_up = din("w_up", [D, 2 * DFF])
    w_down = din("w_down", [DFF, D])
    gcols_d = din("gcols", [128, 24])
    convc_d = din("convc", [128, 4 * 44])
    headg_d = din("headg", [128, 4 * 64])
    sink_d = din("sinkr", [128, 8])
    bias_d = din("biasT", [128, 8 * 384])
    ident_d = din("ident", [128, 128])

    SMAX = max(g["S"] for g in groups)
    NEMAX = max(g["NB"] for g in groups) + 4

    with ExitStack() as es:
        def sb(name, shape, dt=F32):
            return es.enter_context(nc.sbuf_tensor("sA_" + name, list(shape), dt))

        ps = es.enter_context(nc.psum_tensor("psA", [128, 8, 512], F32))
        identf = sb("identf", [128, 65])
        identb = sb("identb", [128, 128], BF16)
        gcols = sb("gcols", [128, 24])
        headg = sb("headg", [128, 4, 64])
        sinkt = sb("sinkt", [128, 8])
        esink = sb("esink", [128, 8])
        biasb = sb("biasb", [128, 8, 384], BF16)
        kval = sb("kval", [128, NEMAX])
        bval = kval
        w_in_s = sb("w_in_s", [128, DC, INW], BF16)
        w_out_s = sb("w_out_s", [128, DC, D], BF16)
        KT = sb("KT", [128, SMAX], BF16)
        Vaug = sb("Vaug", [128, SMAX // 128, 2, 65], BF16)
        KbT = sb("KbT", [128, NEMAX * 128], BF16)
        Vb = sb("Vb", [128, NEMAX, 2, 65], BF16)
        xt = [sb(f"xt{i}", [128, D]) for i in range(2)]
        xs0 = sb("xs0", [128, D], BF16)
        hT = [sb(f"hT{i}", [128, DC, 128], BF16) for i in range(2)]
        sqj = sb("sqj", [128, D])
        ropet = sb("ropet", [128, 64])
        ssx = sb("ssx", [128, 1])
        rsx = sb("rsx", [128, 1])
        qraw = sb("qraw", [128, 1024])
        ssq = sb("ssq", [128, 16])
        rsq = sb("rsq", [128, 16])
        arena2 = sb("arena2", [128, 5120])
        QaT = [arena2[:, i * 1024:(i + 1) * 1024].bitcast(BF16).rearrange("p (j t) -> p j t", j=4) for i in range(2)]
        QbT = arena2[:, 2048:3072].bitcast(BF16).rearrange("p (j t) -> p j t", j=4)
        pT = [arena2[:, 3072 + i * 512:3584 + i * 512].bitcast(BF16).rearrange("p (j t) -> p j t", j=2) for i in range(2)]
        qra = arena2[:, 4096:4352].bitcast(BF16)
        qrb = arena2[:, 4352:4608].bitcast(BF16)
        yT = arena2[:, 4608:5120].bitcast(BF16).rearrange("p (c t) -> p c t", c=DC)
        oT = sb("oT", [128, 512])
        tA = oT[:, 0:256]
        tB = oT[:, 256:512]
        pT2 = sb("pT2", [128, 2, 512], BF16)
        pT3 = sb("pT3", [128, 2, 512], BF16)
        den = sb("den", [128, 4])
        rden = sb("rden", [128, 4])
        ssy = sb("ssy", [128, 8])
        rsy = sb("rsy", [128, 8])
        x1t = sb("x1t", [128, XW])
        ss1 = sb("ss1", [128, 1])
        rs1 = sb("rs1", [128, 1])
        fsc = sb("fsc", [128, 1])
        arena = sb("arena", [128, 6144])
        ya = arena[:, 0:2048].rearrange("p (b f) -> p b f", b=4)
        yb = arena[:, 2048:4096].rearrange("p (b f) -> p b f", b=4)
        ybf = arena[:, 4096:6144].bitcast(BF16).rearrange("p (b f) -> p b f", b=4)
        KW = 5
        _ar = [[arena, 0, 6144], [arena2, 0, 5120]]

        def carve(words):
            for a in _ar:
                if a[1] + words <= a[2]:
                    v = a[0][:, a[1]:a[1] + words]
                    a[1] += words
                    return v
            raise AssertionError("KV scratch arena exhausted")

        xtK, hTK, xsK, krawK, sqkK, knK, kn2K, krK, ropeK = [], [], [], [], [], [], [], [], []
        for i in range(KW):
            xtK.append(xt[i][:] if i < 2 else carve(1024))
            hTK.append(hT[i][:] if i < 2 else carve(512).bitcast(BF16).rearrange("p (c t) -> p c t", c=DC))
            xsK.append(carve(512).bitcast(BF16))
            krawK.append(carve(256))
            sqkK.append(carve(128))
            knK.append(carve(128))
            kn2K.append(carve(128))
            krK.append(carve(64).bitcast(BF16))
            ropeK.append(carve(64))
        ssxK = [sb(f"ssxK{i}", [128, 1]) for i in range(KW)]
        rsxK = [sb(f"rsxK{i}", [128, 1]) for i in range(KW)]
        sskK = [sb(f"sskK{i}", [128, 2]) for i in range(KW)]
        rskK = [sb(f"rskK{i}", [128, 2]) for i in range(KW)]
        ATT_NAMES = ["ya", "yb", "ybf", "QaT0", "QaT1", "QbT", "pT0", "pT1", "qra", "qrb", "yT"]
        KV_NAMES = ([f"xtK{i}" for i in range(KW)] + [f"hTK{i}" for i in range(KW)] + [f"xsK{i}" for i in range(KW)]
                    + [f"krawK{i}" for i in range(KW)] + [f"sqkK{i}" for i in range(KW)] + [f"knK{i}" for i in range(KW)]
                    + [f"kn2K{i}" for i in range(KW)] + [f"krK{i}" for i in range(KW)] + [f"ropeK{i}" for i in range(KW)])

        pT.append(pT2[:])
        pT.append(pT3[:])
        sc = Sched()
        PB = [f"pb{i}" for i in range(8)]
        st_slot = [ps[:, 0:2, :], ps[:, 2:4, :]]
        ST = [["pb0", "pb1"], ["pb2", "pb3"]]
        oacc = [ps[:, 4, :], ps[:, 5, :]]
        OA = ["pb4", "pb5"]
        bank6 = ps[:, 6, :]
        bank6b = ps[:, 6, :].bitcast(BF16)
        bank7 = ps[:, 7, :]
        bank7b = ps[:, 7, :].bitcast(BF16)
        cnt = dict(xt=0, st=0, oa=0, pt=0)

        def dma(out, in_, reads, writes, key):
            sc.add("sp", I("dma_start", out=out, in_=in_), reads=reads, writes=writes, dma=key)

        def fence():
            sc.add("dve", I("memset", fsc[:], 0.0), [], ATT_NAMES + KV_NAMES + ["fsc"])

        dma(xt[0][:, 0:128], ident_d[:, :], [], ["xtK0"], "xtK0")
        sc.add("dve", I("tensor_copy", out=identb[:], in_=xt[0][:, 0:128]), ["xtK0"], ["identb"])
        sc.add("dve", I("tensor_copy", out=identf[:], in_=xt[0][:, 0:65]), ["xtK0"], ["identf"])
        dma(gcols[:], gcols_d[:, :], [], ["gcols"], "gcols")
        dma(headg[:].rearrange("p a d -> p (a d)"), headg_d[:, :], [], ["headg"], "headg")
        dma(sinkt[:], sink_d[:, :], [], ["sinkt"], "sinkt")
        sc.add("act", I("activation", out=esink[:], in_=sinkt[:], func=AF.Exp), ["sinkt"], ["esink"])
        k = 0
        for i in range(3):
            sl = k % 2
            k += 1
            dma(xt[sl][:], bias_d[:, i * 1024:(i + 1) * 1024], [], [f"xtK{sl}"], f"xtK{sl}")
            sc.add("dve", I("tensor_copy", out=biasb[:].rearrange("p h k -> p (h k)")[:, i * 1024:(i + 1) * 1024], in_=xt[sl][:]),
                   [f"xtK{sl}"], ["biasb"])
        w_in_v = w_in.rearrange("(c p) n -> p c n", p=128)
        w_out_v = w_out.rearrange("(c p) n -> p c n", p=128)
        for c in range(DC):
            for h in range(2):
                sl = k % 2
                k += 1
                dma(xt[sl][:, 0:768], w_in_v[:, c, h * 768:(h + 1) * 768], [], [f"xtK{sl}"], f"xtK{sl}")
                sc.add("dve", I("tensor_scalar",
                                out=w_in_s[:, c, h * 768:h * 768 + 512].rearrange("p (j two d) -> p two j d", j=4, two=2),
                                in0=xt[sl][:, 0:512].rearrange("p (two j d) -> p two j d", two=2, j=4), scalar1=gcols[:, c:c + 1],
                                scalar2=None, op0=ALU.mult), [f"xtK{sl}", "gcols"], ["w_in_s"])
                sc.add("act", I("activation", out=w_in_s[:, c, h * 768 + 512:(h + 1) * 768], in_=xt[sl][:, 512:768], func=AF.Copy,
                                scale=gcols[:, c:c + 1]), [f"xtK{sl}", "gcols"], ["w_in_s2"])
        for c in range(DC):
            sl = k % 2
            k += 1
            dma(xt[sl][:], w_out_v[:, c, :], [], [f"xtK{sl}"], f"xtK{sl}")
            sc.add("dve", I("tensor_scalar", out=w_out_s[:, c, :], in0=xt[sl][:], scalar1=gcols[:, 8 + c:9 + c],
                            scalar2=None, op0=ALU.mult), [f"xtK{sl}", "gcols"], ["w_out_s"])
        cnt["xt"] = k
        WIN = ["w_in_s", "w_in_s2"]

        def rstd_ops(ss_ap, rs_ap, n, ssname, rsname, bias2=0.0):
            sc.add("act", I("activation", out=rs_ap, in_=ss_ap, func=AF.Ln, scale=1.0 / n, bias=EPS), [ssname], [rsname])
            sc.add("act", I("activation", out=rs_ap, in_=rs_ap, func=AF.Exp, scale=-0.5, bias=bias2), [rsname], [rsname])

        def rope_ops(xin, H, rope_ap, rname, out_ap_fn, inname, outname):
            xv = xin.rearrange("p (h a f e) -> p h a f e", h=H, a=2, f=2)
            x1 = xv[:, :, :, 0, :]
            x2 = xv[:, :, :, 1, :]
            C = rope_ap[:, 0:32].rearrange("p (a e) -> p a e", a=2).unsqueeze(1).to_broadcast([128, H, 2, 16])
            Sn = rope_ap[:, 32:64].rearrange("p (a e) -> p a e", a=2).unsqueeze(1).to_broadcast([128, H, 2, 16])
            tAv = tA[:, 0:H * 32].rearrange("p (h a e) -> p h a e", h=H, a=2)
            tBv = tB[:, 0:H * 32].rearrange("p (h a e) -> p h a e", h=H, a=2)
            sc.add("dve", I("tensor_tensor", out=tAv, in0=x1, in1=C, op=ALU.mult), [inname, rname], ["oT"])
            sc.add("dve", I("tensor_tensor", out=tBv, in0=x2, in1=Sn, op=ALU.mult), [inname, rname], ["oT"])
            sc.add("dve", I("tensor_tensor", out=out_ap_fn(0), in0=tAv, in1=tBv, op=ALU.subtract), ["oT"], [outname])
            sc.add("dve", I("tensor_tensor", out=tAv, in0=x2, in1=C, op=ALU.mult), [inname, rname], ["oT"])
            sc.add("dve", I("tensor_tensor", out=tBv, in0=x1, in1=Sn, op=ALU.mult), [inname, rname], ["oT"])
            sc.add("dve", I("tensor_tensor", out=out_ap_fn(1), in0=tAv, in1=tBv, op=ALU.add), ["oT"], [outname])

        def kv_gen(src_rows, rope_rows, kcol, gidx, KTt, Vt, idx, ktname, vname, s):
            X, XS, HT, KR, SQ, KN, KN2, KRR, RP = (f"xtK{s}", f"xsK{s}", f"hTK{s}", f"krawK{s}", f"sqkK{s}", f"knK{s}",
                                                   f"kn2K{s}", f"krK{s}", f"ropeK{s}")
            bT, bP = s, s
            bTb = ps[:, bT, :].bitcast(BF16)
            dma(xtK[s], src_rows, [], [X], X)
            if rope_rows is not None:
                dma(ropeK[s], rope_rows, [], [RP], RP)
            yield
            sc.add("act", I("activation", out=qraw[:], in_=xtK[s], func=AF.Square, accum_out=ssxK[s][:]), [X], ["qraw", f"ssxK{s}"])
            yield
            rstd_ops(ssxK[s][:], rsxK[s][:], D, f"ssxK{s}", f"rsxK{s}")
            yield
            sc.add("dve", I("tensor_scalar", out=xsK[s], in0=xtK[s], scalar1=rsxK[s][:, 0:1], scalar2=None, op0=ALU.mult),
                   [X, f"rsxK{s}"], [XS])
            yield
            for c in range(DC):
                sc.add("pe", I("transpose", out=bTb[:, c * 128:(c + 1) * 128], in_=xsK[s][:, c * 128:(c + 1) * 128], identity=identb[:]),
                       [XS, "identb"], [PB[bT]])
            yield
            sc.add("act", I("activation", out=hTK[s].rearrange("p c t -> p (c t)"), in_=bTb[:, :], func=AF.Copy), [PB[bT]], [HT])
            yield
            for c in range(DC):
                sc.add("pe", I("matmul", ps[:, bP, 0:256], lhsT=hTK[s][:, c, :], rhs=w_in_s[:, c, kcol:kcol + 256],
                               start=(c == 0), stop=(c == DC - 1)), [HT] + WIN, [PB[bP]])
            yield
            sc.add("act", I("activation", out=krawK[s], in_=ps[:, bP, 0:256], func=AF.Copy), [PB[bP]], [KR])
            yield
            sc.add("pool", I("tensor_copy", out=Vt[:, idx, :, 0:64], in_=krawK[s][:, 128:256].rearrange("p (h d) -> p h d", h=2)),
                   [KR], [vname])
            sc.add("pool", I("tensor_tensor", out=sqkK[s], in0=krawK[s][:, 0:128], in1=krawK[s][:, 0:128], op=ALU.mult), [KR], [SQ])
            yield
            sc.add("dve", I("tensor_reduce", out=sskK[s][:], in_=sqkK[s].rearrange("p (h d) -> p h d", h=2), axis=AX.X, op=ALU.add),
                   [SQ], [f"sskK{s}"])
            yield
            rstd_ops(sskK[s][:], rskK[s][:], 64, f"sskK{s}", f"rskK{s}")
            yield
            sc.add("dve", I("tensor_tensor", out=knK[s].rearrange("p (h d) -> p h d", h=2),
                            in0=krawK[s][:, 0:128].rearrange("p (h d) -> p h d", h=2),
                            in1=rskK[s][:].unsqueeze(2).to_broadcast([128, 2, 64]), op=ALU.mult), [KR, f"rskK{s}"], [KN])
            gv = headg[:, gidx, :].unsqueeze(1).to_broadcast([128, 2, 64])
            if rope_rows is not None:
                sc.add("dve", I("tensor_tensor", out=kn2K[s].rearrange("p (h d) -> p h d", h=2),
                                in0=knK[s].rearrange("p (h d) -> p h d", h=2), in1=gv, op=ALU.mult), [KN, "headg"], [KN2])
                krv = krK[s].rearrange("p (h a f e) -> p h a f e", h=2, a=2, f=2)
                rope_ops(kn2K[s], 2, ropeK[s], RP, lambda half: krv[:, :, :, half, :], KN2, KRR)
            else:
                sc.add("dve", I("tensor_tensor", out=krK[s].rearrange("p (h d) -> p h d", h=2),
                                in0=knK[s].rearrange("p (h d) -> p h d", h=2), in1=gv, op=ALU.mult), [KN, "headg"], [KRR])
            yield
            sc.add("pe", I("transpose", out=bTb[:, 0:128], in_=krK[s], identity=identb[:]), [KRR, "identb"], [PB[bT]])
            yield
            sc.add("act", I("activation", out=KTt[:, idx * 128:(idx + 1) * 128], in_=bTb[:, 0:128], func=AF.Copy), [PB[bT]], [ktname])
            yield

        def interleave(gens, width, stagger=3):
            active = []
            it = iter(gens)
            since = stagger
            done = False
            while True:
                if not done and len(active) < width and since >= stagger:
                    gnew = next(it, None)
                    if gnew is None:
                        done = True
                    else:
                        active.append(gnew)
                        since = 0
                if not active:
                    if done:
                        break
                    since = stagger
                    continue
                since += 1
                for gg_ in list(active):
                    try:
                        next(gg_)
                    except StopIteration:
                        active.remove(gg_)

        def evac_O(ob, nb, h, ydst, yname, sink):
            sc.add("dve", I("tensor_copy", out=oT[0:65, 0:nb * 128], in_=oacc[ob][0:65, 0:nb * 128]), [OA[ob]], ["oT"])
            b6v = bank6[:, 0:4 * 65].rearrange("p (b d) -> p b d", d=65)
            for bi in range(nb):
                sc.add("pe", I("transpose", out=b6v[:, bi, :], in_=oT[0:65, bi * 128:(bi + 1) * 128],
                               identity=identf[0:65, 0:65]), ["oT", "identf"], ["pb6"])
            if sink:
                sc.add("dve", I("tensor_scalar", out=den[:, 0:nb], in0=b6v[:, 0:nb, 64], scalar1=esink[:, h:h + 1],
                                scalar2=None, op0=ALU.add), ["pb6", "esink"], ["den"])
                sc.add("dve", I("reciprocal", out=rden[:, 0:nb], in_=den[:, 0:nb]), ["den"], ["rden"])
            else:
                sc.add("dve", I("reciprocal", out=rden[:, 0:nb], in_=b6v[:, 0:nb, 64]), ["pb6"], ["rden"])
            sc.add("dve", I("tensor_tensor", out=ydst[:, 0:nb, h * 64:(h + 1) * 64], in0=b6v[:, 0:nb, 0:64],
                            in1=rden[:, 0:nb].unsqueeze(2).to_broadcast([128, nb, 64]), op=ALU.mult), ["pb6", "rden"], [yname])

        def proj_gen(g, tile, qslot):
            n = g["name"]
            for bi, eb in enumerate(tile):
                sl = cnt["xt"] % 2
                cnt["xt"] += 1
                X = f"xtK{sl}"
                dma(ropet[:], dr["rext" + n][eb * 128:(eb + 1) * 128, :], [], ["ropet"], "ropet")
                dma(xt[sl][:], dr["xext" + n][eb * 128:(eb + 1) * 128, :], [], [X], X)
                yield
                sc.add("pool", I("tensor_tensor", out=sqj[:], in0=xt[sl][:], in1=xt[sl][:], op=ALU.mult), [X], ["sqj"])
                yield
                sc.add("dve", I("tensor_reduce", out=ssx[:], in_=sqj[:], axis=AX.X, op=ALU.add), ["sqj"], ["ssx"])
                yield
                rstd_ops(ssx[:], rsx[:], D, "ssx", "rsx")
                yield
                sc.add("dve", I("tensor_scalar", out=xs0[:], in0=xt[sl][:], scalar1=rsx[:, 0:1], scalar2=None, op0=ALU.mult),
                       [X, "rsx"], ["xs0"])
                yield
                for c in range(DC):
                    sc.add("pe", I("transpose", out=bank7b[:, c * 128:(c + 1) * 128], in_=xs0[:, c * 128:(c + 1) * 128], identity=identb[:]),
                           ["xs0", "identb"], ["pb7"])
                yield
                sc.add("dve", I("tensor_copy", out=hT[0][:].rearrange("p c t -> p (c t)"), in_=bank7b[:, :]), ["pb7"], ["hTK0"])
                yield
                for half, col0 in ((0, 0), (1, 768)):
                    for c in range(DC):
                        sc.add("pe", I("matmul", bank7[:, :], lhsT=hT[0][:, c, :], rhs=w_in_s[:, c, col0:col0 + 512],
                                       start=(c == 0), stop=(c == DC - 1)), ["hTK0"] + WIN, ["pb7"])
                    yield
                    sc.add("dve", I("tensor_copy", out=qraw[:, half * 512:(half + 1) * 512], in_=bank7[:, :]), ["pb7"], ["qraw"])
                    yield
                sc.add("pool", I("tensor_tensor", out=sqj[:], in0=qraw[:], in1=qraw[:], op=ALU.mult), ["qraw"], ["sqj"])
                yield
                sc.add("dve", I("tensor_reduce", out=ssq[:], in_=sqj[:].rearrange("p (h d) -> p h d", d=64), axis=AX.X, op=ALU.add),
                       ["sqj"], ["ssq"])
                yield
                rstd_ops(ssq[:], rsq[:], 64, "ssq", "rsq", bias2=-math.log(8.0))
                yield
                sc.add("dve", I("tensor_tensor", out=qraw[:].rearrange("p (h d) -> p h d", d=64),
                                in0=qraw[:].rearrange("p (h d) -> p h d", d=64),
                                in1=rsq[:].unsqueeze(2).to_broadcast([128, 16, 64]), op=ALU.mult), ["qraw", "rsq"], ["qraw"])
                sc.add("dve", I("tensor_tensor", out=qrb[:].rearrange("p (h d) -> p h d", d=64),
                                in0=qraw[:, 512:1024].rearrange("p (h d) -> p h d", d=64),
                                in1=headg[:, 2, :].unsqueeze(1).to_broadcast([128, 8, 64]), op=ALU.mult), ["qraw", "headg"], ["qrb"])
                yield
                for j in range(4):
                    sc.add("pe", I("transpose", out=bank7b[:, j * 128:(j + 1) * 128], in_=qrb[:, j * 128:(j + 1) * 128], identity=identb[:]),
                           ["qrb", "identb"], ["pb7"])
                sc.add("dve", I("tensor_tensor", out=qraw[:, 0:512].rearrange("p (h d) -> p h d", d=64),
                                in0=qraw[:, 0:512].rearrange("p (h d) -> p h d", d=64),
                                in1=headg[:, 0, :].unsqueeze(1).to_broadcast([128, 8, 64]), op=ALU.mult), ["qraw", "headg"], ["qraw"])
                qrav = qra[:].rearrange("p (h a f e) -> p h a f e", h=8, a=2, f=2)
                rope_ops(qraw[:, 0:512], 8, ropet[:], "ropet", lambda half: qrav[:, :, :, half, :], "qraw", "qra")
                yield
                sc.add("dve", I("tensor_copy", out=QbT[:, :, bi * 128:(bi + 1) * 128],
                                in_=bank7b[:, 0:512].rearrange("p (j t) -> p j t", j=4)), ["pb7"], ["QbT"])
                yield
                for j in range(4):
                    sc.add("pe", I("transpose", out=bank7b[:, j * 128:(j + 1) * 128], in_=qra[:, j * 128:(j + 1) * 128], identity=identb[:]),
                           ["qra", "identb"], ["pb7"])
                yield
                sc.add("dve", I("tensor_copy", out=QaT[qslot][:, :, bi * 128:(bi + 1) * 128],
                                in_=bank7b[:, 0:512].rearrange("p (j t) -> p j t", j=4)), ["pb7"], [f"QaT{qslot}"])
                yield

        def tail0(nb):
            for grp, ysrc, yn in ((0, ya, "ya"), (1, yb, "yb")):
                for bi in range(nb):
                    sc.add("act", I("activation", out=oT[:, 0:512], in_=ysrc[:, bi, :], func=AF.Square,
                                    accum_out=ssy[:, grp * 4 + bi:grp * 4 + bi + 1]), [yn], ["oT", "ssy"])
            rstd_ops(ssy[:], rsy[:], 512, "ssy", "rsy")
            for grp, ysrc, yn in ((0, ya, "ya"), (1, yb, "yb")):
                for bi in range(nb):
                    sc.add("dve", I("tensor_scalar", out=ybf[:, bi, grp * 512:(grp + 1) * 512], in0=ysrc[:, bi, :],
                                    scalar1=rsy[:, grp * 4 + bi:grp * 4 + bi + 1], scalar2=None, op0=ALU.mult), [yn, "rsy"], ["ybf"])

        def tail_gen(g, tile):
            n = g["name"]
            for bi, eb in enumerate(tile):
                dma(x1t[:, 0:D], dr["xext" + n][eb * 128:(eb + 1) * 128, :], [], ["x1t"], "x1t_in")
                for c in range(DC):
                    sc.add("pe", I("transpose", out=bank7b[:, c * 128:(c + 1) * 128], in_=ybf[:, bi, c * 128:(c + 1) * 128], identity=identb[:]),
                           ["ybf", "identb"], ["pb7"])
                yield
                sc.add("dve", I("tensor_copy", out=yT[:].rearrange("p c t -> p (c t)"), in_=bank7b[:, :]), ["pb7"], ["yT"])
                yield
                for half in range(2):
                    for c in range(DC):
                        sc.add("pe", I("matmul", bank7[:, :], lhsT=yT[:, c, :], rhs=w_out_s[:, c, half * 512:(half + 1) * 512],
                                       start=(c == 0), stop=(c == DC - 1)), ["yT", "w_out_s"], ["pb7"])
                    yield
                    sc.add("dve", I("tensor_tensor", out=x1t[:, half * 512:(half + 1) * 512], in0=bank7[:, :],
                                    in1=x1t[:, half * 512:(half + 1) * 512], op=ALU.add), ["pb7", "x1t"], ["x1t"])
                    yield
                sc.add("pool", I("tensor_tensor", out=sqj[:], in0=x1t[:, 0:D], in1=x1t[:, 0:D], op=ALU.mult), ["x1t"], ["sqj"])
                yield
                sc.add("dve", I("tensor_reduce", out=ss1[:], in_=sqj[:], axis=AX.X, op=ALU.add), ["sqj"], ["ss1"])
                yield
                rstd_ops(ss1[:], rs1[:], D, "ss1", "rs1")
                yield
                sc.add("dve", I("tensor_tensor", out=x1t[:, D:D + 1], in0=rs1[:], in1=bval[:, eb:eb + 1], op=ALU.mult), ["rs1", "kval"], ["x1t"])
                sc.add("dve", I("memset", x1t[:, D + 1:XW], 0.0), [], ["x1t"])
                yield
                dkey = f"x1{n}_{eb}"
                dma(dr["x1" + n][(eb - 1) * 128:eb * 128, :], x1t[:], ["x1t"], [dkey], "x1t_out")
                yield

        def window_attn(g, tile):
            nb = len(tile)
            halves = [list(range(0, min(2, nb)))] + ([list(range(2, nb))] if nb > 2 else [])
            steps = [(h, hv) for h in range(8) for hv in halves]
            pend = None

            def pv(h, hv, slot, ob):
                two = h // 4
                for bi_l, bi in enumerate(hv):
                    eb = tile[bi]
                    for jj in range(3):
                        ke = eb - 1 + jj
                        sc.add("pe", I("matmul", oacc[ob][0:65, bi * 128:(bi + 1) * 128], lhsT=Vb[:, ke, two, :],
                                       rhs=pT[slot][:, bi_l, jj * 128:(jj + 1) * 128], start=(jj == 0), stop=(jj == 2)),
                               ["Vb", f"pT{slot}"], [OA[ob]])

            for si, (h, hv) in enumerate(steps):
                j, two = h % 4, h // 4
                r0 = two * 64
                slot = cnt["st"] % 2
                cnt["st"] += 1
                if hv is halves[0]:
                    cnt["oa"] += 1
                ob = cnt["oa"] % 2
                for bi_l, bi in enumerate(hv):
                    eb = tile[bi]
                    for jj in range(3):
                        ke = eb - 1 + jj
                        sc.add("pe", I("matmul", st_slot[slot][:, bi_l, jj * 128:(jj + 1) * 128], lhsT=KbT[r0:r0 + 64, ke * 128:(ke + 1) * 128],
                                       rhs=QbT[r0:r0 + 64, j, bi * 128:(bi + 1) * 128], start=True, stop=True),
                               ["KbT", "QbT"], ST[slot])
                nl = len(hv)
                sc.add("dve", I("tensor_tensor", out=st_slot[slot][:, 0:nl, 0:384], in0=st_slot[slot][:, 0:nl, 0:384],
                                in1=biasb[:, h, :].unsqueeze(1).to_broadcast([128, nl, 384]), op=ALU.add), ST[slot] + ["biasb"], ST[slot])
                sc.add("act", I("activation", out=pT[slot][:, 0:nl, 0:384], in_=st_slot[slot][:, 0:nl, 0:384], func=AF.Exp),
                       ST[slot], [f"pT{slot}"])
                if pend is not None:
                    pv(*pend[:4])
                    if pend[4]:
                        evac_O(pend[3], nb, pend[0], yb, "yb", True)
                pend = (h, hv, slot, ob, hv is halves[-1])
            pv(*pend[:4])
            evac_O(pend[3], nb, pend[0], yb, "yb", True)

        def global_attn(g, tile, qslot, bg, nyield):
            nb = len(tile)
            Tq = nb * 128
            nkb = g["S"] // 128
            bgstep = max(1, (4 * nkb) // (nyield + 6))
            it = 0
            for j in range(4):
                obs = (0, 1)
                pend = []

                def pv(pkb, pslot):
                    for two in range(2):
                        sc.add("pe", I("matmul", oacc[obs[two]][0:65, 0:Tq], lhsT=Vaug[:, pkb, two, :], rhs=pT[pslot][:, two, 0:Tq],
                                       start=(pkb == 0), stop=(pkb == nkb - 1)), ["Vaug", f"pT{pslot}"], [OA[obs[two]]])

                for kb2 in range(0, nkb, 2):
                    for kb in range(kb2, min(kb2 + 2, nkb)):
                        slot = cnt["st"] % 2
                        cnt["st"] += 1
                        pslot = cnt["pt"] % 4
                        cnt["pt"] += 1
                        for two in range(2):
                            r0 = two * 64
                            sc.add("pe", I("matmul", st_slot[slot][:, two, 0:Tq], lhsT=KT[r0:r0 + 64, kb * 128:(kb + 1) * 128],
                                           rhs=QaT[qslot][r0:r0 + 64, j, 0:Tq], start=True, stop=True), ["KT", f"QaT{qslot}"], ST[slot])
                        sc.add("act", I("activation", out=pT[pslot][:, :, 0:Tq], in_=st_slot[slot][:, :, 0:Tq], func=AF.Exp),
                               ST[slot], [f"pT{pslot}"])
                        pend.append((kb, pslot))
                    while len(pend) > 2:
                        pv(*pend.pop(0))
                    for _ in range(2):
                        it += 1
                        if it % bgstep == 0 and os.environ.get("K_BG", "1") == "1":
                            next(bg, None)
                while pend:
                    pv(*pend.pop(0))
                for two in range(2):
                    evac_O(obs[two], nb, two * 4 + j, ya, "ya", False)
            for _ in bg:
                pass

        def chain(*gens):
            for gen in gens:
                if gen is None:
                    continue
                for _ in gen:
                    yield

        sc.add("pool", I("memset", ssy[:], 1.0), [], ["ssy"])
        sc.add("pool", I("memset", Vaug[:].rearrange("p b k d -> p (b k d)"), 1.0), [], ["Vaug"])
        for g in groups:
            n = g["name"]
            S, NB = g["S"], g["NB"]
            NE = NB + 4
            fence()
            dma(kval[:, 0:NE], dr["kval" + n][:, :], [], ["kval"], "kval")
            for kvh in range(2):
                sc.add("pool", I("tensor_copy", out=Vb[:, 0:NE, kvh, 64], in_=kval[:, 0:NE]), ["kval"], ["Vb"])
            gens = []
            ctr = 0
            for b in range(S // 128):
                gens.append(kv_gen(dr["xseq" + n][b * 128:(b + 1) * 128, :], dr["rseq" + n][b * 128:(b + 1) * 128, :],
                                   512, 1, KT, Vaug, b, "KT", "Vaug", ctr % KW))
                ctr += 1
            for eb in range(NE):
                gens.append(kv_gen(dr["xext" + n][eb * 128:(eb + 1) * 128, :], None, 1280, 3, KbT, Vb, eb, "KbT", "Vb", ctr % KW))
                ctr += 1
            interleave(gens, KW)
            fence()
            blocks = list(range(1, NB + 3))
            tiles = [blocks[i:i + 4] for i in range(0, len(blocks), 4)]
            for _ in proj_gen(g, tiles[0], 0):
                pass
            prev_tail = None
            for ti, tile in enumerate(tiles):
                qslot = ti % 2
                window_attn(g, tile)
                nxt = proj_gen(g, tiles[ti + 1], (ti + 1) % 2) if ti + 1 < len(tiles) else None
                ny = (11 * 4 if prev_tail is not None else 0) + (20 * 4 if nxt is not None else 0)
                global_attn(g, tile, qslot, chain(prev_tail, nxt), ny)
                tail0(len(tile))
                prev_tail = tail_gen(g, tile)
            for _ in prev_tail:
                pass
        allx1 = [f"x1{g['name']}_{eb}" for g in groups for eb in range(1, g["NB"] + 3)]
        sc.add("sp", I("nop"), allx1, [])
        sc.add("act", I("nop"), allx1, [])
        sc.emit(nc, "A")

    with ExitStack() as es:
        def sb(name, shape, dt=F32):
            return es.enter_context(nc.sbuf_tensor("sB_" + name, list(shape), dt))

        ps = es.enter_context(nc.psum_tensor("psB", [128, 8, 512], F32))
        identf = sb("identfB", [128, 128])
        identb = sb("identbB", [128, 128], BF16)
        gcols = sb("gcolsB", [128, 24])
        convc = sb("convc", [128, 4, 44])
        w_up_s = sb("w_up_s", [128, DC, 2 * DFF], BF16)
        w_down_s = sb("w_down_s", [128, FC, D], BF16)
        SW = 1408
        NSTG = 4
        stg = [sb(f"stg{i}", [128, SW]) for i in range(NSTG)]
        TBK = 2
        x1s = [sb(f"x1s{i}", [128, XW]) for i in range(2 * TBK)]
        halo = [sb(f"halo{i}", [2, XW]) for i in range(2)]
        xsB = [sb(f"xsB{i}", [128, D], BF16) for i in range(2)]
        xsh = sb("xsh", [2, D], BF16)
        h2T = sb("h2T", [128, DC, 258], BF16)
        t1g = [sb(f"t1g{i}", [128, 256]) for i in range(2)]
        t1u = [sb(f"t1u{i}", [128, 256]) for i in range(2)]
        t2g, t2u, t3g, t3u = t1g, t1u, t1g, t1u
        gg = [sb(f"gg{i}", [128, 256]) for i in range(2)]
        aT = [sb(f"aT{i}", [128, 256], BF16) for i in range(3)]
        yout = [sb(f"yout{i}", [128, D]) for i in range(2)]

        sc = Sched()

        def dma(out, in_, reads, writes, key):
            sc.add("sp", I("dma_start", out=out, in_=in_), reads=reads, writes=writes, dma=key)

        dma(identf[:], ident_d[:, :], [], ["identf"], "identf")
        sc.add("dve", I("tensor_copy", out=identb[:], in_=identf[:]), ["identf"], ["identb"])
        dma(gcols[:], gcols_d[:, :], [], ["gcols"], "gcols")
        dma(convc[:].rearrange("p a c -> p (a c)"), convc_d[:, :], [], ["convc"], "convc")
        w_up_v = w_up.rearrange("(c p) n -> p c n", p=128)
        w_down_v = w_down.rearrange("(c p) n -> p c n", p=128)
        k = 0
        cengs = ["dve", "act", "dve", "act"]
        for c in range(DC):
            for q in range(4):
                sl = k % NSTG
                k += 1
                dma(stg[sl][:], w_up_v[:, c, q * SW:(q + 1) * SW], [], [f"stg{sl}"], f"stg{sl}")
                if cengs[sl] == "act":
                    sc.add("act", I("activation", out=w_up_s[:, c, q * SW:(q + 1) * SW], in_=stg[sl][:],
                                                                          func=AF.Copy, scale=gcols[:, 16 + c:17 + c]),
                           [f"stg{sl}", "gcols"], [f"w_up_s{sl}"])
                else:
                    sc.add(cengs[sl], I("tensor_scalar", out=w_up_s[:, c, q * SW:(q + 1) * SW], in0=stg[sl][:],
                                                                                 scalar1=gcols[:, 16 + c:17 + c], scalar2=None, op0=ALU.mult),
                           [f"stg{sl}", "gcols"], [f"w_up_s{sl}"])
        for c in range(FC):
            sl = k % NSTG
            k += 1
            dma(stg[sl][:, 0:D], w_down_v[:, c, :], [], [f"stg{sl}"], f"stg{sl}")
            if cengs[sl] == "act":
                sc.add("act", I("activation", out=w_down_s[:, c, :], in_=stg[sl][:, 0:D], func=AF.Copy),
                       [f"stg{sl}"], [f"w_down_s{sl}"])
            else:
                sc.add(cengs[sl], I("tensor_copy", out=w_down_s[:, c, :], in_=stg[sl][:, 0:D]),
                       [f"stg{sl}"], [f"w_down_s{sl}"])
        WUP = [f"w_up_s{i}" for i in range(NSTG)]
        WDN = [f"w_down_s{i}" for i in range(NSTG)]

        ytiles = []
        tcount = 0
        for g in groups:
            n = g["name"]
            NB = g["NB"]
            x1d = dr["x1" + n]
            t0 = 0
            while t0 < NB:
                nbt = min(TBK, NB - t0)
                T = nbt * 128
                par = tcount % 2
                tcount += 1
                r0 = (1 + t0) * 128
                xb = [x1s[par * TBK + i] for i in range(nbt)]
                xbn = [f"x1s{par * TBK + i}" for i in range(nbt)]
                hl = halo[par]
                HL = f"halo{par}"
                for i in range(nbt):
                    dma(xb[i][:], x1d[r0 + i * 128:r0 + (i + 1) * 128, :], [], [xbn[i]], xbn[i])
                dma(hl[0:1, :], x1d[r0 - 1:r0, :], [], [HL + "a"], HL + "a")
                dma(hl[1:2, :], x1d[r0 + T:r0 + T + 1, :], [], [HL + "b"], HL + "b")
                for i in range(nbt):
                    xi = i % 2
                    sc.add("dve", I("tensor_scalar", out=xsB[xi][:], in0=xb[i][:, 0:D], scalar1=xb[i][:, D:D + 1],
                                                                        scalar2=None, op0=ALU.mult), [xbn[i]], [f"xsB{xi}"])
                    for c in range(DC):
                        sc.add("pe", I("transpose", out=ps[:, 0, :].bitcast(BF16)[:, c * 128:(c + 1) * 128],
                                                                       in_=xsB[xi][:, c * 128:(c + 1) * 128], identity=identb[:]),
                               [f"xsB{xi}", "identb"], ["pb0"])
                    sc.add("dve", I("tensor_copy", out=h2T[:, :, 1 + i * 128:1 + (i + 1) * 128],
                                                               in_=ps[:, 0, :].bitcast(BF16)[:, :].rearrange("p (c t) -> p c t", c=DC)),
                           ["pb0"], ["h2T"])
                sc.add("dve", I("tensor_scalar", out=xsh[:], in0=hl[:, 0:D], scalar1=hl[:, D:D + 1], scalar2=None, op0=ALU.mult),
                       [HL + "a", HL + "b"], ["xsh"])
                for c in range(DC):
                    sc.add("pe", I("transpose", out=ps[:, 1, :].bitcast(BF16)[:, c * 2:(c + 1) * 2],
                                                            in_=xsh[0:2, c * 128:(c + 1) * 128], identity=identb[0:2, 0:2]),
                           ["xsh", "identb"], ["pb1"])
                hv = ps[:, 1, :].bitcast(BF16)[:, 0:16].rearrange("p (c t) -> p c t", c=DC)
                sc.add("dve", I("tensor_copy", out=h2T[:, :, 0:1], in_=hv[:, :, 0:1]), ["pb1"], ["h2T"])
                sc.add("dve", I("tensor_copy", out=h2T[:, :, T + 1:T + 2], in_=hv[:, :, 1:2]), ["pb1"], ["h2T"])

                def up(c, T=T):
                    sl = c % 2
                    for (bank, col0, nm) in ((2 * sl, c * 128, f"pbG{sl}"), (2 * sl + 1, DFF + c * 128, f"pbU{sl}")):
                        for kk in range(DC):
                            sc.add("pe", I("matmul",
                                ps[:, bank, 0:T + 2], lhsT=w_up_s[:, kk, col0:col0 + 128], rhs=h2T[:, kk, 0:T + 2],
                                start=(kk == 0), stop=(kk == DC - 1)), ["h2T"] + WUP, [f"pb{bank}"])

                def conv(c, T=T):
                    sl = c % 2
                    bG, bU = 2 * sl, 2 * sl + 1
                    w0 = convc[:, 0, c:c + 1]
                    w1 = convc[:, 1, c:c + 1]
                    w2 = convc[:, 2, c:c + 1]
                    bb = convc[:, 3, c:c + 1]
                    w0u = convc[:, 0, FC + c:FC + c + 1]
                    w1u = convc[:, 1, FC + c:FC + c + 1]
                    w2u = convc[:, 2, FC + c:FC + c + 1]
                    bbu = convc[:, 3, FC + c:FC + c + 1]
                    sc.add("act", I("activation", out=t1g[sl][:, 0:T], in_=ps[:, bG, 1:T + 1], func=AF.Identity, scale=w1, bias=bb),
                           [f"pb{bG}", "convc"], [f"t1g{sl}"])
                    sc.add("act", I("activation", out=t1u[sl][:, 0:T], in_=ps[:, bU, 1:T + 1], func=AF.Identity, scale=w1u, bias=bbu),
                           [f"pb{bU}", "convc"], [f"t1u{sl}"])
                    sc.add("dve", I("scalar_tensor_tensor", out=t2g[sl][:, 0:T], in0=ps[:, bG, 0:T], scalar=w0, in1=t1g[sl][:, 0:T],
                                                                   op0=ALU.mult, op1=ALU.add), [f"pb{bG}", f"t1g{sl}", "convc"], [f"t1g{sl}"])
                    sc.add("dve", I("scalar_tensor_tensor", out=t3g[sl][:, 0:T], in0=ps[:, bG, 2:T + 2], scalar=w2, in1=t2g[sl][:, 0:T],
                                                                   op0=ALU.mult, op1=ALU.add), [f"pb{bG}", f"t1g{sl}", "convc"], [f"t1g{sl}"])
                    sc.add("dve", I("scalar_tensor_tensor", out=t2u[sl][:, 0:T], in0=ps[:, bU, 0:T], scalar=w0u, in1=t1u[sl][:, 0:T],
                                                                   op0=ALU.mult, op1=ALU.add), [f"pb{bU}", f"t1u{sl}", "convc"], [f"t1u{sl}"])
                    sc.add("dve", I("scalar_tensor_tensor", out=t3u[sl][:, 0:T], in0=ps[:, bU, 2:T + 2], scalar=w2u, in1=t2u[sl][:, 0:T],
                                                                   op0=ALU.mult, op1=ALU.add), [f"pb{bU}", f"t1u{sl}", "convc"], [f"t1u{sl}"])
                    sc.add("act", I("activation", out=gg[sl][:, 0:T], in_=t3g[sl][:, 0:T], func=AF.Gelu_apprx_tanh),
                           [f"t1g{sl}"], [f"gg{sl}"])
                    sc.add("dve", I("tensor_tensor", out=aT[c % 3][:, 0:T], in0=gg[sl][:, 0:T], in1=t3u[sl][:, 0:T], op=ALU.mult),
                           [f"gg{sl}", f"t1u{sl}"], [f"aT{c % 3}"])

                def down(c, nbt=nbt):
                    sl = c % 2
                    for tb in range(nbt):
                        for half in range(2):
                            bank = 4 + tb * 2 + half
                            sc.add("pe", I("matmul",
                                ps[:, bank, :], lhsT=aT[c % 3][:, tb * 128:(tb + 1) * 128], rhs=w_down_s[:, c, half * 512:(half + 1) * 512],
                                start=(c == 0), stop=(c == FC - 1)), [f"aT{c % 3}"] + WDN, [f"pb{bank}"])

                up(0)
                up(1)
                conv(0)
                for c in range(FC):
                    if c + 1 < FC:
                        conv(c + 1)
                    if c + 2 < FC:
                        up(c + 2)
                    down(c)
                for tb in range(nbt):
                    yo = yout[tb]
                    YO = f"yout{tb}"
                    for half in range(2):
                        bank = 4 + tb * 2 + half
                        sc.add("dve", I("tensor_tensor",
                            out=yo[:, half * 512:(half + 1) * 512], in0=ps[:, bank, :], in1=xb[tb][:, half * 512:(half + 1) * 512],
                            op=ALU.add), [f"pb{bank}", xbn[tb]], [YO])
                    ykey = f"y{n}_{t0 + tb}"
                    ytiles.append(ykey)
                    dma(dr["y" + n][(t0 + tb) * 128:(t0 + tb + 1) * 128, :], yo[:], [YO], [ykey], YO + "_out")
                t0 += nbt
        sc.add("sp", I("nop", ), ytiles, [])
        sc.add("act", I("nop", ), ytiles, [])
        sc.emit(nc, "B")
    return nc


def _rope_table(pos, S):
    pos = np.clip(pos, 0, S - 1)
    row = (pos // 64).astype(np.float32)
    col = (pos % 64).astype(np.float32)
    inv = (np.float32(10000.0) ** (-np.arange(0, 32, 2, dtype=np.float32) / np.float32(32))).astype(np.float32)
    ang = np.concatenate([row[:, None] * inv[None, :], col[:, None] * inv[None, :]], axis=1).astype(np.float32)
    return np.concatenate([np.cos(ang), np.sin(ang)], axis=1).astype(np.float32)


def _bias_table():
    slopes = np.exp2(-8.0 * np.arange(1, 9, dtype=np.float32) / 8.0).astype(np.float32)
    s = np.arange(128)[:, None]
    q = np.arange(128)[None, :]
    out = np.zeros((128, 8, 3, 128), np.float32)
    for jj in range(3):
        dist = np.abs(q - (s + (jj - 1) * 128)).astype(np.float32)
        valid = dist <= 128
        for h in range(8):
            out[:, h, jj, :] = np.where(valid, -slopes[h] * dist, NEG)
    return out.reshape(128, 8 * 384)


def prepare_core_inputs(core, groups_full, x_by_group, shared):
    m = dict(shared)
    for g in groups_full:
        n, S, NB = g["name"], g["S"], g["NB"]
        x = x_by_group[n]
        b, qtr = core // 4, core % 4
        start = qtr * NB * 128
        NE = NB + 4
        xe = np.zeros((NE * 128, D), np.float32)
        lo, hi = start - 256, start + NB * 128 + 256
        slo, shi = max(lo, 0), min(hi, S)
        xe[slo - lo:shi - lo] = x[b, slo:shi]
        m["xseq" + n] = np.ascontiguousarray(x[b])
        m["xext" + n] = xe
        m["rseq" + n] = g["rseq"]
        m["rext" + n] = _rope_table(np.arange(lo, hi), S)
        blk0 = lo // 128
        val = np.array([1.0 if 0 <= blk0 + e < S // 128 else 0.0 for e in range(NE)], np.float32)
        m["kval" + n] = np.ascontiguousarray(np.broadcast_to(val[None, :], (128, NE)))
        m["bval" + n] = m["kval" + n]
    return m


def make_shared(inp):
    def col(v, nch):
        return np.asarray(v, np.float32).reshape(nch, 128).T

    gcols = np.concatenate([col(inp["norm_mix_g"][0], 8),
                            col(np.concatenate([inp["out_norm_a_g"][0], inp["out_norm_b_g"][0]]), 8),
                            col(inp["norm_ffn_g"][0], 8)], axis=1)
    cw = np.asarray(inp["conv_w"][0], np.float32)
    cb = np.asarray(inp["conv_b"][0], np.float32)
    convc = np.stack([col(cw[0], 44), col(cw[1], 44), col(cw[2], 44), col(cb, 44)], axis=1).reshape(128, 4 * 44)
    hg = np.concatenate([inp["qnorm_a_g"][0], inp["knorm_a_g"][0], inp["qnorm_b_g"][0], inp["knorm_b_g"][0]]).astype(np.float32)
    shared = dict(
        w_in=np.ascontiguousarray(inp["w_in"][0], dtype=np.float32),
        w_out=np.ascontiguousarray(inp["w_out"][0], dtype=np.float32),
        w_up=np.ascontiguousarray(inp["w_up"][0], dtype=np.float32),
        w_down=np.ascontiguousarray(inp["w_down"][0], dtype=np.float32),
        gcols=np.ascontiguousarray(gcols, dtype=np.float32),
        convc=np.ascontiguousarray(convc, dtype=np.float32),
        headg=np.ascontiguousarray(np.broadcast_to(hg[None, :], (128, 256))),
        sinkr=np.ascontiguousarray(np.broadcast_to(np.asarray(inp["sink_b"][0], np.float32)[None, :], (128, 8))),
        biasT=_bias_table(),
        ident=np.eye(128, dtype=np.float32),
    )
    return shared


def run(inp, SP, SS, runner):
    inp = {k: np.asarray(v) for k, v in inp.items()}
    groups = [dict(name="P", S=SP, NB=SP // 512), dict(name="S", S=SS, NB=SS // 512)]
    for g in groups:
        g["rseq"] = _rope_table(np.arange(g["S"]), g["S"])
    nc = build(groups)
    shared = make_shared(inp)
    xg = {"P": np.asarray(inp["x_prompt"], np.float32), "S": np.asarray(inp["x_sample"], np.float32)}
    in_maps = [prepare_core_inputs(c, groups, xg, shared) for c in range(8)]
    results = runner(nc, in_maps)
    outs = []
    for g, key in zip(groups, ("x_prompt", "x_sample")):
        n, S, NB = g["name"], g["S"], g["NB"]
        y = np.zeros((2, S, D), np.float32)
        for c in range(8):
            b, qtr = c // 4, c % 4
            y[b, qtr * NB * 128:(qtr + 1) * NB * 128] = results[c]["y" + n]
        outs.append(y)
    return tuple(outs)


def kernel(**inputs):
    def runner(nc, in_maps):
        res = run_bass_kernel_spmd(nc, in_maps, core_ids=list(range(8)))
        return res.results
    return run(inputs, 16384, 8192, runner)
```

```python
import math
import os
from contextlib import ExitStack
import numpy as np
import concourse.bass as bass
import concourse.mybir as mybir
from concourse.bass_utils import run_bass_kernel_spmd

F32 = mybir.dt.float32
BF16 = mybir.dt.bfloat16
AF = mybir.ActivationFunctionType
ALU = mybir.AluOpType
AX = mybir.AxisListType

D = 1024
DC = 8
INW = 1536
DFF = 2816
FC = 22
EPS = 1e-6
DEBUG_TAGS = os.environ.get("K_TAGS", "0") == "1"
XW = 1028
NEG = -30000.0


def I(name, *args, **kw):
    return (name, args, kw)


class Sched:
    def __init__(self):
        self.ops = []
        self.lw = {}
        self.rd = {}

    @staticmethod
    def _key(op):
        return ("dma", op["dma"]) if op["dma"] is not None else op["eng"]

    def add(self, eng, fn, reads=(), writes=(), dma=None):
        i = len(self.ops)
        deps = {}

        def dep(j):
            k = self._key(self.ops[j])
            if k not in deps or deps[k] < j:
                deps[k] = j

        for b in reads:
            w = self.lw.get(b)
            if w is not None:
                dep(w)
        for b in writes:
            w = self.lw.get(b)
            if w is not None:
                dep(w)
            for j in self.rd.get(b, {}).values():
                dep(j)
        op = dict(eng=eng, fn=fn, deps=[], dma=dma, inc=False, done=None, tag=f"#{i} {fn[0]} r={list(reads)} w={list(writes)}")
        for k, j in deps.items():
            dop = self.ops[j]
            if dop["dma"] is None and dop["eng"] == "pe" and eng == "pe" and dma is None:
                continue
            dop["inc"] = True
            op["deps"].append(j)
        self.ops.append(op)
        me = self._key(op)
        for b in writes:
            self.lw[b] = i
            self.rd[b] = {}
        for b in reads:
            self.rd.setdefault(b, {})[me] = i
        return i

    def emit(self, nc, tag):
        with ExitStack() as es:
            sems = {}
            cnt = {}
            for op in self.ops:
                k = self._key(op)
                if op["dma"] is not None:
                    cnt[k] = cnt.get(k, 0) + 16
                    op["done"] = (k, cnt[k])
                elif op["inc"]:
                    cnt[k] = cnt.get(k, 0) + 1
                    op["done"] = (k, cnt[k])
            for n, k in enumerate(cnt.keys()):
                sems[k] = es.enter_context(nc.semaphore(f"{tag}_s{n}"))
            block = es.enter_context(nc.Block())
            ops = self.ops

            def run(engname):
                def body(eng):
                    waited = {}
                    for op in ops:
                        if op["eng"] != engname:
                            continue
                        for j in op["deps"]:
                            k, v = ops[j]["done"]
                            if waited.get(k, 0) >= v:
                                continue
                            eng.wait_ge(sems[k], v)
                            waited[k] = v
                        name, args, kw = op["fn"]
                        inst = getattr(eng, name)(*args, **kw)
                        if DEBUG_TAGS:
                            inst.annotate(op["tag"])
                        if op["done"] is not None:
                            k, v = op["done"]
                            inst.then_inc(sems[k], 16 if op["dma"] is not None else 1)
                return body

            block.sync(run("sp"))
            block.tensor(run("pe"))
            block.vector(run("dve"))
            block.scalar(run("act"))
            block.gpsimd(run("pool"))


def build(groups):
    nc = bass.Bass("TRN2", target_bir_lowering=False)

    def din(name, shape):
        return nc.dram_tensor(name, list(shape), F32, kind="ExternalInput").ap()

    def dout(name, shape):
        return nc.dram_tensor(name, list(shape), F32, kind="ExternalOutput").ap()

    dr = {}
    for g in groups:
        n = g["name"]
        S, NB = g["S"], g["NB"]
        NE = NB + 4
        dr["xseq" + n] = din("xseq" + n, [S, D])
        dr["xext" + n] = din("xext" + n, [NE * 128, D])
        dr["rseq" + n] = din("rseq" + n, [S, 64])
        dr["rext" + n] = din("rext" + n, [NE * 128, 64])
        dr["kval" + n] = din("kval" + n, [128, NE])
        dr["bval" + n] = din("bval" + n, [128, NE])
        dr["y" + n] = dout("y" + n, [NB * 128, D])
        dr["x1" + n] = nc.dram_tensor("x1" + n, [(NB + 2) * 128, XW], F32).ap()
    w_in = din("w_in", [D, INW])
    w_out = din("w_out", [D, D])
    w_up = din("w_up", [D, 2 * DFF])
    w_down = din("w_down", [DFF, D])
    gcols_d = din("gcols", [128, 24])
    convc_d = din("convc", [128, 4 * 44])
    headg_d = din("headg", [128, 4 * 64])
    sink_d = din("sinkr", [128, 8])
    bias_d = din("biasT", [128, 8 * 384])
    ident_d = din("ident", [128, 128])

    SMAX = max(g["S"] for g in groups)
    NEMAX = max(g["NB"] for g in groups) + 4

    with ExitStack() as es:
        def sb(name, shape, dt=F32):
            return es.enter_context(nc.sbuf_tensor("sA_" + name, list(shape), dt))

        ps = es.enter_context(nc.psum_tensor("psA", [128, 8, 512], F32))
        identf = sb("identf", [128, 128])
        identb = sb("identb", [128, 128], BF16)
        gcols = sb("gcols", [128, 24])
        headg = sb("headg", [128, 4, 64])
        sinkt = sb("sinkt", [128, 8])
        esink = sb("esink", [128, 8])
        biasb = sb("biasb", [128, 8, 384], BF16)
        kval = sb("kval", [128, NEMAX])
        bval = sb("bval", [128, NEMAX])
        w_in_s = sb("w_in_s", [128, DC, INW], BF16)
        w_out_s = sb("w_out_s", [128, DC, D], BF16)
        KT = sb("KT", [128, SMAX], BF16)
        Vaug = sb("Vaug", [128, SMAX // 128, 2, 65], BF16)
        KbT = sb("KbT", [128, NEMAX * 128], BF16)
        Vb = sb("Vb", [128, NEMAX, 2, 65], BF16)
        xt = [sb(f"xt{i}", [128, D]) for i in range(2)]
        xs0 = sb("xs0", [128, D], BF16)
        hT = [sb(f"hT{i}", [128, DC, 128], BF16) for i in range(2)]
        sqj = sb("sqj", [128, D])
        ropet = sb("ropet", [128, 64])
        ssx = sb("ssx", [128, 1])
        rsx = sb("rsx", [128, 1])
        qraw = sb("qraw", [128, 1024])
        ssq = sb("ssq", [128, 16])
        rsq = sb("rsq", [128, 16])
        arena2 = sb("arena2", [128, 5120])
        QaT = [arena2[:, i * 1024:(i + 1) * 1024].bitcast(BF16).rearrange("p (j t) -> p j t", j=4) for i in range(2)]
        QbT = arena2[:, 2048:3072].bitcast(BF16).rearrange("p (j t) -> p j t", j=4)
        pT = [arena2[:, 3072 + i * 512:3584 + i * 512].bitcast(BF16).rearrange("p (j t) -> p j t", j=2) for i in range(2)]
        qra = arena2[:, 4096:4352].bitcast(BF16)
        qrb = arena2[:, 4352:4608].bitcast(BF16)
        yT = arena2[:, 4608:5120].bitcast(BF16).rearrange("p (c t) -> p c t", c=DC)
        oT = sb("oT", [128, 512])
        pT2 = sb("pT2", [128, 2, 512], BF16)
        den = sb("den", [128, 4])
        rden = sb("rden", [128, 4])
        ssy = sb("ssy", [128, 8])
        rsy = sb("rsy", [128, 8])
        x1t = sb("x1t", [128, XW])
        tA = x1t[:, 0:256]
        tB = x1t[:, 256:512]
        ss1 = sb("ss1", [128, 1])
        rs1 = sb("rs1", [128, 1])
        fsc = sb("fsc", [128, 1])
        arena = sb("arena", [128, 6144])
        ya = arena[:, 0:2048].rearrange("p (b f) -> p b f", b=4)
        yb = arena[:, 2048:4096].rearrange("p (b f) -> p b f", b=4)
        ybf = arena[:, 4096:6144].bitcast(BF16).rearrange("p (b f) -> p b f", b=4)
        KW = 5
        _ar = [[arena, 0, 6144], [arena2, 0, 5120]]

        def carve(words):
            for a in _ar:
                if a[1] + words <= a[2]:
                    v = a[0][:, a[1]:a[1] + words]
                    a[1] += words
                    return v
            raise AssertionError("KV scratch arena exhausted")

        xtK, hTK, xsK, krawK, sqkK, knK, kn2K, krK, ropeK = [], [], [], [], [], [], [], [], []
        for i in range(KW):
            xtK.append(xt[i][:] if i < 2 else carve(1024))
            hTK.append(hT[i][:] if i < 2 else carve(512).bitcast(BF16).rearrange("p (c t) -> p c t", c=DC))
            xsK.append(carve(512).bitcast(BF16))
            krawK.append(carve(256))
            sqkK.append(carve(128))
            knK.append(carve(128))
            kn2K.append(carve(128))
            krK.append(carve(64).bitcast(BF16))
            ropeK.append(carve(64))
        ssxK = [sb(f"ssxK{i}", [128, 1]) for i in range(KW)]
        rsxK = [sb(f"rsxK{i}", [128, 1]) for i in range(KW)]
        sskK = [sb(f"sskK{i}", [128, 2]) for i in range(KW)]
        rskK = [sb(f"rskK{i}", [128, 2]) for i in range(KW)]
        ATT_NAMES = ["ya", "yb", "ybf", "QaT0", "QaT1", "QbT", "pT0", "pT1", "qra", "qrb", "yT"]
        KV_NAMES = ([f"xtK{i}" for i in range(KW)] + [f"hTK{i}" for i in range(KW)] + [f"xsK{i}" for i in range(KW)]
                    + [f"krawK{i}" for i in range(KW)] + [f"sqkK{i}" for i in range(KW)] + [f"knK{i}" for i in range(KW)]
                    + [f"kn2K{i}" for i in range(KW)] + [f"krK{i}" for i in range(KW)] + [f"ropeK{i}" for i in range(KW)])

        pT.append(pT2[:])
        sc = Sched()
        PB = [f"pb{i}" for i in range(8)]
        st_slot = [ps[:, 0:2, :], ps[:, 2:4, :]]
        ST = [["pb0", "pb1"], ["pb2", "pb3"]]
        oacc = [ps[:, 4, :], ps[:, 5, :]]
        OA = ["pb4", "pb5"]
        bank6 = ps[:, 6, :]
        bank6b = ps[:, 6, :].bitcast(BF16)
        bank7 = ps[:, 7, :]
        bank7b = ps[:, 7, :].bitcast(BF16)
        cnt = dict(xt=0, st=0, oa=0, pt=0)

        def dma(out, in_, reads, writes, key):
            sc.add("sp", I("dma_start", out=out, in_=in_), reads=reads, writes=writes, dma=key)

        def fence():
            sc.add("dve", I("memset", fsc[:], 0.0), [], ATT_NAMES + KV_NAMES + ["fsc"])

        dma(identf[:], ident_d[:, :], [], ["identf"], "identf")
        sc.add("dve", I("tensor_copy", out=identb[:], in_=identf[:]), ["identf"], ["identb"])
        dma(gcols[:], gcols_d[:, :], [], ["gcols"], "gcols")
        dma(headg[:].rearrange("p a d -> p (a d)"), headg_d[:, :], [], ["headg"], "headg")
        dma(sinkt[:], sink_d[:, :], [], ["sinkt"], "sinkt")
        sc.add("act", I("activation", out=esink[:], in_=sinkt[:], func=AF.Exp), ["sinkt"], ["esink"])
        k = 0
        for i in range(3):
            sl = k % 2
            k += 1
            dma(xt[sl][:], bias_d[:, i * 1024:(i + 1) * 1024], [], [f"xtK{sl}"], f"xtK{sl}")
            sc.add("dve", I("tensor_copy", out=biasb[:].rearrange("p h k -> p (h k)")[:, i * 1024:(i + 1) * 1024], in_=xt[sl][:]),
                   [f"xtK{sl}"], ["biasb"])
        w_in_v = w_in.rearrange("(c p) n -> p c n", p=128)
        w_out_v = w_out.rearrange("(c p) n -> p c n", p=128)
        for c in range(DC):
            for h in range(2):
                sl = k % 2
                k += 1
                dma(xt[sl][:, 0:768], w_in_v[:, c, h * 768:(h + 1) * 768], [], [f"xtK{sl}"], f"xtK{sl}")
                sc.add("dve", I("tensor_scalar",
                                out=w_in_s[:, c, h * 768:h * 768 + 512].rearrange("p (j two d) -> p two j d", j=4, two=2),
                                in0=xt[sl][:, 0:512].rearrange("p (two j d) -> p two j d", two=2, j=4), scalar1=gcols[:, c:c + 1],
                                scalar2=None, op0=ALU.mult), [f"xtK{sl}", "gcols"], ["w_in_s"])
                sc.add("act", I("activation", out=w_in_s[:, c, h * 768 + 512:(h + 1) * 768], in_=xt[sl][:, 512:768], func=AF.Copy,
                                scale=gcols[:, c:c + 1]), [f"xtK{sl}", "gcols"], ["w_in_s2"])
        for c in range(DC):
            sl = k % 2
            k += 1
            dma(xt[sl][:], w_out_v[:, c, :], [], [f"xtK{sl}"], f"xtK{sl}")
            sc.add("dve", I("tensor_scalar", out=w_out_s[:, c, :], in0=xt[sl][:], scalar1=gcols[:, 8 + c:9 + c],
                            scalar2=None, op0=ALU.mult), [f"xtK{sl}", "gcols"], ["w_out_s"])
        cnt["xt"] = k
        WIN = ["w_in_s", "w_in_s2"]

        def rstd_ops(ss_ap, rs_ap, n, ssname, rsname, bias2=0.0):
            sc.add("act", I("activation", out=rs_ap, in_=ss_ap, func=AF.Ln, scale=1.0 / n, bias=EPS), [ssname], [rsname])
            sc.add("act", I("activation", out=rs_ap, in_=rs_ap, func=AF.Exp, scale=-0.5, bias=bias2), [rsname], [rsname])

        def rope_ops(xin, H, rope_ap, rname, out_ap_fn, inname, outname):
            xv = xin.rearrange("p (h a f e) -> p h a f e", h=H, a=2, f=2)
            x1 = xv[:, :, :, 0, :]
            x2 = xv[:, :, :, 1, :]
            C = rope_ap[:, 0:32].rearrange("p (a e) -> p a e", a=2).unsqueeze(1).to_broadcast([128, H, 2, 16])
            Sn = rope_ap[:, 32:64].rearrange("p (a e) -> p a e", a=2).unsqueeze(1).to_broadcast([128, H, 2, 16])
            tAv = tA[:, 0:H * 32].rearrange("p (h a e) -> p h a e", h=H, a=2)
            tBv = tB[:, 0:H * 32].rearrange("p (h a e) -> p h a e", h=H, a=2)
            sc.add("dve", I("tensor_tensor", out=tAv, in0=x1, in1=C, op=ALU.mult), [inname, rname], ["x1t"])
            sc.add("dve", I("tensor_tensor", out=tBv, in0=x2, in1=Sn, op=ALU.mult), [inname, rname], ["x1t"])
            sc.add("dve", I("tensor_tensor", out=out_ap_fn(0), in0=tAv, in1=tBv, op=ALU.subtract), ["x1t"], [outname])
            sc.add("dve", I("tensor_tensor", out=tAv, in0=x2, in1=C, op=ALU.mult), [inname, rname], ["x1t"])
            sc.add("dve", I("tensor_tensor", out=tBv, in0=x1, in1=Sn, op=ALU.mult), [inname, rname], ["x1t"])
            sc.add("dve", I("tensor_tensor", out=out_ap_fn(1), in0=tAv, in1=tBv, op=ALU.add), ["x1t"], [outname])

        def kv_gen(src_rows, rope_rows, kcol, gidx, KTt, Vt, idx, ktname, vname, s):
            X, XS, HT, KR, SQ, KN, KN2, KRR, RP = (f"xtK{s}", f"xsK{s}", f"hTK{s}", f"krawK{s}", f"sqkK{s}", f"knK{s}",
                                                   f"kn2K{s}", f"krK{s}", f"ropeK{s}")
            bT, bP = s, s
            bTb = ps[:, bT, :].bitcast(BF16)
            dma(xtK[s], src_rows, [], [X], X)
            if rope_rows is not None:
                dma(ropeK[s], rope_rows, [], [RP], RP)
            yield
            sc.add("act", I("activation", out=qraw[:], in_=xtK[s], func=AF.Square, accum_out=ssxK[s][:]), [X], ["qraw", f"ssxK{s}"])
            yield
            rstd_ops(ssxK[s][:], rsxK[s][:], D, f"ssxK{s}", f"rsxK{s}")
            yield
            sc.add("dve", I("tensor_scalar", out=xsK[s], in0=xtK[s], scalar1=rsxK[s][:, 0:1], scalar2=None, op0=ALU.mult),
                   [X, f"rsxK{s}"], [XS])
            yield
            for c in range(DC):
                sc.add("pe", I("transpose", out=bTb[:, c * 128:(c + 1) * 128], in_=xsK[s][:, c * 128:(c + 1) * 128], identity=identb[:]),
                       [XS, "identb"], [PB[bT]])
            yield
            sc.add("act", I("activation", out=hTK[s].rearrange("p c t -> p (c t)"), in_=bTb[:, :], func=AF.Copy), [PB[bT]], [HT])
            yield
            for c in range(DC):
                sc.add("pe", I("matmul", ps[:, bP, 0:256], lhsT=hTK[s][:, c, :], rhs=w_in_s[:, c, kcol:kcol + 256],
                               start=(c == 0), stop=(c == DC - 1)), [HT] + WIN, [PB[bP]])
            yield
            sc.add("act", I("activation", out=krawK[s], in_=ps[:, bP, 0:256], func=AF.Copy), [PB[bP]], [KR])
            yield
            sc.add("pool", I("tensor_copy", out=Vt[:, idx, :, 0:64], in_=krawK[s][:, 128:256].rearrange("p (h d) -> p h d", h=2)),
                   [KR], [vname])
            sc.add("pool", I("tensor_tensor", out=sqkK[s], in0=krawK[s][:, 0:128], in1=krawK[s][:, 0:128], op=ALU.mult), [KR], [SQ])
            yield
            sc.add("dve", I("tensor_reduce", out=sskK[s][:], in_=sqkK[s].rearrange("p (h d) -> p h d", h=2), axis=AX.X, op=ALU.add),
                   [SQ], [f"sskK{s}"])
            yield
            rstd_ops(sskK[s][:], rskK[s][:], 64, f"sskK{s}", f"rskK{s}")
            yield
            sc.add("dve", I("tensor_tensor", out=knK[s].rearrange("p (h d) -> p h d", h=2),
                            in0=krawK[s][:, 0:128].rearrange("p (h d) -> p h d", h=2),
                            in1=rskK[s][:].unsqueeze(2).to_broadcast([128, 2, 64]), op=ALU.mult), [KR, f"rskK{s}"], [KN])
            gv = headg[:, gidx, :].unsqueeze(1).to_broadcast([128, 2, 64])
            if rope_rows is not None:
                sc.add("dve", I("tensor_tensor", out=kn2K[s].rearrange("p (h d) -> p h d", h=2),
                                in0=knK[s].rearrange("p (h d) -> p h d", h=2), in1=gv, op=ALU.mult), [KN, "headg"], [KN2])
                krv = krK[s].rearrange("p (h a f e) -> p h a f e", h=2, a=2, f=2)
                rope_ops(kn2K[s], 2, ropeK[s], RP, lambda half: krv[:, :, :, half, :], KN2, KRR)
            else:
                sc.add("dve", I("tensor_tensor", out=krK[s].rearrange("p (h d) -> p h d", h=2),
                                in0=knK[s].rearrange("p (h d) -> p h d", h=2), in1=gv, op=ALU.mult), [KN, "headg"], [KRR])
            yield
            sc.add("pe", I("transpose", out=bTb[:, 0:128], in_=krK[s], identity=identb[:]), [KRR, "identb"], [PB[bT]])
            yield
            sc.add("act", I("activation", out=KTt[:, idx * 128:(idx + 1) * 128], in_=bTb[:, 0:128], func=AF.Copy), [PB[bT]], [ktname])
            yield

        def interleave(gens, width, stagger=3):
            active = []
            it = iter(gens)
            since = stagger
            done = False
            while True:
                if not done and len(active) < width and since >= stagger:
                    gnew = next(it, None)
                    if gnew is None:
                        done = True
                    else:
                        active.append(gnew)
                        since = 0
                if not active:
                    if done:
                        break
                    since = stagger
                    continue
                since += 1
                for gg_ in list(active):
                    try:
                        next(gg_)
                    except StopIteration:
                        active.remove(gg_)

        def evac_copy(ob, nb):
            sc.add("dve", I("tensor_copy", out=oT[0:65, 0:nb * 128], in_=oacc[ob][0:65, 0:nb * 128]), [OA[ob]], ["oT"])

        def evac_finish(nb, h, ydst, yname, sink):
            b6v = bank6[:, 0:4 * 65].rearrange("p (b d) -> p b d", d=65)
            for bi in range(nb):
                sc.add("pe", I("transpose", out=b6v[:, bi, :], in_=oT[0:65, bi * 128:(bi + 1) * 128],
                               identity=identf[0:65, 0:65]), ["oT", "identf"], ["pb6"])
            if sink:
                sc.add("dve", I("tensor_scalar", out=den[:, 0:nb], in0=b6v[:, 0:nb, 64], scalar1=esink[:, h:h + 1],
                                scalar2=None, op0=ALU.add), ["pb6", "esink"], ["den"])
                sc.add("dve", I("reciprocal", out=rden[:, 0:nb], in_=den[:, 0:nb]), ["den"], ["rden"])
            else:
                sc.add("dve", I("reciprocal", out=rden[:, 0:nb], in_=b6v[:, 0:nb, 64]), ["pb6"], ["rden"])
            sc.add("dve", I("tensor_tensor", out=ydst[:, 0:nb, h * 64:(h + 1) * 64], in0=b6v[:, 0:nb, 0:64],
                            in1=rden[:, 0:nb].unsqueeze(2).to_broadcast([128, nb, 64]), op=ALU.mult), ["pb6", "rden"], [yname])

        def evac_O(ob, nb, h, ydst, yname, sink):
            evac_copy(ob, nb)
            evac_finish(nb, h, ydst, yname, sink)

        def proj_gen(g, tile, qslot):
            n = g["name"]
            for bi, eb in enumerate(tile):
                sl = cnt["xt"] % 2
                cnt["xt"] += 1
                X = f"xtK{sl}"
                dma(ropet[:], dr["rext" + n][eb * 128:(eb + 1) * 128, :], [], ["ropet"], "ropet")
                dma(xt[sl][:], dr["xext" + n][eb * 128:(eb + 1) * 128, :], [], [X], X)
                yield
                sc.add("pool", I("tensor_tensor", out=sqj[:], in0=xt[sl][:], in1=xt[sl][:], op=ALU.mult), [X], ["sqj"])
                yield
                sc.add("dve", I("tensor_reduce", out=ssx[:], in_=sqj[:], axis=AX.X, op=ALU.add), ["sqj"], ["ssx"])
                yield
                rstd_ops(ssx[:], rsx[:], D, "ssx", "rsx")
                yield
                sc.add("dve", I("tensor_scalar", out=xs0[:], in0=xt[sl][:], scalar1=rsx[:, 0:1], scalar2=None, op0=ALU.mult),
                       [X, "rsx"], ["xs0"])
                yield
                for c in range(DC):
                    sc.add("pe", I("transpose", out=bank7b[:, c * 128:(c + 1) * 128], in_=xs0[:, c * 128:(c + 1) * 128], identity=identb[:]),
                           ["xs0", "identb"], ["pb7"])
                yield
                sc.add("dve", I("tensor_copy", out=hT[0][:].rearrange("p c t -> p (c t)"), in_=bank7b[:, :]), ["pb7"], ["hTK0"])
                yield
                for half, col0 in ((0, 0), (1, 768)):
                    for c in range(DC):
                        sc.add("pe", I("matmul", bank7[:, :], lhsT=hT[0][:, c, :], rhs=w_in_s[:, c, col0:col0 + 512],
                                       start=(c == 0), stop=(c == DC - 1)), ["hTK0"] + WIN, ["pb7"])
                    yield
                    sc.add("dve", I("tensor_copy", out=qraw[:, half * 512:(half + 1) * 512], in_=bank7[:, :]), ["pb7"], ["qraw"])
                    yield
                sc.add("pool", I("tensor_tensor", out=sqj[:], in0=qraw[:], in1=qraw[:], op=ALU.mult), ["qraw"], ["sqj"])
                yield
                sc.add("dve", I("tensor_reduce", out=ssq[:], in_=sqj[:].rearrange("p (h d) -> p h d", d=64), axis=AX.X, op=ALU.add),
                       ["sqj"], ["ssq"])
                yield
                rstd_ops(ssq[:], rsq[:], 64, "ssq", "rsq", bias2=-math.log(8.0))
                yield
                sc.add("dve", I("tensor_tensor", out=qraw[:].rearrange("p (h d) -> p h d", d=64),
                                in0=qraw[:].rearrange("p (h d) -> p h d", d=64),
                                in1=rsq[:].unsqueeze(2).to_broadcast([128, 16, 64]), op=ALU.mult), ["qraw", "rsq"], ["qraw"])
                sc.add("dve", I("tensor_tensor", out=qrb[:].rearrange("p (h d) -> p h d", d=64),
                                in0=qraw[:, 512:1024].rearrange("p (h d) -> p h d", d=64),
                                in1=headg[:, 2, :].unsqueeze(1).to_broadcast([128, 8, 64]), op=ALU.mult), ["qraw", "headg"], ["qrb"])
                yield
                for j in range(4):
                    sc.add("pe", I("transpose", out=bank7b[:, j * 128:(j + 1) * 128], in_=qrb[:, j * 128:(j + 1) * 128], identity=identb[:]),
                           ["qrb", "identb"], ["pb7"])
                sc.add("dve", I("tensor_tensor", out=qraw[:, 0:512].rearrange("p (h d) -> p h d", d=64),
                                in0=qraw[:, 0:512].rearrange("p (h d) -> p h d", d=64),
                                in1=headg[:, 0, :].unsqueeze(1).to_broadcast([128, 8, 64]), op=ALU.mult), ["qraw", "headg"], ["qraw"])
                qrav = qra[:].rearrange("p (h a f e) -> p h a f e", h=8, a=2, f=2)
                rope_ops(qraw[:, 0:512], 8, ropet[:], "ropet", lambda half: qrav[:, :, :, half, :], "qraw", "qra")
                yield
                sc.add("dve", I("tensor_copy", out=QbT[:, :, bi * 128:(bi + 1) * 128],
                                in_=bank7b[:, 0:512].rearrange("p (j t) -> p j t", j=4)), ["pb7"], ["QbT"])
                yield
                for j in range(4):
                    sc.add("pe", I("transpose", out=bank7b[:, j * 128:(j + 1) * 128], in_=qra[:, j * 128:(j + 1) * 128], identity=identb[:]),
                           ["qra", "identb"], ["pb7"])
                yield
                sc.add("dve", I("tensor_copy", out=QaT[qslot][:, :, bi * 128:(bi + 1) * 128],
                                in_=bank7b[:, 0:512].rearrange("p (j t) -> p j t", j=4)), ["pb7"], [f"QaT{qslot}"])
                yield

        def tail0(nb):
            for grp, ysrc, yn in ((0, ya, "ya"), (1, yb, "yb")):
                for bi in range(nb):
                    sc.add("act", I("activation", out=oT[:, 0:512], in_=ysrc[:, bi, :], func=AF.Square,
                                    accum_out=ssy[:, grp * 4 + bi:grp * 4 + bi + 1]), [yn], ["oT", "ssy"])
            rstd_ops(ssy[:], rsy[:], 512, "ssy", "rsy")
            for grp, ysrc, yn in ((0, ya, "ya"), (1, yb, "yb")):
                for bi in range(nb):
                    sc.add("dve", I("tensor_scalar", out=ybf[:, bi, grp * 512:(grp + 1) * 512], in0=ysrc[:, bi, :],
                                    scalar1=rsy[:, grp * 4 + bi:grp * 4 + bi + 1], scalar2=None, op0=ALU.mult), [yn, "rsy"], ["ybf"])

        def tail_gen(g, tile):
            n = g["name"]
            for bi, eb in enumerate(tile):
                dma(x1t[:, 0:D], dr["xext" + n][eb * 128:(eb + 1) * 128, :], [], ["x1t"], "x1t_in")
                for c in range(DC):
                    sc.add("pe", I("transpose", out=bank7b[:, c * 128:(c + 1) * 128], in_=ybf[:, bi, c * 128:(c + 1) * 128], identity=identb[:]),
                           ["ybf", "identb"], ["pb7"])
                yield
                sc.add("dve", I("tensor_copy", out=yT[:].rearrange("p c t -> p (c t)"), in_=bank7b[:, :]), ["pb7"], ["yT"])
                yield
                for half in range(2):
                    for c in range(DC):
                        sc.add("pe", I("matmul", bank7[:, :], lhsT=yT[:, c, :], rhs=w_out_s[:, c, half * 512:(half + 1) * 512],
                                       start=(c == 0), stop=(c == DC - 1)), ["yT", "w_out_s"], ["pb7"])
                    yield
                    sc.add("dve", I("tensor_tensor", out=x1t[:, half * 512:(half + 1) * 512], in0=bank7[:, :],
                                    in1=x1t[:, half * 512:(half + 1) * 512], op=ALU.add), ["pb7", "x1t"], ["x1t"])
                    yield
                sc.add("pool", I("tensor_tensor", out=sqj[:], in0=x1t[:, 0:D], in1=x1t[:, 0:D], op=ALU.mult), ["x1t"], ["sqj"])
                yield
                sc.add("dve", I("tensor_reduce", out=ss1[:], in_=sqj[:], axis=AX.X, op=ALU.add), ["sqj"], ["ss1"])
                yield
                rstd_ops(ss1[:], rs1[:], D, "ss1", "rs1")
                yield
                sc.add("dve", I("tensor_tensor", out=x1t[:, D:D + 1], in0=rs1[:], in1=bval[:, eb:eb + 1], op=ALU.mult), ["rs1", "bval"], ["x1t"])
                sc.add("dve", I("memset", x1t[:, D + 1:XW], 0.0), [], ["x1t"])
                yield
                dkey = f"x1{n}_{eb}"
                dma(dr["x1" + n][(eb - 1) * 128:eb * 128, :], x1t[:], ["x1t"], [dkey], "x1t_out")
                yield

        def window_attn(g, tile):
            nb = len(tile)
            halves = [list(range(0, min(2, nb)))] + ([list(range(2, nb))] if nb > 2 else [])
            steps = [(h, hv) for h in range(8) for hv in halves]
            pend = None

            def pv(h, hv, slot, ob):
                two = h // 4
                for bi_l, bi in enumerate(hv):
                    eb = tile[bi]
                    for jj in range(3):
                        ke = eb - 1 + jj
                        sc.add("pe", I("matmul", oacc[ob][0:65, bi * 128:(bi + 1) * 128], lhsT=Vb[:, ke, two, :],
                                       rhs=pT[slot][:, bi_l, jj * 128:(jj + 1) * 128], start=(jj == 0), stop=(jj == 2)),
                               ["Vb", f"pT{slot}"], [OA[ob]])

            for si, (h, hv) in enumerate(steps):
                j, two = h % 4, h // 4
                r0 = two * 64
                slot = cnt["st"] % 2
                cnt["st"] += 1
                if hv is halves[0]:
                    cnt["oa"] += 1
                ob = cnt["oa"] % 2
                for bi_l, bi in enumerate(hv):
                    eb = tile[bi]
                    for jj in range(3):
                        ke = eb - 1 + jj
                        sc.add("pe", I("matmul", st_slot[slot][:, bi_l, jj * 128:(jj + 1) * 128], lhsT=KbT[r0:r0 + 64, ke * 128:(ke + 1) * 128],
                                       rhs=QbT[r0:r0 + 64, j, bi * 128:(bi + 1) * 128], start=True, stop=True),
                               ["KbT", "QbT"], ST[slot])
                nl = len(hv)
                sc.add("dve", I("tensor_tensor", out=st_slot[slot][:, 0:nl, 0:384], in0=st_slot[slot][:, 0:nl, 0:384],
                                in1=biasb[:, h, :].unsqueeze(1).to_broadcast([128, nl, 384]), op=ALU.add), ST[slot] + ["biasb"], ST[slot])
                sc.add("act", I("activation", out=pT[slot][:, 0:nl, 0:384], in_=st_slot[slot][:, 0:nl, 0:384], func=AF.Exp),
                       ST[slot], [f"pT{slot}"])
                if pend is not None:
                    pv(*pend[:4])
                    if pend[4]:
                        evac_O(pend[3], nb, pend[0], yb, "yb", True)
                pend = (h, hv, slot, ob, hv is halves[-1])
            pv(*pend[:4])
            evac_O(pend[3], nb, pend[0], yb, "yb", True)

        def global_attn(g, tile, qslot, bg, nyield):
            nb = len(tile)
            Tq = nb * 128
            nkb = g["S"] // 128
            bgstep = max(1, (4 * nkb) // (nyield + 6))
            it = 0
            deferred = []
            for j in range(4):
                obs = (0, 1)
                pend = []

                def pv(pkb, pslot):
                    for two in range(2):
                        sc.add("pe", I("matmul", oacc[obs[two]][0:65, 0:Tq], lhsT=Vaug[:, pkb, two, :], rhs=pT[pslot][:, two, 0:Tq],
                                       start=(pkb == 0), stop=(pkb == nkb - 1)), ["Vaug", f"pT{pslot}"], [OA[obs[two]]])

                for kb in range(nkb):
                    slot = cnt["st"] % 2
                    cnt["st"] += 1
                    pslot = cnt["pt"] % 3
                    cnt["pt"] += 1
                    for two in range(2):
                        r0 = two * 64
                        sc.add("pe", I("matmul", st_slot[slot][:, two, 0:Tq], lhsT=KT[r0:r0 + 64, kb * 128:(kb + 1) * 128],
                                       rhs=QaT[qslot][r0:r0 + 64, j, 0:Tq], start=True, stop=True), ["KT", f"QaT{qslot}"], ST[slot])
                    sc.add("act", I("activation", out=pT[pslot][:, :, 0:Tq], in_=st_slot[slot][:, :, 0:Tq], func=AF.Exp),
                           ST[slot], [f"pT{pslot}"])
                    pend.append((kb, pslot))
                    if kb in (1, 3) and deferred:
                        deferred.pop(0)()
                    if len(pend) > 2:
                        pv(*pend.pop(0))
                    it += 1
                    if it % bgstep == 0 and os.environ.get("K_BG", "1") == "1":
                        next(bg, None)
                while pend:
                    pv(*pend.pop(0))
                for d_ in deferred:
                    d_()
                deferred.clear()
                evac_copy(obs[0], nb)
                deferred.append(lambda j=j: (evac_finish(nb, j, ya, "ya", False), evac_copy(obs[1], nb)))
                deferred.append(lambda j=j: evac_finish(nb, 4 + j, ya, "ya", False))
            for d_ in deferred:
                d_()
            deferred.clear()
            for _ in bg:
                pass

        def chain(*gens):
            for gen in gens:
                if gen is None:
                    continue
                for _ in gen:
                    yield

        sc.add("pool", I("memset", ssy[:], 1.0), [], ["ssy"])
        sc.add("pool", I("memset", Vaug[:].rearrange("p b k d -> p (b k d)"), 1.0), [], ["Vaug"])
        for g in groups:
            n = g["name"]
            S, NB = g["S"], g["NB"]
            NE = NB + 4
            fence()
            dma(kval[:, 0:NE], dr["kval" + n][:, :], [], ["kval"], "kval")
            dma(bval[:, 0:NE], dr["bval" + n][:, :], [], ["bval"], "bval")
            for kvh in range(2):
                sc.add("pool", I("tensor_copy", out=Vb[:, 0:NE, kvh, 64], in_=kval[:, 0:NE]), ["kval"], ["Vb"])
            gens = []
            ctr = 0
            for b in range(S // 128):
                gens.append(kv_gen(dr["xseq" + n][b * 128:(b + 1) * 128, :], dr["rseq" + n][b * 128:(b + 1) * 128, :],
                                   512, 1, KT, Vaug, b, "KT", "Vaug", ctr % KW))
                ctr += 1
            for eb in range(NE):
                gens.append(kv_gen(dr["xext" + n][eb * 128:(eb + 1) * 128, :], None, 1280, 3, KbT, Vb, eb, "KbT", "Vb", ctr % KW))
                ctr += 1
            interleave(gens, KW)
            fence()
            blocks = list(range(1, NB + 3))
            tiles = [blocks[i:i + 4] for i in range(0, len(blocks), 4)]
            for _ in proj_gen(g, tiles[0], 0):
                pass
            prev_tail = None
            for ti, tile in enumerate(tiles):
                qslot = ti % 2
                window_attn(g, tile)
                nxt = proj_gen(g, tiles[ti + 1], (ti + 1) % 2) if ti + 1 < len(tiles) else None
                ny = (11 * 4 if prev_tail is not None else 0) + (20 * 4 if nxt is not None else 0)
                global_attn(g, tile, qslot, chain(prev_tail, nxt), ny)
                tail0(len(tile))
                prev_tail = tail_gen(g, tile)
            for _ in prev_tail:
                pass
        allx1 = [f"x1{g['name']}_{eb}" for g in groups for eb in range(1, g["NB"] + 3)]
        sc.add("sp", I("nop"), allx1, [])
        sc.add("act", I("nop"), allx1, [])
        sc.emit(nc, "A")

    with ExitStack() as es:
        def sb(name, shape, dt=F32):
            return es.enter_context(nc.sbuf_tensor("sB_" + name, list(shape), dt))

        ps = es.enter_context(nc.psum_tensor("psB", [128, 8, 512], F32))
        identf = sb("identfB", [128, 128])
        identb = sb("identbB", [128, 128], BF16)
        gcols = sb("gcolsB", [128, 24])
        convc = sb("convc", [128, 4, 44])
        w_up_s = sb("w_up_s", [128, DC, 2 * DFF], BF16)
        w_down_s = sb("w_down_s", [128, FC, D], BF16)
        SW = 1408
        NSTG = 4
        stg = [sb(f"stg{i}", [128, SW]) for i in range(NSTG)]
        TBK = 2
        x1s = [sb(f"x1s{i}", [128, XW]) for i in range(2 * TBK)]
        halo = [sb(f"halo{i}", [2, XW]) for i in range(2)]
        xsB = [sb(f"xsB{i}", [128, D], BF16) for i in range(2)]
        xsh = sb("xsh", [2, D], BF16)
        h2T = sb("h2T", [128, DC, 258], BF16)
        t1g = [sb(f"t1g{i}", [128, 256]) for i in range(2)]
        t1u = [sb(f"t1u{i}", [128, 256]) for i in range(2)]
        t2g, t2u, t3g, t3u = t1g, t1u, t1g, t1u
        gg = [sb(f"gg{i}", [128, 256]) for i in range(2)]
        aT = [sb(f"aT{i}", [128, 256], BF16) for i in range(3)]
        yout = [sb(f"yout{i}", [128, D]) for i in range(2)]

        sc = Sched()

        def dma(out, in_, reads, writes, key):
            sc.add("sp", I("dma_start", out=out, in_=in_), reads=reads, writes=writes, dma=key)

        dma(identf[:], ident_d[:, :], [], ["identf"], "identf")
        sc.add("dve", I("tensor_copy", out=identb[:], in_=identf[:]), ["identf"], ["identb"])
        dma(gcols[:], gcols_d[:, :], [], ["gcols"], "gcols")
        dma(convc[:].rearrange("p a c -> p (a c)"), convc_d[:, :], [], ["convc"], "convc")
        w_up_v = w_up.rearrange("(c p) n -> p c n", p=128)
        w_down_v = w_down.rearrange("(c p) n -> p c n", p=128)
        k = 0
        cengs = ["dve", "act", "dve", "act"]
        for c in range(DC):
            for q in range(4):
                sl = k % NSTG
                k += 1
                dma(stg[sl][:], w_up_v[:, c, q * SW:(q + 1) * SW], [], [f"stg{sl}"], f"stg{sl}")
                if cengs[sl] == "act":
                    sc.add("act", I("activation", out=w_up_s[:, c, q * SW:(q + 1) * SW], in_=stg[sl][:],
                                                                          func=AF.Copy, scale=gcols[:, 16 + c:17 + c]),
                           [f"stg{sl}", "gcols"], [f"w_up_s{sl}"])
                else:
                    sc.add(cengs[sl], I("tensor_scalar", out=w_up_s[:, c, q * SW:(q + 1) * SW], in0=stg[sl][:],
                                                                                 scalar1=gcols[:, 16 + c:17 + c], scalar2=None, op0=ALU.mult),
                           [f"stg{sl}", "gcols"], [f"w_up_s{sl}"])
        for c in range(FC):
            sl = k % NSTG
            k += 1
            dma(stg[sl][:, 0:D], w_down_v[:, c, :], [], [f"stg{sl}"], f"stg{sl}")
            if cengs[sl] == "act":
                sc.add("act", I("activation", out=w_down_s[:, c, :], in_=stg[sl][:, 0:D], func=AF.Copy),
                       [f"stg{sl}"], [f"w_down_s{sl}"])
            else:
                sc.add(cengs[sl], I("tensor_copy", out=w_down_s[:, c, :], in_=stg[sl][:, 0:D]),
                       [f"stg{sl}"], [f"w_down_s{sl}"])
        WUP = [f"w_up_s{i}" for i in range(NSTG)]
        WDN = [f"w_down_s{i}" for i in range(NSTG)]

        ytiles = []
        tcount = 0
        for g in groups:
            n = g["name"]
            NB = g["NB"]
            x1d = dr["x1" + n]
            t0 = 0
            while t0 < NB:
                nbt = min(TBK, NB - t0)
                T = nbt * 128
                par = tcount % 2
                tcount += 1
                r0 = (1 + t0) * 128
                xb = [x1s[par * TBK + i] for i in range(nbt)]
                xbn = [f"x1s{par * TBK + i}" for i in range(nbt)]
                hl = halo[par]
                HL = f"halo{par}"
                for i in range(nbt):
                    dma(xb[i][:], x1d[r0 + i * 128:r0 + (i + 1) * 128, :], [], [xbn[i]], xbn[i])
                dma(hl[0:1, :], x1d[r0 - 1:r0, :], [], [HL + "a"], HL + "a")
                dma(hl[1:2, :], x1d[r0 + T:r0 + T + 1, :], [], [HL + "b"], HL + "b")
                for i in range(nbt):
                    xi = i % 2
                    sc.add("dve", I("tensor_scalar", out=xsB[xi][:], in0=xb[i][:, 0:D], scalar1=xb[i][:, D:D + 1],
                                                                        scalar2=None, op0=ALU.mult), [xbn[i]], [f"xsB{xi}"])
                    for c in range(DC):
                        sc.add("pe", I("transpose", out=ps[:, 0, :].bitcast(BF16)[:, c * 128:(c + 1) * 128],
                                                                       in_=xsB[xi][:, c * 128:(c + 1) * 128], identity=identb[:]),
                               [f"xsB{xi}", "identb"], ["pb0"])
                    sc.add("dve", I("tensor_copy", out=h2T[:, :, 1 + i * 128:1 + (i + 1) * 128],
                                                               in_=ps[:, 0, :].bitcast(BF16)[:, :].rearrange("p (c t) -> p c t", c=DC)),
                           ["pb0"], ["h2T"])
                sc.add("dve", I("tensor_scalar", out=xsh[:], in0=hl[:, 0:D], scalar1=hl[:, D:D + 1], scalar2=None, op0=ALU.mult),
                       [HL + "a", HL + "b"], ["xsh"])
                for c in range(DC):
                    sc.add("pe", I("transpose", out=ps[:, 1, :].bitcast(BF16)[:, c * 2:(c + 1) * 2],
                                                            in_=xsh[0:2, c * 128:(c + 1) * 128], identity=identb[0:2, 0:2]),
                           ["xsh", "identb"], ["pb1"])
                hv = ps[:, 1, :].bitcast(BF16)[:, 0:16].rearrange("p (c t) -> p c t", c=DC)
                sc.add("dve", I("tensor_copy", out=h2T[:, :, 0:1], in_=hv[:, :, 0:1]), ["pb1"], ["h2T"])
                sc.add("dve", I("tensor_copy", out=h2T[:, :, T + 1:T + 2], in_=hv[:, :, 1:2]), ["pb1"], ["h2T"])

                def up(c, T=T):
                    sl = c % 2
                    for (bank, col0, nm) in ((2 * sl, c * 128, f"pbG{sl}"), (2 * sl + 1, DFF + c * 128, f"pbU{sl}")):
                        for kk in range(DC):
                            sc.add("pe", I("matmul",
                                ps[:, bank, 0:T + 2], lhsT=w_up_s[:, kk, col0:col0 + 128], rhs=h2T[:, kk, 0:T + 2],
                                start=(kk == 0), stop=(kk == DC - 1)), ["h2T"] + WUP, [f"pb{bank}"])

                def conv(c, T=T):
                    sl = c % 2
                    bG, bU = 2 * sl, 2 * sl + 1
                    w0 = convc[:, 0, c:c + 1]
                    w1 = convc[:, 1, c:c + 1]
                    w2 = convc[:, 2, c:c + 1]
                    bb = convc[:, 3, c:c + 1]
                    w0u = convc[:, 0, FC + c:FC + c + 1]
                    w1u = convc[:, 1, FC + c:FC + c + 1]
                    w2u = convc[:, 2, FC + c:FC + c + 1]
                    bbu = convc[:, 3, FC + c:FC + c + 1]
                    sc.add("act", I("activation", out=t1g[sl][:, 0:T], in_=ps[:, bG, 1:T + 1], func=AF.Identity, scale=w1, bias=bb),
                           [f"pb{bG}", "convc"], [f"t1g{sl}"])
                    sc.add("act", I("activation", out=t1u[sl][:, 0:T], in_=ps[:, bU, 1:T + 1], func=AF.Identity, scale=w1u, bias=bbu),
                           [f"pb{bU}", "convc"], [f"t1u{sl}"])
                    sc.add("dve", I("scalar_tensor_tensor", out=t2g[sl][:, 0:T], in0=ps[:, bG, 0:T], scalar=w0, in1=t1g[sl][:, 0:T],
                                                                   op0=ALU.mult, op1=ALU.add), [f"pb{bG}", f"t1g{sl}", "convc"], [f"t1g{sl}"])
                    sc.add("dve", I("scalar_tensor_tensor", out=t3g[sl][:, 0:T], in0=ps[:, bG, 2:T + 2], scalar=w2, in1=t2g[sl][:, 0:T],
                                                                   op0=ALU.mult, op1=ALU.add), [f"pb{bG}", f"t1g{sl}", "convc"], [f"t1g{sl}"])
                    sc.add("dve", I("scalar_tensor_tensor", out=t2u[sl][:, 0:T], in0=ps[:, bU, 0:T], scalar=w0u, in1=t1u[sl][:, 0:T],
                                                                   op0=ALU.mult, op1=ALU.add), [f"pb{bU}", f"t1u{sl}", "convc"], [f"t1u{sl}"])
                    sc.add("dve", I("scalar_tensor_tensor", out=t3u[sl][:, 0:T], in0=ps[:, bU, 2:T + 2], scalar=w2u, in1=t2u[sl][:, 0:T],
                                                                   op0=ALU.mult, op1=ALU.add), [f"pb{bU}", f"t1u{sl}", "convc"], [f"t1u{sl}"])
                    sc.add("act", I("activation", out=gg[sl][:, 0:T], in_=t3g[sl][:, 0:T], func=AF.Gelu_apprx_tanh),
                           [f"t1g{sl}"], [f"gg{sl}"])
                    sc.add("dve", I("tensor_tensor", out=aT[c % 3][:, 0:T], in0=gg[sl][:, 0:T], in1=t3u[sl][:, 0:T], op=ALU.mult),
                           [f"gg{sl}", f"t1u{sl}"], [f"aT{c % 3}"])

                def down(c, nbt=nbt):
                    sl = c % 2
                    for tb in range(nbt):
                        for half in range(2):
                            bank = 4 + tb * 2 + half
                            sc.add("pe", I("matmul",
                                ps[:, bank, :], lhsT=aT[c % 3][:, tb * 128:(tb + 1) * 128], rhs=w_down_s[:, c, half * 512:(half + 1) * 512],
                                start=(c == 0), stop=(c == FC - 1)), [f"aT{c % 3}"] + WDN, [f"pb{bank}"])

                up(0)
                up(1)
                conv(0)
                for c in range(FC):
                    if c + 1 < FC:
                        conv(c + 1)
                    if c + 2 < FC:
                        up(c + 2)
                    down(c)
                for tb in range(nbt):
                    yo = yout[tb]
                    YO = f"yout{tb}"
                    for half in range(2):
                        bank = 4 + tb * 2 + half
                        sc.add("dve", I("tensor_tensor",
                            out=yo[:, half * 512:(half + 1) * 512], in0=ps[:, bank, :], in1=xb[tb][:, half * 512:(half + 1) * 512],
                            op=ALU.add), [f"pb{bank}", xbn[tb]], [YO])
                    ykey = f"y{n}_{t0 + tb}"
                    ytiles.append(ykey)
                    dma(dr["y" + n][(t0 + tb) * 128:(t0 + tb + 1) * 128, :], yo[:], [YO], [ykey], YO + "_out")
                t0 += nbt
        sc.add("sp", I("nop", ), ytiles, [])
        sc.add("act", I("nop", ), ytiles, [])
        sc.emit(nc, "B")
    return nc


def _rope_table(pos, S):
    pos = np.clip(pos, 0, S - 1)
    row = (pos // 64).astype(np.float32)
    col = (pos % 64).astype(np.float32)
    inv = (np.float32(10000.0) ** (-np.arange(0, 32, 2, dtype=np.float32) / np.float32(32))).astype(np.float32)
    ang = np.concatenate([row[:, None] * inv[None, :], col[:, None] * inv[None, :]], axis=1).astype(np.float32)
    return np.concatenate([np.cos(ang), np.sin(ang)], axis=1).astype(np.float32)


def _bias_table():
    slopes = np.exp2(-8.0 * np.arange(1, 9, dtype=np.float32) / 8.0).astype(np.float32)
    s = np.arange(128)[:, None]
    q = np.arange(128)[None, :]
    out = np.zeros((128, 8, 3, 128), np.float32)
    for jj in range(3):
        dist = np.abs(q - (s + (jj - 1) * 128)).astype(np.float32)
        valid = dist <= 128
        for h in range(8):
            out[:, h, jj, :] = np.where(valid, -slopes[h] * dist, NEG)
    return out.reshape(128, 8 * 384)


def prepare_core_inputs(core, groups_full, x_by_group, shared):
    m = dict(shared)
    for g in groups_full:
        n, S, NB = g["name"], g["S"], g["NB"]
        x = x_by_group[n]
        b, qtr = core // 4, core % 4
        start = qtr * NB * 128
        NE = NB + 4
        xe = np.zeros((NE * 128, D), np.float32)
        lo, hi = start - 256, start + NB * 128 + 256
        slo, shi = max(lo, 0), min(hi, S)
        xe[slo - lo:shi - lo] = x[b, slo:shi]
        m["xseq" + n] = np.ascontiguousarray(x[b])
        m["xext" + n] = xe
        m["rseq" + n] = g["rseq"]
        m["rext" + n] = _rope_table(np.arange(lo, hi), S)
        blk0 = lo // 128
        val = np.array([1.0 if 0 <= blk0 + e < S // 128 else 0.0 for e in range(NE)], np.float32)
        m["kval" + n] = np.ascontiguousarray(np.broadcast_to(val[None, :], (128, NE)))
        m["bval" + n] = m["kval" + n]
    return m


def make_shared(inp):
    def col(v, nch):
        return np.asarray(v, np.float32).reshape(nch, 128).T

    gcols = np.concatenate([col(inp["norm_mix_g"][0], 8),
                            col(np.concatenate([inp["out_norm_a_g"][0], inp["out_norm_b_g"][0]]), 8),
                            col(inp["norm_ffn_g"][0], 8)], axis=1)
    cw = np.asarray(inp["conv_w"][0], np.float32)
    cb = np.asarray(inp["conv_b"][0], np.float32)
    convc = np.stack([col(cw[0], 44), col(cw[1], 44), col(cw[2], 44), col(cb, 44)], axis=1).reshape(128, 4 * 44)
    hg = np.concatenate([inp["qnorm_a_g"][0], inp["knorm_a_g"][0], inp["qnorm_b_g"][0], inp["knorm_b_g"][0]]).astype(np.float32)
    shared = dict(
        w_in=np.ascontiguousarray(inp["w_in"][0], dtype=np.float32),
        w_out=np.ascontiguousarray(inp["w_out"][0], dtype=np.float32),
        w_up=np.ascontiguousarray(inp["w_up"][0], dtype=np.float32),
        w_down=np.ascontiguousarray(inp["w_down"][0], dtype=np.float32),
        gcols=np.ascontiguousarray(gcols, dtype=np.float32),
        convc=np.ascontiguousarray(convc, dtype=np.float32),
        headg=np.ascontiguousarray(np.broadcast_to(hg[None, :], (128, 256))),
        sinkr=np.ascontiguousarray(np.broadcast_to(np.asarray(inp["sink_b"][0], np.float32)[None, :], (128, 8))),
        biasT=_bias_table(),
        ident=np.eye(128, dtype=np.float32),
    )
    return shared


def run(inp, SP, SS, runner):
    inp = {k: np.asarray(v) for k, v in inp.items()}
    groups = [dict(name="P", S=SP, NB=SP // 512), dict(name="S", S=SS, NB=SS // 512)]
    for g in groups:
        g["rseq"] = _rope_table(np.arange(g["S"]), g["S"])
    nc = build(groups)
    shared = make_shared(inp)
    xg = {"P": np.asarray(inp["x_prompt"], np.float32), "S": np.asarray(inp["x_sample"], np.float32)}
    in_maps = [prepare_core_inputs(c, groups, xg, shared) for c in range(8)]
    results = runner(nc, in_maps)
    outs = []
    for g, key in zip(groups, ("x_prompt", "x_sample")):
        n, S, NB = g["name"], g["S"], g["NB"]
        y = np.zeros((2, S, D), np.float32)
        for c in range(8):
            b, qtr = c // 4, c % 4
            y[b, qtr * NB * 128:(qtr + 1) * NB * 128] = results[c]["y" + n]
        outs.append(y)
    return tuple(outs)


def kernel(**inputs):
    def runner(nc, in_maps):
        res = run_bass_kernel_spmd(nc, in_maps, core_ids=list(range(8)))
        return res.results
    return run(inputs, 16384, 8192, runner)
```

```python
import math
import os
from contextlib import ExitStack
import numpy as np
import concourse.bass as bass
import concourse.mybir as mybir
from concourse.bass_utils import run_bass_kernel_spmd

F32 = mybir.dt.float32
BF16 = mybir.dt.bfloat16
AF = mybir.ActivationFunctionType
ALU = mybir.AluOpType
AX = mybir.AxisListType

D = 1024
DC = 8
INW = 1536
DFF = 2816
FC = 22
EPS = 1e-6
DEBUG_TAGS = os.environ.get("K_TAGS", "0") == "1"
XW = 1028
NEG = -30000.0


def I(name, *args, **kw):
    return (name, args, kw)


class Sched:
    def __init__(self):
        self.ops = []
        self.lw = {}
        self.rd = {}

    @staticmethod
    def _key(op):
        return ("dma", op["dma"]) if op["dma"] is not None else op["eng"]

    def add(self, eng, fn, reads=(), writes=(), dma=None):
        i = len(self.ops)
        deps = {}

        def dep(j):
            k = self._key(self.ops[j])
            if k not in deps or deps[k] < j:
                deps[k] = j

        for b in reads:
            w = self.lw.get(b)
            if w is not None:
                dep(w)
        for b in writes:
            w = self.lw.get(b)
            if w is not None:
                dep(w)
            for j in self.rd.get(b, {}).values():
                dep(j)
        op = dict(eng=eng, fn=fn, deps=[], dma=dma, inc=False, done=None, tag=f"#{i} {fn[0]} r={list(reads)} w={list(writes)}")
        for k, j in deps.items():
            dop = self.ops[j]
            if dop["dma"] is None and dop["eng"] == "pe" and eng == "pe" and dma is None:
                continue
            dop["inc"] = True
            op["deps"].append(j)
        self.ops.append(op)
        me = self._key(op)
        for b in writes:
            self.lw[b] = i
            self.rd[b] = {}
        for b in reads:
            self.rd.setdefault(b, {})[me] = i
        return i

    def emit(self, nc, tag):
        with ExitStack() as es:
            sems = {}
            cnt = {}
            for op in self.ops:
                k = self._key(op)
                if op["dma"] is not None:
                    cnt[k] = cnt.get(k, 0) + 16
                    op["done"] = (k, cnt[k])
                elif op["inc"]:
                    cnt[k] = cnt.get(k, 0) + 1
                    op["done"] = (k, cnt[k])
            for n, k in enumerate(cnt.keys()):
                sems[k] = es.enter_context(nc.semaphore(f"{tag}_s{n}"))
            block = es.enter_context(nc.Block())
            ops = self.ops

            def run(engname):
                def body(eng):
                    waited = {}
                    for op in ops:
                        if op["eng"] != engname:
                            continue
                        for j in op["deps"]:
                            k, v = ops[j]["done"]
                            if waited.get(k, 0) >= v:
                                continue
                            eng.wait_ge(sems[k], v)
                            waited[k] = v
                        name, args, kw = op["fn"]
                        inst = getattr(eng, name)(*args, **kw)
                        if DEBUG_TAGS:
                            inst.annotate(op["tag"])
                        if op["done"] is not None:
                            k, v = op["done"]
                            inst.then_inc(sems[k], 16 if op["dma"] is not None else 1)
                return body

            block.sync(run("sp"))
            block.tensor(run("pe"))
            block.vector(run("dve"))
            block.scalar(run("act"))
            block.gpsimd(run("pool"))


def build(groups):
    nc = bass.Bass("TRN2", target_bir_lowering=False)

    def din(name, shape):
        return nc.dram_tensor(name, list(shape), F32, kind="ExternalInput").ap()

    def dout(name, shape):
        return nc.dram_tensor(name, list(shape), F32, kind="ExternalOutput").ap()

    dr = {}
    for g in groups:
        n = g["name"]
        S, NB = g["S"], g["NB"]
        NE = NB + 4
        dr["xseq" + n] = din("xseq" + n, [S, D])
        dr["xext" + n] = din("xext" + n, [NE * 128, D])
        dr["rseq" + n] = din("rseq" + n, [S, 64])
        dr["rext" + n] = din("rext" + n, [NE * 128, 64])
        dr["kval" + n] = din("kval" + n, [128, NE])
        dr["bval" + n] = din("bval" + n, [128, NE])
        dr["y" + n] = dout("y" + n, [NB * 128, D])
        dr["x1" + n] = nc.dram_tensor("x1" + n, [(NB + 2) * 128, XW], F32).ap()
    w_in = din("w_in", [D, INW])
    w_out = din("w_out", [D, D])
    w_up = din("w_up", [D, 2 * DFF])
    w_down = din("w_down", [DFF, D])
    gcols_d = din("gcols", [128, 24])
    convc_d = din("convc", [128, 4 * 44])
    headg_d = din("headg", [128, 4 * 64])
    sink_d = din("sinkr", [128, 8])
    bias_d = din("biasT", [128, 8 * 384])
    ident_d = din("ident", [128, 128])

    SMAX = max(g["S"] for g in groups)
    NEMAX = max(g["NB"] for g in groups) + 4

    with ExitStack() as es:
        def sb(name, shape, dt=F32):
            return es.enter_context(nc.sbuf_tensor("sA_" + name, list(shape), dt))

        ps = es.enter_context(nc.psum_tensor("psA", [128, 8, 512], F32))
        identf = sb("identf", [128, 128])
        identb = sb("identb", [128, 128], BF16)
        gcols = sb("gcols", [128, 24])
        headg = sb("headg", [128, 4, 64])
        sinkt = sb("sinkt", [128, 8])
        esink = sb("esink", [128, 8])
        biasb = sb("biasb", [128, 8, 384], BF16)
        kval = sb("kval", [128, NEMAX])
        bval = sb("bval", [128, NEMAX])
        w_in_s = sb("w_in_s", [128, DC, INW], BF16)
        w_out_s = sb("w_out_s", [128, DC, D], BF16)
        KT = sb("KT", [128, SMAX], BF16)
        Vaug = sb("Vaug", [128, SMAX // 128, 2, 65], BF16)
        KbT = sb("KbT", [128, NEMAX * 128], BF16)
        Vb = sb("Vb", [128, NEMAX, 2, 65], BF16)
        xt = [sb(f"xt{i}", [128, D]) for i in range(2)]
        xs0 = sb("xs0", [128, D], BF16)
        hT = [sb(f"hT{i}", [128, DC, 128], BF16) for i in range(2)]
        sqj = sb("sqj", [128, D])
        ropet = sb("ropet", [128, 64])
        ssx = sb("ssx", [128, 1])
        rsx = sb("rsx", [128, 1])
        qraw = sb("qraw", [128, 1024])
        ssq = sb("ssq", [128, 16])
        rsq = sb("rsq", [128, 16])
        arena2 = sb("arena2", [128, 5120])
        QaT = [arena2[:, i * 1024:(i + 1) * 1024].bitcast(BF16).rearrange("p (j t) -> p j t", j=4) for i in range(2)]
        QbT = arena2[:, 2048:3072].bitcast(BF16).rearrange("p (j t) -> p j t", j=4)
        pT = [arena2[:, 3072 + i * 512:3584 + i * 512].bitcast(BF16).rearrange("p (j t) -> p j t", j=2) for i in range(2)]
        qra = arena2[:, 4096:4352].bitcast(BF16)
        qrb = arena2[:, 4352:4608].bitcast(BF16)
        yT = arena2[:, 4608:5120].bitcast(BF16).rearrange("p (c t) -> p c t", c=DC)
        oT = sb("oT", [128, 512])
        pT2 = sb("pT2", [128, 2, 512], BF16)
        den = sb("den", [128, 4])
        rden = sb("rden", [128, 4])
        ssy = sb("ssy", [128, 8])
        rsy = sb("rsy", [128, 8])
        x1t = sb("x1t", [128, XW])
        tA = x1t[:, 0:256]
        tB = x1t[:, 256:512]
        ss1 = sb("ss1", [128, 1])
        rs1 = sb("rs1", [128, 1])
        fsc = sb("fsc", [128, 1])
        arena = sb("arena", [128, 6144])
        ya = arena[:, 0:2048].rearrange("p (b f) -> p b f", b=4)
        yb = arena[:, 2048:4096].rearrange("p (b f) -> p b f", b=4)
        ybf = arena[:, 4096:6144].bitcast(BF16).rearrange("p (b f) -> p b f", b=4)
        KW = 5
        _ar = [[arena, 0, 6144], [arena2, 0, 5120]]

        def carve(words):
            for a in _ar:
                if a[1] + words <= a[2]:
                    v = a[0][:, a[1]:a[1] + words]
                    a[1] += words
                    return v
            raise AssertionError("KV scratch arena exhausted")

        xtK, hTK, xsK, krawK, sqkK, knK, kn2K, krK, ropeK = [], [], [], [], [], [], [], [], []
        for i in range(KW):
            xtK.append(xt[i][:] if i < 2 else carve(1024))
            hTK.append(hT[i][:] if i < 2 else carve(512).bitcast(BF16).rearrange("p (c t) -> p c t", c=DC))
            xsK.append(carve(512).bitcast(BF16))
            krawK.append(carve(256))
            sqkK.append(carve(128))
            knK.append(carve(128))
            kn2K.append(carve(128))
            krK.append(carve(64).bitcast(BF16))
            ropeK.append(carve(64))
        ssxK = [sb(f"ssxK{i}", [128, 1]) for i in range(KW)]
        rsxK = [sb(f"rsxK{i}", [128, 1]) for i in range(KW)]
        sskK = [sb(f"sskK{i}", [128, 2]) for i in range(KW)]
        rskK = [sb(f"rskK{i}", [128, 2]) for i in range(KW)]
        ATT_NAMES = ["ya", "yb", "ybf", "QaT0", "QaT1", "QbT", "pT0", "pT1", "qra", "qrb", "yT"]
        KV_NAMES = ([f"xtK{i}" for i in range(KW)] + [f"hTK{i}" for i in range(KW)] + [f"xsK{i}" for i in range(KW)]
                    + [f"krawK{i}" for i in range(KW)] + [f"sqkK{i}" for i in range(KW)] + [f"knK{i}" for i in range(KW)]
                    + [f"kn2K{i}" for i in range(KW)] + [f"krK{i}" for i in range(KW)] + [f"ropeK{i}" for i in range(KW)])

        pT.append(pT2[:])
        sc = Sched()
        PB = [f"pb{i}" for i in range(8)]
        st_slot = [ps[:, 0:2, :], ps[:, 2:4, :]]
        ST = [["pb0", "pb1"], ["pb2", "pb3"]]
        oacc = [ps[:, 4, :], ps[:, 5, :]]
        OA = ["pb4", "pb5"]
        bank6 = ps[:, 6, :]
        bank6b = ps[:, 6, :].bitcast(BF16)
        bank7 = ps[:, 7, :]
        bank7b = ps[:, 7, :].bitcast(BF16)
        cnt = dict(xt=0, st=0, oa=0, pt=0)

        def dma(out, in_, reads, writes, key):
            sc.add("sp", I("dma_start", out=out, in_=in_), reads=reads, writes=writes, dma=key)

        def fence():
            sc.add("dve", I("memset", fsc[:], 0.0), [], ATT_NAMES + KV_NAMES + ["fsc"])

        dma(identf[:], ident_d[:, :], [], ["identf"], "identf")
        sc.add("dve", I("tensor_copy", out=identb[:], in_=identf[:]), ["identf"], ["identb"])
        dma(gcols[:], gcols_d[:, :], [], ["gcols"], "gcols")
        dma(headg[:].rearrange("p a d -> p (a d)"), headg_d[:, :], [], ["headg"], "headg")
        dma(sinkt[:], sink_d[:, :], [], ["sinkt"], "sinkt")
        sc.add("act", I("activation", out=esink[:], in_=sinkt[:], func=AF.Exp), ["sinkt"], ["esink"])
        k = 0
        for i in range(3):
            sl = k % 2
            k += 1
            dma(xt[sl][:], bias_d[:, i * 1024:(i + 1) * 1024], [], [f"xtK{sl}"], f"xtK{sl}")
            sc.add("dve", I("tensor_copy", out=biasb[:].rearrange("p h k -> p (h k)")[:, i * 1024:(i + 1) * 1024], in_=xt[sl][:]),
                   [f"xtK{sl}"], ["biasb"])
        w_in_v = w_in.rearrange("(c p) n -> p c n", p=128)
        w_out_v = w_out.rearrange("(c p) n -> p c n", p=128)
        for c in range(DC):
            for h in range(2):
                sl = k % 2
                k += 1
                dma(xt[sl][:, 0:768], w_in_v[:, c, h * 768:(h + 1) * 768], [], [f"xtK{sl}"], f"xtK{sl}")
                sc.add("dve", I("tensor_scalar",
                                out=w_in_s[:, c, h * 768:h * 768 + 512].rearrange("p (j two d) -> p two j d", j=4, two=2),
                                in0=xt[sl][:, 0:512].rearrange("p (two j d) -> p two j d", two=2, j=4), scalar1=gcols[:, c:c + 1],
                                scalar2=None, op0=ALU.mult), [f"xtK{sl}", "gcols"], ["w_in_s"])
                sc.add("act", I("activation", out=w_in_s[:, c, h * 768 + 512:(h + 1) * 768], in_=xt[sl][:, 512:768], func=AF.Copy,
                                scale=gcols[:, c:c + 1]), [f"xtK{sl}", "gcols"], ["w_in_s2"])
        cnt["xt"] = k

        def load_w_out():
            for c in range(DC):
                sl = cnt["xt"] % 2
                cnt["xt"] += 1
                dma(xt[sl][:], w_out_v[:, c, :], [], [f"xtK{sl}"], f"xtK{sl}")
                sc.add("dve", I("tensor_scalar", out=w_out_s[:, c, :], in0=xt[sl][:], scalar1=gcols[:, 8 + c:9 + c],
                                scalar2=None, op0=ALU.mult), [f"xtK{sl}", "gcols"], ["w_out_s"])
        WIN = ["w_in_s", "w_in_s2"]

        def rstd_ops(ss_ap, rs_ap, n, ssname, rsname, bias2=0.0):
            sc.add("act", I("activation", out=rs_ap, in_=ss_ap, func=AF.Ln, scale=1.0 / n, bias=EPS), [ssname], [rsname])
            sc.add("act", I("activation", out=rs_ap, in_=rs_ap, func=AF.Exp, scale=-0.5, bias=bias2), [rsname], [rsname])

        def rope_ops(xin, H, rope_ap, rname, out_ap_fn, inname, outname):
            xv = xin.rearrange("p (h a f e) -> p h a f e", h=H, a=2, f=2)
            x1 = xv[:, :, :, 0, :]
            x2 = xv[:, :, :, 1, :]
            C = rope_ap[:, 0:32].rearrange("p (a e) -> p a e", a=2).unsqueeze(1).to_broadcast([128, H, 2, 16])
            Sn = rope_ap[:, 32:64].rearrange("p (a e) -> p a e", a=2).unsqueeze(1).to_broadcast([128, H, 2, 16])
            tAv = tA[:, 0:H * 32].rearrange("p (h a e) -> p h a e", h=H, a=2)
            tBv = tB[:, 0:H * 32].rearrange("p (h a e) -> p h a e", h=H, a=2)
            sc.add("dve", I("tensor_tensor", out=tAv, in0=x1, in1=C, op=ALU.mult), [inname, rname], ["x1t"])
            sc.add("dve", I("tensor_tensor", out=tBv, in0=x2, in1=Sn, op=ALU.mult), [inname, rname], ["x1t"])
            sc.add("dve", I("tensor_tensor", out=out_ap_fn(0), in0=tAv, in1=tBv, op=ALU.subtract), ["x1t"], [outname])
            sc.add("dve", I("tensor_tensor", out=tAv, in0=x2, in1=C, op=ALU.mult), [inname, rname], ["x1t"])
            sc.add("dve", I("tensor_tensor", out=tBv, in0=x1, in1=Sn, op=ALU.mult), [inname, rname], ["x1t"])
            sc.add("dve", I("tensor_tensor", out=out_ap_fn(1), in0=tAv, in1=tBv, op=ALU.add), ["x1t"], [outname])

        def kv_gen(src_rows, rope_rows, kcol, gidx, KTt, Vt, idx, ktname, vname, s):
            X, XS, HT, KR, SQ, KN, KN2, KRR, RP = (f"xtK{s}", f"xsK{s}", f"hTK{s}", f"krawK{s}", f"sqkK{s}", f"knK{s}",
                                                   f"kn2K{s}", f"krK{s}", f"ropeK{s}")
            bT, bP = s, s
            bTb = ps[:, bT, :].bitcast(BF16)
            dma(xtK[s], src_rows, [], [X], X)
            if rope_rows is not None:
                dma(ropeK[s], rope_rows, [], [RP], RP)
            yield
            sc.add("act", I("activation", out=qraw[:], in_=xtK[s], func=AF.Square, accum_out=ssxK[s][:]), [X], ["qraw", f"ssxK{s}"])
            yield
            rstd_ops(ssxK[s][:], rsxK[s][:], D, f"ssxK{s}", f"rsxK{s}")
            yield
            sc.add("dve", I("tensor_scalar", out=xsK[s], in0=xtK[s], scalar1=rsxK[s][:, 0:1], scalar2=None, op0=ALU.mult),
                   [X, f"rsxK{s}"], [XS])
            yield
            for c in range(DC):
                sc.add("pe", I("transpose", out=bTb[:, c * 128:(c + 1) * 128], in_=xsK[s][:, c * 128:(c + 1) * 128], identity=identb[:]),
                       [XS, "identb"], [PB[bT]])
            yield
            sc.add("act", I("activation", out=hTK[s].rearrange("p c t -> p (c t)"), in_=bTb[:, :], func=AF.Copy), [PB[bT]], [HT])
            yield
            for c in range(DC):
                sc.add("pe", I("matmul", ps[:, bP, 0:256], lhsT=hTK[s][:, c, :], rhs=w_in_s[:, c, kcol:kcol + 256],
                               start=(c == 0), stop=(c == DC - 1)), [HT] + WIN, [PB[bP]])
            yield
            sc.add("act", I("activation", out=krawK[s], in_=ps[:, bP, 0:256], func=AF.Copy), [PB[bP]], [KR])
            yield
            sc.add("pool", I("tensor_copy", out=Vt[:, idx, :, 0:64], in_=krawK[s][:, 128:256].rearrange("p (h d) -> p h d", h=2)),
                   [KR], [vname])
            sc.add("pool", I("tensor_tensor", out=sqkK[s], in0=krawK[s][:, 0:128], in1=krawK[s][:, 0:128], op=ALU.mult), [KR], [SQ])
            yield
            sc.add("dve", I("tensor_reduce", out=sskK[s][:], in_=sqkK[s].rearrange("p (h d) -> p h d", h=2), axis=AX.X, op=ALU.add),
                   [SQ], [f"sskK{s}"])
            yield
            rstd_ops(sskK[s][:], rskK[s][:], 64, f"sskK{s}", f"rskK{s}")
            yield
            sc.add("dve", I("tensor_tensor", out=knK[s].rearrange("p (h d) -> p h d", h=2),
                            in0=krawK[s][:, 0:128].rearrange("p (h d) -> p h d", h=2),
                            in1=rskK[s][:].unsqueeze(2).to_broadcast([128, 2, 64]), op=ALU.mult), [KR, f"rskK{s}"], [KN])
            gv = headg[:, gidx, :].unsqueeze(1).to_broadcast([128, 2, 64])
            if rope_rows is not None:
                sc.add("dve", I("tensor_tensor", out=kn2K[s].rearrange("p (h d) -> p h d", h=2),
                                in0=knK[s].rearrange("p (h d) -> p h d", h=2), in1=gv, op=ALU.mult), [KN, "headg"], [KN2])
                krv = krK[s].rearrange("p (h a f e) -> p h a f e", h=2, a=2, f=2)
                rope_ops(kn2K[s], 2, ropeK[s], RP, lambda half: krv[:, :, :, half, :], KN2, KRR)
            else:
                sc.add("dve", I("tensor_tensor", out=krK[s].rearrange("p (h d) -> p h d", h=2),
                                in0=knK[s].rearrange("p (h d) -> p h d", h=2), in1=gv, op=ALU.mult), [KN, "headg"], [KRR])
            yield
            sc.add("pe", I("transpose", out=bTb[:, 0:128], in_=krK[s], identity=identb[:]), [KRR, "identb"], [PB[bT]])
            yield
            sc.add("act", I("activation", out=KTt[:, idx * 128:(idx + 1) * 128], in_=bTb[:, 0:128], func=AF.Copy), [PB[bT]], [ktname])
            yield

        def interleave(gens, width, stagger=3):
            active = []
            it = iter(gens)
            since = stagger
            done = False
            while True:
                if not done and len(active) < width and since >= stagger:
                    gnew = next(it, None)
                    if gnew is None:
                        done = True
                    else:
                        active.append(gnew)
                        since = 0
                if not active:
                    if done:
                        break
                    since = stagger
                    continue
                since += 1
                for gg_ in list(active):
                    try:
                        next(gg_)
                    except StopIteration:
                        active.remove(gg_)

        def evac_copy(ob, nb):
            sc.add("dve", I("tensor_copy", out=oT[0:65, 0:nb * 128], in_=oacc[ob][0:65, 0:nb * 128]), [OA[ob]], ["oT"])

        def evac_finish(nb, h, ydst, yname, sink):
            b6v = bank6[:, 0:4 * 65].rearrange("p (b d) -> p b d", d=65)
            for bi in range(nb):
                sc.add("pe", I("transpose", out=b6v[:, bi, :], in_=oT[0:65, bi * 128:(bi + 1) * 128],
                               identity=identf[0:65, 0:65]), ["oT", "identf"], ["pb6"])
            if sink:
                sc.add("dve", I("tensor_scalar", out=den[:, 0:nb], in0=b6v[:, 0:nb, 64], scalar1=esink[:, h:h + 1],
                                scalar2=None, op0=ALU.add), ["pb6", "esink"], ["den"])
                sc.add("dve", I("reciprocal", out=rden[:, 0:nb], in_=den[:, 0:nb]), ["den"], ["rden"])
            else:
                sc.add("dve", I("reciprocal", out=rden[:, 0:nb], in_=b6v[:, 0:nb, 64]), ["pb6"], ["rden"])
            sc.add("dve", I("tensor_tensor", out=ydst[:, 0:nb, h * 64:(h + 1) * 64], in0=b6v[:, 0:nb, 0:64],
                            in1=rden[:, 0:nb].unsqueeze(2).to_broadcast([128, nb, 64]), op=ALU.mult), ["pb6", "rden"], [yname])

        def evac_O(ob, nb, h, ydst, yname, sink):
            evac_copy(ob, nb)
            evac_finish(nb, h, ydst, yname, sink)

        def proj_gen(g, tile, qslot):
            n = g["name"]
            for bi, eb in enumerate(tile):
                sl = cnt["xt"] % 2
                cnt["xt"] += 1
                X = f"xtK{sl}"
                dma(ropet[:], dr["rext" + n][eb * 128:(eb + 1) * 128, :], [], ["ropet"], "ropet")
                dma(xt[sl][:], dr["xext" + n][eb * 128:(eb + 1) * 128, :], [], [X], X)
                yield
                sc.add("pool", I("tensor_tensor", out=sqj[:], in0=xt[sl][:], in1=xt[sl][:], op=ALU.mult), [X], ["sqj"])
                yield
                sc.add("dve", I("tensor_reduce", out=ssx[:], in_=sqj[:], axis=AX.X, op=ALU.add), ["sqj"], ["ssx"])
                yield
                rstd_ops(ssx[:], rsx[:], D, "ssx", "rsx")
                yield
                sc.add("dve", I("tensor_scalar", out=xs0[:], in0=xt[sl][:], scalar1=rsx[:, 0:1], scalar2=None, op0=ALU.mult),
                       [X, "rsx"], ["xs0"])
                yield
                for c in range(DC):
                    sc.add("pe", I("transpose", out=bank7b[:, c * 128:(c + 1) * 128], in_=xs0[:, c * 128:(c + 1) * 128], identity=identb[:]),
                           ["xs0", "identb"], ["pb7"])
                yield
                sc.add("dve", I("tensor_copy", out=hT[0][:].rearrange("p c t -> p (c t)"), in_=bank7b[:, :]), ["pb7"], ["hTK0"])
                yield
                for half, col0 in ((0, 0), (1, 768)):
                    for c in range(DC):
                        sc.add("pe", I("matmul", bank7[:, :], lhsT=hT[0][:, c, :], rhs=w_in_s[:, c, col0:col0 + 512],
                                       start=(c == 0), stop=(c == DC - 1)), ["hTK0"] + WIN, ["pb7"])
                    yield
                    sc.add("dve", I("tensor_copy", out=qraw[:, half * 512:(half + 1) * 512], in_=bank7[:, :]), ["pb7"], ["qraw"])
                    yield
                sc.add("pool", I("tensor_tensor", out=sqj[:], in0=qraw[:], in1=qraw[:], op=ALU.mult), ["qraw"], ["sqj"])
                yield
                sc.add("dve", I("tensor_reduce", out=ssq[:], in_=sqj[:].rearrange("p (h d) -> p h d", d=64), axis=AX.X, op=ALU.add),
                       ["sqj"], ["ssq"])
                yield
                rstd_ops(ssq[:], rsq[:], 64, "ssq", "rsq", bias2=-math.log(8.0))
                yield
                sc.add("dve", I("tensor_tensor", out=qraw[:].rearrange("p (h d) -> p h d", d=64),
                                in0=qraw[:].rearrange("p (h d) -> p h d", d=64),
                                in1=rsq[:].unsqueeze(2).to_broadcast([128, 16, 64]), op=ALU.mult), ["qraw", "rsq"], ["qraw"])
                sc.add("dve", I("tensor_tensor", out=qrb[:].rearrange("p (h d) -> p h d", d=64),
                                in0=qraw[:, 512:1024].rearrange("p (h d) -> p h d", d=64),
                                in1=headg[:, 2, :].unsqueeze(1).to_broadcast([128, 8, 64]), op=ALU.mult), ["qraw", "headg"], ["qrb"])
                yield
                for j in range(4):
                    sc.add("pe", I("transpose", out=bank7b[:, j * 128:(j + 1) * 128], in_=qrb[:, j * 128:(j + 1) * 128], identity=identb[:]),
                           ["qrb", "identb"], ["pb7"])
                sc.add("dve", I("tensor_tensor", out=qraw[:, 0:512].rearrange("p (h d) -> p h d", d=64),
                                in0=qraw[:, 0:512].rearrange("p (h d) -> p h d", d=64),
                                in1=headg[:, 0, :].unsqueeze(1).to_broadcast([128, 8, 64]), op=ALU.mult), ["qraw", "headg"], ["qraw"])
                qrav = qra[:].rearrange("p (h a f e) -> p h a f e", h=8, a=2, f=2)
                rope_ops(qraw[:, 0:512], 8, ropet[:], "ropet", lambda half: qrav[:, :, :, half, :], "qraw", "qra")
                yield
                sc.add("dve", I("tensor_copy", out=QbT[:, :, bi * 128:(bi + 1) * 128],
                                in_=bank7b[:, 0:512].rearrange("p (j t) -> p j t", j=4)), ["pb7"], ["QbT"])
                yield
                for j in range(4):
                    sc.add("pe", I("transpose", out=bank7b[:, j * 128:(j + 1) * 128], in_=qra[:, j * 128:(j + 1) * 128], identity=identb[:]),
                           ["qra", "identb"], ["pb7"])
                yield
                sc.add("dve", I("tensor_copy", out=QaT[qslot][:, :, bi * 128:(bi + 1) * 128],
                                in_=bank7b[:, 0:512].rearrange("p (j t) -> p j t", j=4)), ["pb7"], [f"QaT{qslot}"])
                yield

        def tail0(nb):
            for grp, ysrc, yn in ((0, ya, "ya"), (1, yb, "yb")):
                for bi in range(nb):
                    sc.add("act", I("activation", out=oT[:, 0:512], in_=ysrc[:, bi, :], func=AF.Square,
                                    accum_out=ssy[:, grp * 4 + bi:grp * 4 + bi + 1]), [yn], ["oT", "ssy"])
            rstd_ops(ssy[:], rsy[:], 512, "ssy", "rsy")
            for grp, ysrc, yn in ((0, ya, "ya"), (1, yb, "yb")):
                for bi in range(nb):
                    sc.add("dve", I("tensor_scalar", out=ybf[:, bi, grp * 512:(grp + 1) * 512], in0=ysrc[:, bi, :],
                                    scalar1=rsy[:, grp * 4 + bi:grp * 4 + bi + 1], scalar2=None, op0=ALU.mult), [yn, "rsy"], ["ybf"])

        def tail_gen(g, tile):
            n = g["name"]
            for bi, eb in enumerate(tile):
                dma(x1t[:, 0:D], dr["xext" + n][eb * 128:(eb + 1) * 128, :], [], ["x1t"], "x1t_in")
                for c in range(DC):
                    sc.add("pe", I("transpose", out=bank7b[:, c * 128:(c + 1) * 128], in_=ybf[:, bi, c * 128:(c + 1) * 128], identity=identb[:]),
                           ["ybf", "identb"], ["pb7"])
                yield
                sc.add("dve", I("tensor_copy", out=yT[:].rearrange("p c t -> p (c t)"), in_=bank7b[:, :]), ["pb7"], ["yT"])
                yield
                for half in range(2):
                    for c in range(DC):
                        sc.add("pe", I("matmul", bank7[:, :], lhsT=yT[:, c, :], rhs=w_out_s[:, c, half * 512:(half + 1) * 512],
                                       start=(c == 0), stop=(c == DC - 1)), ["yT", "w_out_s"], ["pb7"])
                    yield
                    sc.add("dve", I("tensor_tensor", out=x1t[:, half * 512:(half + 1) * 512], in0=bank7[:, :],
                                    in1=x1t[:, half * 512:(half + 1) * 512], op=ALU.add), ["pb7", "x1t"], ["x1t"])
                    yield
                sc.add("pool", I("tensor_tensor", out=sqj[:], in0=x1t[:, 0:D], in1=x1t[:, 0:D], op=ALU.mult), ["x1t"], ["sqj"])
                yield
                sc.add("dve", I("tensor_reduce", out=ss1[:], in_=sqj[:], axis=AX.X, op=ALU.add), ["sqj"], ["ss1"])
                yield
                rstd_ops(ss1[:], rs1[:], D, "ss1", "rs1")
                yield
                sc.add("dve", I("tensor_tensor", out=x1t[:, D:D + 1], in0=rs1[:], in1=bval[:, eb:eb + 1], op=ALU.mult), ["rs1", "bval"], ["x1t"])
                sc.add("dve", I("memset", x1t[:, D + 1:XW], 0.0), [], ["x1t"])
                yield
                dkey = f"x1{n}_{eb}"
                dma(dr["x1" + n][(eb - 1) * 128:eb * 128, :], x1t[:], ["x1t"], [dkey], "x1t_out")
                yield

        def window_attn(g, tile):
            nb = len(tile)
            halves = [list(range(0, min(2, nb)))] + ([list(range(2, nb))] if nb > 2 else [])
            steps = [(h, hv) for h in range(8) for hv in halves]
            pend = None
            wdef = []

            def pv(h, hv, slot, ob):
                two = h // 4
                for bi_l, bi in enumerate(hv):
                    eb = tile[bi]
                    for jj in range(3):
                        ke = eb - 1 + jj
                        sc.add("pe", I("matmul", oacc[ob][0:65, bi * 128:(bi + 1) * 128], lhsT=Vb[:, ke, two, :],
                                       rhs=pT[slot][:, bi_l, jj * 128:(jj + 1) * 128], start=(jj == 0), stop=(jj == 2)),
                               ["Vb", f"pT{slot}"], [OA[ob]])

            for si, (h, hv) in enumerate(steps):
                j, two = h % 4, h // 4
                r0 = two * 64
                slot = cnt["st"] % 2
                cnt["st"] += 1
                if hv is halves[0]:
                    cnt["oa"] += 1
                ob = cnt["oa"] % 2
                for bi_l, bi in enumerate(hv):
                    eb = tile[bi]
                    for jj in range(3):
                        ke = eb - 1 + jj
                        sc.add("pe", I("matmul", st_slot[slot][:, bi_l, jj * 128:(jj + 1) * 128], lhsT=KbT[r0:r0 + 64, ke * 128:(ke + 1) * 128],
                                       rhs=QbT[r0:r0 + 64, j, bi * 128:(bi + 1) * 128], start=True, stop=True),
                               ["KbT", "QbT"], ST[slot])
                nl = len(hv)
                sc.add("dve", I("tensor_tensor", out=st_slot[slot][:, 0:nl, 0:384], in0=st_slot[slot][:, 0:nl, 0:384],
                                in1=biasb[:, h, :].unsqueeze(1).to_broadcast([128, nl, 384]), op=ALU.add), ST[slot] + ["biasb"], ST[slot])
                sc.add("act", I("activation", out=pT[slot][:, 0:nl, 0:384], in_=st_slot[slot][:, 0:nl, 0:384], func=AF.Exp),
                       ST[slot], [f"pT{slot}"])
                if wdef:
                    wdef.pop(0)()
                if pend is not None:
                    pv(*pend[:4])
                    if pend[4]:
                        evac_copy(pend[3], nb)
                        wdef.append(lambda hh=pend[0]: evac_finish(nb, hh, yb, "yb", True))
                pend = (h, hv, slot, ob, hv is halves[-1])
            while wdef:
                wdef.pop(0)()
            pv(*pend[:4])
            evac_O(pend[3], nb, pend[0], yb, "yb", True)

        def global_attn(g, tile, qslot, bg, nyield):
            nb = len(tile)
            Tq = nb * 128
            nkb = g["S"] // 128
            bgstep = max(1, (4 * nkb) // (nyield + 6))
            it = 0
            deferred = []
            for j in range(4):
                obs = (0, 1)
                pend = []

                def pv(pkb, pslot):
                    for two in range(2):
                        sc.add("pe", I("matmul", oacc[obs[two]][0:65, 0:Tq], lhsT=Vaug[:, pkb, two, :], rhs=pT[pslot][:, two, 0:Tq],
                                       start=(pkb == 0), stop=(pkb == nkb - 1)), ["Vaug", f"pT{pslot}"], [OA[obs[two]]])

                for kb in range(nkb):
                    slot = cnt["st"] % 2
                    cnt["st"] += 1
                    pslot = cnt["pt"] % 3
                    cnt["pt"] += 1
                    for two in range(2):
                        r0 = two * 64
                        sc.add("pe", I("matmul", st_slot[slot][:, two, 0:Tq], lhsT=KT[r0:r0 + 64, kb * 128:(kb + 1) * 128],
                                       rhs=QaT[qslot][r0:r0 + 64, j, 0:Tq], start=True, stop=True), ["KT", f"QaT{qslot}"], ST[slot])
                    sc.add("act", I("activation", out=pT[pslot][:, :, 0:Tq], in_=st_slot[slot][:, :, 0:Tq], func=AF.Exp),
                           ST[slot], [f"pT{pslot}"])
                    pend.append((kb, pslot))
                    if kb in (1, 3) and deferred:
                        deferred.pop(0)()
                    if len(pend) > 2:
                        pv(*pend.pop(0))
                    it += 1
                    if it % bgstep == 0 and os.environ.get("K_BG", "1") == "1":
                        next(bg, None)
                while pend:
                    pv(*pend.pop(0))
                for d_ in deferred:
                    d_()
                deferred.clear()
                evac_copy(obs[0], nb)
                deferred.append(lambda j=j: (evac_finish(nb, j, ya, "ya", False), evac_copy(obs[1], nb)))
                deferred.append(lambda j=j: evac_finish(nb, 4 + j, ya, "ya", False))
            for d_ in deferred:
                d_()
            deferred.clear()
            for _ in bg:
                pass

        def chain(*gens):
            for gen in gens:
                if gen is None:
                    continue
                for _ in gen:
                    yield

        sc.add("pool", I("memset", ssy[:], 1.0), [], ["ssy"])
        sc.add("pool", I("memset", Vaug[:].rearrange("p b k d -> p (b k d)"), 1.0), [], ["Vaug"])
        for g in groups:
            n = g["name"]
            S, NB = g["S"], g["NB"]
            NE = NB + 4
            fence()
            dma(kval[:, 0:NE], dr["kval" + n][:, :], [], ["kval"], "kval")
            dma(bval[:, 0:NE], dr["bval" + n][:, :], [], ["bval"], "bval")
            for kvh in range(2):
                sc.add("pool", I("tensor_copy", out=Vb[:, 0:NE, kvh, 64], in_=kval[:, 0:NE]), ["kval"], ["Vb"])
            gens = []
            ctr = 0
            for b in range(S // 128):
                gens.append(kv_gen(dr["xseq" + n][b * 128:(b + 1) * 128, :], dr["rseq" + n][b * 128:(b + 1) * 128, :],
                                   512, 1, KT, Vaug, b, "KT", "Vaug", ctr % KW))
                ctr += 1
            for eb in range(NE):
                gens.append(kv_gen(dr["xext" + n][eb * 128:(eb + 1) * 128, :], None, 1280, 3, KbT, Vb, eb, "KbT", "Vb", ctr % KW))
                ctr += 1
            interleave(gens, KW)
            fence()
            blocks = list(range(1, NB + 3))
            tiles = [blocks[i:i + 4] for i in range(0, len(blocks), 4)]
            for _ in proj_gen(g, tiles[0], 0):
                pass
            prev_tail = None
            for ti, tile in enumerate(tiles):
                qslot = ti % 2
                window_attn(g, tile)
                if g is groups[0] and ti == 0:
                    load_w_out()
                nxt = proj_gen(g, tiles[ti + 1], (ti + 1) % 2) if ti + 1 < len(tiles) else None
                ny = (11 * 4 if prev_tail is not None else 0) + (20 * 4 if nxt is not None else 0)
                global_attn(g, tile, qslot, chain(prev_tail, nxt), ny)
                tail0(len(tile))
                prev_tail = tail_gen(g, tile)
            for _ in prev_tail:
                pass
        allx1 = [f"x1{g['name']}_{eb}" for g in groups for eb in range(1, g["NB"] + 3)]
        sc.add("sp", I("nop"), allx1, [])
        sc.add("act", I("nop"), allx1, [])
        sc.emit(nc, "A")

    with ExitStack() as es:
        def sb(name, shape, dt=F32):
            return es.enter_context(nc.sbuf_tensor("sB_" + name, list(shape), dt))

        ps = es.enter_context(nc.psum_tensor("psB", [128, 8, 512], F32))
        identf = sb("identfB", [128, 128])
        identb = sb("identbB", [128, 128], BF16)
        gcols = sb("gcolsB", [128, 24])
        convc = sb("convc", [128, 4, 44])
        w_up_s = sb("w_up_s", [128, DC, 2 * DFF], BF16)
        w_down_s = sb("w_down_s", [128, FC, D], BF16)
        SW = 1408
        NSTG = 4
        stg = [sb(f"stg{i}", [128, SW]) for i in range(NSTG)]
        TBK = 2
        x1s = [sb(f"x1s{i}", [128, XW]) for i in range(2 * TBK)]
        halo = [sb(f"halo{i}", [2, XW]) for i in range(2)]
        xsB = [sb(f"xsB{i}", [128, D], BF16) for i in range(2)]
        xsh = sb("xsh", [2, D], BF16)
        h2T = sb("h2T", [128, DC, 258], BF16)
        t1g = [sb(f"t1g{i}", [128, 256]) for i in range(2)]
        t1u = [sb(f"t1u{i}", [128, 256]) for i in range(2)]
        t2g, t2u, t3g, t3u = t1g, t1u, t1g, t1u
        gg = [sb(f"gg{i}", [128, 256]) for i in range(2)]
        aT = [sb(f"aT{i}", [128, 256], BF16) for i in range(3)]
        yout = [sb(f"yout{i}", [128, D]) for i in range(2)]

        sc = Sched()

        def dma(out, in_, reads, writes, key):
            sc.add("sp", I("dma_start", out=out, in_=in_), reads=reads, writes=writes, dma=key)

        dma(identf[:], ident_d[:, :], [], ["identf"], "identf")
        sc.add("dve", I("tensor_copy", out=identb[:], in_=identf[:]), ["identf"], ["identb"])
        dma(gcols[:], gcols_d[:, :], [], ["gcols"], "gcols")
        dma(convc[:].rearrange("p a c -> p (a c)"), convc_d[:, :], [], ["convc"], "convc")
        w_up_v = w_up.rearrange("(c p) n -> p c n", p=128)
        w_down_v = w_down.rearrange("(c p) n -> p c n", p=128)
        k = 0
        cengs = ["dve", "act", "dve", "act"]
        for c in range(DC):
            for q in range(4):
                sl = k % NSTG
                k += 1
                dma(stg[sl][:], w_up_v[:, c, q * SW:(q + 1) * SW], [], [f"stg{sl}"], f"stg{sl}")
                if cengs[sl] == "act":
                    sc.add("act", I("activation", out=w_up_s[:, c, q * SW:(q + 1) * SW], in_=stg[sl][:],
                                                                          func=AF.Copy, scale=gcols[:, 16 + c:17 + c]),
                           [f"stg{sl}", "gcols"], [f"w_up_s{sl}"])
                else:
                    sc.add(cengs[sl], I("tensor_scalar", out=w_up_s[:, c, q * SW:(q + 1) * SW], in0=stg[sl][:],
                                                                                 scalar1=gcols[:, 16 + c:17 + c], scalar2=None, op0=ALU.mult),
                           [f"stg{sl}", "gcols"], [f"w_up_s{sl}"])
        for c in range(FC):
            sl = k % NSTG
            k += 1
            dma(stg[sl][:, 0:D], w_down_v[:, c, :], [], [f"stg{sl}"], f"stg{sl}")
            if cengs[sl] == "act":
                sc.add("act", I("activation", out=w_down_s[:, c, :], in_=stg[sl][:, 0:D], func=AF.Copy),
                       [f"stg{sl}"], [f"w_down_s{sl}"])
            else:
                sc.add(cengs[sl], I("tensor_copy", out=w_down_s[:, c, :], in_=stg[sl][:, 0:D]),
                       [f"stg{sl}"], [f"w_down_s{sl}"])
        WUP = [f"w_up_s{i}" for i in range(NSTG)]
        WDN = [f"w_down_s{i}" for i in range(NSTG)]

        ytiles = []
        tcount = 0
        for g in groups:
            n = g["name"]
            NB = g["NB"]
            x1d = dr["x1" + n]
            t0 = 0
            while t0 < NB:
                nbt = min(TBK, NB - t0)
                T = nbt * 128
                par = tcount % 2
                tcount += 1
                r0 = (1 + t0) * 128
                xb = [x1s[par * TBK + i] for i in range(nbt)]
                xbn = [f"x1s{par * TBK + i}" for i in range(nbt)]
                hl = halo[par]
                HL = f"halo{par}"
                for i in range(nbt):
                    dma(xb[i][:], x1d[r0 + i * 128:r0 + (i + 1) * 128, :], [], [xbn[i]], xbn[i])
                dma(hl[0:1, :], x1d[r0 - 1:r0, :], [], [HL + "a"], HL + "a")
                dma(hl[1:2, :], x1d[r0 + T:r0 + T + 1, :], [], [HL + "b"], HL + "b")
                for i in range(nbt):
                    xi = i % 2
                    sc.add("dve", I("tensor_scalar", out=xsB[xi][:], in0=xb[i][:, 0:D], scalar1=xb[i][:, D:D + 1],
                                                                        scalar2=None, op0=ALU.mult), [xbn[i]], [f"xsB{xi}"])
                    for c in range(DC):
                        sc.add("pe", I("transpose", out=ps[:, 0, :].bitcast(BF16)[:, c * 128:(c + 1) * 128],
                                                                       in_=xsB[xi][:, c * 128:(c + 1) * 128], identity=identb[:]),
                               [f"xsB{xi}", "identb"], ["pb0"])
                    sc.add("dve", I("tensor_copy", out=h2T[:, :, 1 + i * 128:1 + (i + 1) * 128],
                                                               in_=ps[:, 0, :].bitcast(BF16)[:, :].rearrange("p (c t) -> p c t", c=DC)),
                           ["pb0"], ["h2T"])
                sc.add("dve", I("tensor_scalar", out=xsh[:], in0=hl[:, 0:D], scalar1=hl[:, D:D + 1], scalar2=None, op0=ALU.mult),
                       [HL + "a", HL + "b"], ["xsh"])
                for c in range(DC):
                    sc.add("pe", I("transpose", out=ps[:, 1, :].bitcast(BF16)[:, c * 2:(c + 1) * 2],
                                                            in_=xsh[0:2, c * 128:(c + 1) * 128], identity=identb[0:2, 0:2]),
                           ["xsh", "identb"], ["pb1"])
                hv = ps[:, 1, :].bitcast(BF16)[:, 0:16].rearrange("p (c t) -> p c t", c=DC)
                sc.add("dve", I("tensor_copy", out=h2T[:, :, 0:1], in_=hv[:, :, 0:1]), ["pb1"], ["h2T"])
                sc.add("dve", I("tensor_copy", out=h2T[:, :, T + 1:T + 2], in_=hv[:, :, 1:2]), ["pb1"], ["h2T"])

                def up(c, T=T):
                    sl = c % 2
                    for (bank, col0, nm) in ((2 * sl, c * 128, f"pbG{sl}"), (2 * sl + 1, DFF + c * 128, f"pbU{sl}")):
                        for kk in range(DC):
                            sc.add("pe", I("matmul",
                                ps[:, bank, 0:T + 2], lhsT=w_up_s[:, kk, col0:col0 + 128], rhs=h2T[:, kk, 0:T + 2],
                                start=(kk == 0), stop=(kk == DC - 1)), ["h2T"] + WUP, [f"pb{bank}"])

                def conv(c, T=T):
                    sl = c % 2
                    bG, bU = 2 * sl, 2 * sl + 1
                    w0 = convc[:, 0, c:c + 1]
                    w1 = convc[:, 1, c:c + 1]
                    w2 = convc[:, 2, c:c + 1]
                    bb = convc[:, 3, c:c + 1]
                    w0u = convc[:, 0, FC + c:FC + c + 1]
                    w1u = convc[:, 1, FC + c:FC + c + 1]
                    w2u = convc[:, 2, FC + c:FC + c + 1]
                    bbu = convc[:, 3, FC + c:FC + c + 1]
                    sc.add("act", I("activation", out=t1g[sl][:, 0:T], in_=ps[:, bG, 1:T + 1], func=AF.Identity, scale=w1, bias=bb),
                           [f"pb{bG}", "convc"], [f"t1g{sl}"])
                    sc.add("act", I("activation", out=t1u[sl][:, 0:T], in_=ps[:, bU, 1:T + 1], func=AF.Identity, scale=w1u, bias=bbu),
                           [f"pb{bU}", "convc"], [f"t1u{sl}"])
                    sc.add("dve", I("scalar_tensor_tensor", out=t2g[sl][:, 0:T], in0=ps[:, bG, 0:T], scalar=w0, in1=t1g[sl][:, 0:T],
                                                                   op0=ALU.mult, op1=ALU.add), [f"pb{bG}", f"t1g{sl}", "convc"], [f"t1g{sl}"])
                    sc.add("dve", I("scalar_tensor_tensor", out=t3g[sl][:, 0:T], in0=ps[:, bG, 2:T + 2], scalar=w2, in1=t2g[sl][:, 0:T],
                                                                   op0=ALU.mult, op1=ALU.add), [f"pb{bG}", f"t1g{sl}", "convc"], [f"t1g{sl}"])
                    sc.add("dve", I("scalar_tensor_tensor", out=t2u[sl][:, 0:T], in0=ps[:, bU, 0:T], scalar=w0u, in1=t1u[sl][:, 0:T],
                                                                   op0=ALU.mult, op1=ALU.add), [f"pb{bU}", f"t1u{sl}", "convc"], [f"t1u{sl}"])
                    sc.add("dve", I("scalar_tensor_tensor", out=t3u[sl][:, 0:T], in0=ps[:, bU, 2:T + 2], scalar=w2u, in1=t2u[sl][:, 0:T],
                                                                   op0=ALU.mult, op1=ALU.add), [f"pb{bU}", f"t1u{sl}", "convc"], [f"t1u{sl}"])
                    sc.add("act", I("activation", out=gg[sl][:, 0:T], in_=t3g[sl][:, 0:T], func=AF.Gelu_apprx_tanh),
                           [f"t1g{sl}"], [f"gg{sl}"])
                    sc.add("dve", I("tensor_tensor", out=aT[c % 3][:, 0:T], in0=gg[sl][:, 0:T], in1=t3u[sl][:, 0:T], op=ALU.mult),
                           [f"gg{sl}", f"t1u{sl}"], [f"aT{c % 3}"])

                def down(c, nbt=nbt):
                    sl = c % 2
                    for tb in range(nbt):
                        for half in range(2):
                            bank = 4 + tb * 2 + half
                            sc.add("pe", I("matmul",
                                ps[:, bank, :], lhsT=aT[c % 3][:, tb * 128:(tb + 1) * 128], rhs=w_down_s[:, c, half * 512:(half + 1) * 512],
                                start=(c == 0), stop=(c == FC - 1)), [f"aT{c % 3}"] + WDN, [f"pb{bank}"])

                up(0)
                up(1)
                conv(0)
                for c in range(FC):
                    if c + 1 < FC:
                        conv(c + 1)
                    if c + 2 < FC:
                        up(c + 2)
                    down(c)
                for tb in range(nbt):
                    yo = yout[tb]
                    YO = f"yout{tb}"
                    for half in range(2):
                        bank = 4 + tb * 2 + half
                        sc.add("dve", I("tensor_tensor",
                            out=yo[:, half * 512:(half + 1) * 512], in0=ps[:, bank, :], in1=xb[tb][:, half * 512:(half + 1) * 512],
                            op=ALU.add), [f"pb{bank}", xbn[tb]], [YO])
                    ykey = f"y{n}_{t0 + tb}"
                    ytiles.append(ykey)
                    dma(dr["y" + n][(t0 + tb) * 128:(t0 + tb + 1) * 128, :], yo[:], [YO], [ykey], YO + "_out")
                t0 += nbt
        sc.add("sp", I("nop", ), ytiles, [])
        sc.add("act", I("nop", ), ytiles, [])
        sc.emit(nc, "B")
    return nc


def _rope_table(pos, S):
    pos = np.clip(pos, 0, S - 1)
    row = (pos // 64).astype(np.float32)
    col = (pos % 64).astype(np.float32)
    inv = (np.float32(10000.0) ** (-np.arange(0, 32, 2, dtype=np.float32) / np.float32(32))).astype(np.float32)
    ang = np.concatenate([row[:, None] * inv[None, :], col[:, None] * inv[None, :]], axis=1).astype(np.float32)
    return np.concatenate([np.cos(ang), np.sin(ang)], axis=1).astype(np.float32)


def _bias_table():
    slopes = np.exp2(-8.0 * np.arange(1, 9, dtype=np.float32) / 8.0).astype(np.float32)
    s = np.arange(128)[:, None]
    q = np.arange(128)[None, :]
    out = np.zeros((128, 8, 3, 128), np.float32)
    for jj in range(3):
        dist = np.abs(q - (s + (jj - 1) * 128)).astype(np.float32)
        valid = dist <= 128
        for h in range(8):
            out[:, h, jj, :] = np.where(valid, -slopes[h] * dist, NEG)
    return out.reshape(128, 8 * 384)


def prepare_core_inputs(core, groups_full, x_by_group, shared):
    m = dict(shared)
    for g in groups_full:
        n, S, NB = g["name"], g["S"], g["NB"]
        x = x_by_group[n]
        b, qtr = core // 4, core % 4
        start = qtr * NB * 128
        NE = NB + 4
        xe = np.zeros((NE * 128, D), np.float32)
        lo, hi = start - 256, start + NB * 128 + 256
        slo, shi = max(lo, 0), min(hi, S)
        xe[slo - lo:shi - lo] = x[b, slo:shi]
        m["xseq" + n] = np.ascontiguousarray(x[b])
        m["xext" + n] = xe
        m["rseq" + n] = g["rseq"]
        m["rext" + n] = _rope_table(np.arange(lo, hi), S)
        blk0 = lo // 128
        val = np.array([1.0 if 0 <= blk0 + e < S // 128 else 0.0 for e in range(NE)], np.float32)
        m["kval" + n] = np.ascontiguousarray(np.broadcast_to(val[None, :], (128, NE)))
        m["bval" + n] = m["kval" + n]
    return m


def make_shared(inp):
    def col(v, nch):
        return np.asarray(v, np.float32).reshape(nch, 128).T

    gcols = np.concatenate([col(inp["norm_mix_g"][0], 8),
                            col(np.concatenate([inp["out_norm_a_g"][0], inp["out_norm_b_g"][0]]), 8),
                            col(inp["norm_ffn_g"][0], 8)], axis=1)
    cw = np.asarray(inp["conv_w"][0], np.float32)
    cb = np.asarray(inp["conv_b"][0], np.float32)
    convc = np.stack([col(cw[0], 44), col(cw[1], 44), col(cw[2], 44), col(cb, 44)], axis=1).reshape(128, 4 * 44)
    hg = np.concatenate([inp["qnorm_a_g"][0], inp["knorm_a_g"][0], inp["qnorm_b_g"][0], inp["knorm_b_g"][0]]).astype(np.float32)
    shared = dict(
        w_in=np.ascontiguousarray(inp["w_in"][0], dtype=np.float32),
        w_out=np.ascontiguousarray(inp["w_out"][0], dtype=np.float32),
        w_up=np.ascontiguousarray(inp["w_up"][0], dtype=np.float32),
        w_down=np.ascontiguousarray(inp["w_down"][0], dtype=np.float32),
        gcols=np.ascontiguousarray(gcols, dtype=np.float32),
        convc=np.ascontiguousarray(convc, dtype=np.float32),
        headg=np.ascontiguousarray(np.broadcast_to(hg[None, :], (128, 256))),
        sinkr=np.ascontiguousarray(np.broadcast_to(np.asarray(inp["sink_b"][0], np.float32)[None, :], (128, 8))),
        biasT=_bias_table(),
        ident=np.eye(128, dtype=np.float32),
    )
    return shared


def run(inp, SP, SS, runner):
    inp = {k: np.asarray(v) for k, v in inp.items()}
    groups = [dict(name="P", S=SP, NB=SP // 512), dict(name="S", S=SS, NB=SS // 512)]
    for g in groups:
        g["rseq"] = _rope_table(np.arange(g["S"]), g["S"])
    nc = build(groups)
    shared = make_shared(inp)
    xg = {"P": np.asarray(inp["x_prompt"], np.float32), "S": np.asarray(inp["x_sample"], np.float32)}
    in_maps = [prepare_core_inputs(c, groups, xg, shared) for c in range(8)]
    results = runner(nc, in_maps)
    outs = []
    for g, key in zip(groups, ("x_prompt", "x_sample")):
        n, S, NB = g["name"], g["S"], g["NB"]
        y = np.zeros((2, S, D), np.float32)
        for c in range(8):
            b, qtr = c // 4, c % 4
            y[b, qtr * NB * 128:(qtr + 1) * NB * 128] = results[c]["y" + n]
        outs.append(y)
    return tuple(outs)


def kernel(**inputs):
    def runner(nc, in_maps):
        res = run_bass_kernel_spmd(nc, in_maps, core_ids=list(range(8)))
        return res.results
    return run(inputs, 16384, 8192, runner)
```

```python
import math
import os
from contextlib import ExitStack
import numpy as np
import concourse.bass as bass
import concourse.mybir as mybir
from concourse.bass_utils import run_bass_kernel_spmd

F32 = mybir.dt.float32
BF16 = mybir.dt.bfloat16
AF = mybir.ActivationFunctionType
ALU = mybir.AluOpType
AX = mybir.AxisListType

D = 1024
DC = 8
INW = 1536
DFF = 2816
FC = 22
EPS = 1e-6
DEBUG_TAGS = os.environ.get("K_TAGS", "0") == "1"
XW = 1028
NEG = -30000.0


def I(name, *args, **kw):
    return (name, args, kw)


class Sched:
    def __init__(self):
        self.ops = []
        self.lw = {}
        self.rd = {}

    @staticmethod
    def _key(op):
        return ("dma", op["dma"]) if op["dma"] is not None else op["eng"]

    def add(self, eng, fn, reads=(), writes=(), dma=None):
        i = len(self.ops)
        deps = {}

        def dep(j):
            k = self._key(self.ops[j])
            if k not in deps or deps[k] < j:
                deps[k] = j

        for b in reads:
            w = self.lw.get(b)
            if w is not None:
                dep(w)
        for b in writes:
            w = self.lw.get(b)
            if w is not None:
                dep(w)
            for j in self.rd.get(b, {}).values():
                dep(j)
        op = dict(eng=eng, fn=fn, deps=[], dma=dma, inc=False, done=None, tag=f"#{i} {fn[0]} r={list(reads)} w={list(writes)}")
        for k, j in deps.items():
            dop = self.ops[j]
            if dop["dma"] is None and dop["eng"] == "pe" and eng == "pe" and dma is None:
                continue
            dop["inc"] = True
            op["deps"].append(j)
        self.ops.append(op)
        me = self._key(op)
        for b in writes:
            self.lw[b] = i
            self.rd[b] = {}
        for b in reads:
            self.rd.setdefault(b, {})[me] = i
        return i

    def emit(self, nc, tag):
        with ExitStack() as es:
            sems = {}
            cnt = {}
            for op in self.ops:
                k = self._key(op)
                if op["dma"] is not None:
                    cnt[k] = cnt.get(k, 0) + 16
                    op["done"] = (k, cnt[k])
                elif op["inc"]:
                    cnt[k] = cnt.get(k, 0) + 1
                    op["done"] = (k, cnt[k])
            for n, k in enumerate(cnt.keys()):
                sems[k] = es.enter_context(nc.semaphore(f"{tag}_s{n}"))
            block = es.enter_context(nc.Block())
            ops = self.ops

            def run(engname):
                def body(eng):
                    waited = {}
                    for op in ops:
                        if op["eng"] != engname:
                            continue
                        for j in op["deps"]:
                            k, v = ops[j]["done"]
                            if waited.get(k, 0) >= v:
                                continue
                            eng.wait_ge(sems[k], v)
                            waited[k] = v
                        name, args, kw = op["fn"]
                        inst = getattr(eng, name)(*args, **kw)
                        if DEBUG_TAGS:
                            inst.annotate(op["tag"])
                        if op["done"] is not None:
                            k, v = op["done"]
                            inst.then_inc(sems[k], 16 if op["dma"] is not None else 1)
                return body

            block.sync(run("sp"))
            block.tensor(run("pe"))
            block.vector(run("dve"))
            block.scalar(run("act"))
            block.gpsimd(run("pool"))


def build(groups):
    nc = bass.Bass("TRN2", target_bir_lowering=False)

    def din(name, shape):
        return nc.dram_tensor(name, list(shape), F32, kind="ExternalInput").ap()

    def dout(name, shape):
        return nc.dram_tensor(name, list(shape), F32, kind="ExternalOutput").ap()

    dr = {}
    for g in groups:
        n = g["name"]
        S, NB = g["S"], g["NB"]
        NE = NB + 4
        dr["xseq" + n] = din("xseq" + n, [S, D])
        dr["xext" + n] = din("xext" + n, [NE * 128, D])
        dr["rseq" + n] = din("rseq" + n, [S, 64])
        dr["rext" + n] = din("rext" + n, [NE * 128, 64])
        dr["kval" + n] = din("kval" + n, [128, NE])
        dr["bval" + n] = din("bval" + n, [128, NE])
        dr["y" + n] = dout("y" + n, [NB * 128, D])
        dr["x1" + n] = nc.dram_tensor("x1" + n, [(NB + 2) * 128, XW], F32).ap()
    w_in = din("w_in", [D, INW])
    w_out = din("w_out", [D, D])
    w_up = din("w_up", [D, 2 * DFF])
    w_down = din("w_down", [DFF, D])
    gcols_d = din("gcols", [128, 24])
    convc_d = din("convc", [128, 4 * 44])
    headg_d = din("headg", [128, 4 * 64])
    sink_d = din("sinkr", [128, 8])
    bias_d = din("biasT", [128, 8 * 384])
    ident_d = din("ident", [128, 128])

    SMAX = max(g["S"] for g in groups)
    NEMAX = max(g["NB"] for g in groups) + 4

    with ExitStack() as es:
        def sb(name, shape, dt=F32):
            return es.enter_context(nc.sbuf_tensor("sA_" + name, list(shape), dt))

        ps = es.enter_context(nc.psum_tensor("psA", [128, 8, 512], F32))
        identf = sb("identf", [128, 128])
        identb = sb("identb", [128, 128], BF16)
        gcols = sb("gcols", [128, 24])
        headg = sb("headg", [128, 4, 64])
        sinkt = sb("sinkt", [128, 8])
        esink = sb("esink", [128, 8])
        biasb = sb("biasb", [128, 8, 384], BF16)
        kval = sb("kval", [128, NEMAX])
        bval = sb("bval", [128, NEMAX])
        w_in_s = sb("w_in_s", [128, DC, INW], BF16)
        w_out_s = sb("w_out_s", [128, DC, D], BF16)
        KT = sb("KT", [128, SMAX], BF16)
        Vaug = sb("Vaug", [128, SMAX // 128, 2, 65], BF16)
        KbT = sb("KbT", [128, NEMAX * 128], BF16)
        Vb = sb("Vb", [128, NEMAX, 2, 65], BF16)
        xt = [sb(f"xt{i}", [128, D]) for i in range(2)]
        xs0 = sb("xs0", [128, D], BF16)
        hT = [sb(f"hT{i}", [128, DC, 128], BF16) for i in range(2)]
        sqj = sb("sqj", [128, D])
        ropet = sb("ropet", [128, 64])
        ssx = sb("ssx", [128, 1])
        rsx = sb("rsx", [128, 1])
        qraw = sb("qraw", [128, 1024])
        ssq = sb("ssq", [128, 16])
        rsq = sb("rsq", [128, 16])
        arena2 = sb("arena2", [128, 5120])
        QaT = [arena2[:, i * 1024:(i + 1) * 1024].bitcast(BF16).rearrange("p (j t) -> p j t", j=4) for i in range(2)]
        QbT = arena2[:, 2048:3072].bitcast(BF16).rearrange("p (j t) -> p j t", j=4)
        pT = [arena2[:, 3072 + i * 512:3584 + i * 512].bitcast(BF16).rearrange("p (j t) -> p j t", j=2) for i in range(2)]
        qra = arena2[:, 4096:4352].bitcast(BF16)
        qrb = arena2[:, 4352:4608].bitcast(BF16)
        yT = arena2[:, 4608:5120].bitcast(BF16).rearrange("p (c t) -> p c t", c=DC)
        oT = sb("oT", [128, 512])
        pT2 = sb("pT2", [128, 2, 512], BF16)
        den = sb("den", [128, 4])
        rden = sb("rden", [128, 4])
        ssy = sb("ssy", [128, 8])
        rsy = sb("rsy", [128, 8])
        x1t = sb("x1t", [128, XW])
        tA = x1t[:, 0:256]
        tB = x1t[:, 256:512]
        ss1 = sb("ss1", [128, 1])
        rs1 = sb("rs1", [128, 1])
        fsc = sb("fsc", [128, 1])
        arena = sb("arena", [128, 6144])
        ya = arena[:, 0:2048].rearrange("p (b f) -> p b f", b=4)
        yb = arena[:, 2048:4096].rearrange("p (b f) -> p b f", b=4)
        ybf = arena[:, 4096:6144].bitcast(BF16).rearrange("p (b f) -> p b f", b=4)
        KW = 5
        _ar = [[arena, 0, 6144], [arena2, 0, 5120]]

        def carve(words):
            for a in _ar:
                if a[1] + words <= a[2]:
                    v = a[0][:, a[1]:a[1] + words]
                    a[1] += words
                    return v
            raise AssertionError("KV scratch arena exhausted")

        xtK, hTK, xsK, krawK, sqkK, knK, kn2K, krK, ropeK = [], [], [], [], [], [], [], [], []
        for i in range(KW):
            xtK.append(xt[i][:] if i < 2 else carve(1024))
            hTK.append(hT[i][:] if i < 2 else carve(512).bitcast(BF16).rearrange("p (c t) -> p c t", c=DC))
            xsK.append(carve(512).bitcast(BF16))
            krawK.append(carve(256))
            sqkK.append(carve(128))
            knK.append(carve(128))
            kn2K.append(carve(128))
            krK.append(carve(64).bitcast(BF16))
            ropeK.append(carve(64))
        ssxK = [sb(f"ssxK{i}", [128, 1]) for i in range(KW)]
        rsxK = [sb(f"rsxK{i}", [128, 1]) for i in range(KW)]
        sskK = [sb(f"sskK{i}", [128, 2]) for i in range(KW)]
        rskK = [sb(f"rskK{i}", [128, 2]) for i in range(KW)]
        ATT_NAMES = ["ya", "yb", "ybf", "QaT0", "QaT1", "QbT", "pT0", "pT1", "qra", "qrb", "yT"]
        KV_NAMES = ([f"xtK{i}" for i in range(KW)] + [f"hTK{i}" for i in range(KW)] + [f"xsK{i}" for i in range(KW)]
                    + [f"krawK{i}" for i in range(KW)] + [f"sqkK{i}" for i in range(KW)] + [f"knK{i}" for i in range(KW)]
                    + [f"kn2K{i}" for i in range(KW)] + [f"krK{i}" for i in range(KW)] + [f"ropeK{i}" for i in range(KW)])

        pT.append(pT2[:])
        sc = Sched()
        PB = [f"pb{i}" for i in range(8)]
        st_slot = [ps[:, 0:2, :], ps[:, 2:4, :]]
        ST = [["pb0", "pb1"], ["pb2", "pb3"]]
        oacc = [ps[:, 4, :], ps[:, 5, :]]
        OA = ["pb4", "pb5"]
        bank6 = ps[:, 6, :]
        bank6b = ps[:, 6, :].bitcast(BF16)
        bank7 = ps[:, 7, :]
        bank7b = ps[:, 7, :].bitcast(BF16)
        cnt = dict(xt=0, st=0, oa=0, pt=0)

        def dma(out, in_, reads, writes, key):
            sc.add("sp", I("dma_start", out=out, in_=in_), reads=reads, writes=writes, dma=key)

        def fence():
            sc.add("dve", I("memset", fsc[:], 0.0), [], ATT_NAMES + KV_NAMES + ["fsc"])

        dma(identf[:], ident_d[:, :], [], ["identf"], "identf")
        sc.add("dve", I("tensor_copy", out=identb[:], in_=identf[:]), ["identf"], ["identb"])
        dma(gcols[:], gcols_d[:, :], [], ["gcols"], "gcols")
        dma(headg[:].rearrange("p a d -> p (a d)"), headg_d[:, :], [], ["headg"], "headg")
        dma(sinkt[:], sink_d[:, :], [], ["sinkt"], "sinkt")
        sc.add("act", I("activation", out=esink[:], in_=sinkt[:], func=AF.Exp), ["sinkt"], ["esink"])
        k = 0
        for i in range(3):
            sl = k % 2
            k += 1
            dma(xt[sl][:], bias_d[:, i * 1024:(i + 1) * 1024], [], [f"xtK{sl}"], f"xtK{sl}")
            sc.add("dve", I("tensor_copy", out=biasb[:].rearrange("p h k -> p (h k)")[:, i * 1024:(i + 1) * 1024], in_=xt[sl][:]),
                   [f"xtK{sl}"], ["biasb"])
        w_in_v = w_in.rearrange("(c p) n -> p c n", p=128)
        w_out_v = w_out.rearrange("(c p) n -> p c n", p=128)
        for c in range(DC):
            for h in range(2):
                sl = k % 2
                k += 1
                dma(xt[sl][:, 0:256], w_in_v[:, c, h * 768 + 512:(h + 1) * 768], [], [f"xtK{sl}"], f"xtK{sl}")
                sc.add("act", I("activation", out=w_in_s[:, c, h * 768 + 512:(h + 1) * 768], in_=xt[sl][:, 0:256], func=AF.Copy,
                                scale=gcols[:, c:c + 1]), [f"xtK{sl}", "gcols"], ["w_in_s2"])
        for c in range(DC):
            for h in range(2):
                sl = k % 2
                k += 1
                dma(xt[sl][:, 0:512], w_in_v[:, c, h * 768:h * 768 + 512], [], [f"xtK{sl}"], f"xtK{sl}")
                sc.add("dve", I("tensor_scalar",
                                out=w_in_s[:, c, h * 768:h * 768 + 512].rearrange("p (j two d) -> p two j d", j=4, two=2),
                                in0=xt[sl][:, 0:512].rearrange("p (two j d) -> p two j d", two=2, j=4), scalar1=gcols[:, c:c + 1],
                                scalar2=None, op0=ALU.mult), [f"xtK{sl}", "gcols"], ["w_in_s"])
        cnt["xt"] = k

        def load_w_out():
            for c in range(DC):
                sl = cnt["xt"] % 2
                cnt["xt"] += 1
                dma(xt[sl][:], w_out_v[:, c, :], [], [f"xtK{sl}"], f"xtK{sl}")
                sc.add("dve", I("tensor_scalar", out=w_out_s[:, c, :], in0=xt[sl][:], scalar1=gcols[:, 8 + c:9 + c],
                                scalar2=None, op0=ALU.mult), [f"xtK{sl}", "gcols"], ["w_out_s"])
        WIN = ["w_in_s", "w_in_s2"]

        def rstd_ops(ss_ap, rs_ap, n, ssname, rsname, bias2=0.0):
            sc.add("act", I("activation", out=rs_ap, in_=ss_ap, func=AF.Ln, scale=1.0 / n, bias=EPS), [ssname], [rsname])
            sc.add("act", I("activation", out=rs_ap, in_=rs_ap, func=AF.Exp, scale=-0.5, bias=bias2), [rsname], [rsname])

        def rope_ops(xin, H, rope_ap, rname, out_ap_fn, inname, outname):
            xv = xin.rearrange("p (h a f e) -> p h a f e", h=H, a=2, f=2)
            x1 = xv[:, :, :, 0, :]
            x2 = xv[:, :, :, 1, :]
            C = rope_ap[:, 0:32].rearrange("p (a e) -> p a e", a=2).unsqueeze(1).to_broadcast([128, H, 2, 16])
            Sn = rope_ap[:, 32:64].rearrange("p (a e) -> p a e", a=2).unsqueeze(1).to_broadcast([128, H, 2, 16])
            tAv = tA[:, 0:H * 32].rearrange("p (h a e) -> p h a e", h=H, a=2)
            tBv = tB[:, 0:H * 32].rearrange("p (h a e) -> p h a e", h=H, a=2)
            sc.add("dve", I("tensor_tensor", out=tAv, in0=x1, in1=C, op=ALU.mult), [inname, rname], ["x1t"])
            sc.add("dve", I("tensor_tensor", out=tBv, in0=x2, in1=Sn, op=ALU.mult), [inname, rname], ["x1t"])
            sc.add("dve", I("tensor_tensor", out=out_ap_fn(0), in0=tAv, in1=tBv, op=ALU.subtract), ["x1t"], [outname])
            sc.add("dve", I("tensor_tensor", out=tAv, in0=x2, in1=C, op=ALU.mult), [inname, rname], ["x1t"])
            sc.add("dve", I("tensor_tensor", out=tBv, in0=x1, in1=Sn, op=ALU.mult), [inname, rname], ["x1t"])
            sc.add("dve", I("tensor_tensor", out=out_ap_fn(1), in0=tAv, in1=tBv, op=ALU.add), ["x1t"], [outname])

        def kv_gen(src_rows, rope_rows, kcol, gidx, KTt, Vt, idx, ktname, vname, s):
            X, XS, HT, KR, SQ, KN, KN2, KRR, RP = (f"xtK{s}", f"xsK{s}", f"hTK{s}", f"krawK{s}", f"sqkK{s}", f"knK{s}",
                                                   f"kn2K{s}", f"krK{s}", f"ropeK{s}")
            bT, bP = s, s
            bTb = ps[:, bT, :].bitcast(BF16)
            dma(xtK[s], src_rows, [], [X], X)
            if rope_rows is not None:
                dma(ropeK[s], rope_rows, [], [RP], RP)
            yield
            sc.add("act", I("activation", out=qraw[:], in_=xtK[s], func=AF.Square, accum_out=ssxK[s][:]), [X], ["qraw", f"ssxK{s}"])
            yield
            rstd_ops(ssxK[s][:], rsxK[s][:], D, f"ssxK{s}", f"rsxK{s}")
            yield
            sc.add("dve", I("tensor_scalar", out=xsK[s], in0=xtK[s], scalar1=rsxK[s][:, 0:1], scalar2=None, op0=ALU.mult),
                   [X, f"rsxK{s}"], [XS])
            yield
            for c in range(DC):
                sc.add("pe", I("transpose", out=bTb[:, c * 128:(c + 1) * 128], in_=xsK[s][:, c * 128:(c + 1) * 128], identity=identb[:]),
                       [XS, "identb"], [PB[bT]])
            yield
            sc.add("act", I("activation", out=hTK[s].rearrange("p c t -> p (c t)"), in_=bTb[:, :], func=AF.Copy), [PB[bT]], [HT])
            yield
            for c in range(DC):
                sc.add("pe", I("matmul", ps[:, bP, 0:256], lhsT=hTK[s][:, c, :], rhs=w_in_s[:, c, kcol:kcol + 256],
                               start=(c == 0), stop=(c == DC - 1)), [HT, "w_in_s2"], [PB[bP]])
            yield
            sc.add("act", I("activation", out=krawK[s], in_=ps[:, bP, 0:256], func=AF.Copy), [PB[bP]], [KR])
            yield
            sc.add("pool", I("tensor_copy", out=Vt[:, idx, :, 0:64], in_=krawK[s][:, 128:256].rearrange("p (h d) -> p h d", h=2)),
                   [KR], [vname])
            sc.add("pool", I("tensor_tensor", out=sqkK[s], in0=krawK[s][:, 0:128], in1=krawK[s][:, 0:128], op=ALU.mult), [KR], [SQ])
            yield
            sc.add("dve", I("tensor_reduce", out=sskK[s][:], in_=sqkK[s].rearrange("p (h d) -> p h d", h=2), axis=AX.X, op=ALU.add),
                   [SQ], [f"sskK{s}"])
            yield
            rstd_ops(sskK[s][:], rskK[s][:], 64, f"sskK{s}", f"rskK{s}")
            yield
            sc.add("dve", I("tensor_tensor", out=knK[s].rearrange("p (h d) -> p h d", h=2),
                            in0=krawK[s][:, 0:128].rearrange("p (h d) -> p h d", h=2),
                            in1=rskK[s][:].unsqueeze(2).to_broadcast([128, 2, 64]), op=ALU.mult), [KR, f"rskK{s}"], [KN])
            gv = headg[:, gidx, :].unsqueeze(1).to_broadcast([128, 2, 64])
            if rope_rows is not None:
                sc.add("dve", I("tensor_tensor", out=kn2K[s].rearrange("p (h d) -> p h d", h=2),
                                in0=knK[s].rearrange("p (h d) -> p h d", h=2), in1=gv, op=ALU.mult), [KN, "headg"], [KN2])
                krv = krK[s].rearrange("p (h a f e) -> p h a f e", h=2, a=2, f=2)
                rope_ops(kn2K[s], 2, ropeK[s], RP, lambda half: krv[:, :, :, half, :], KN2, KRR)
            else:
                sc.add("dve", I("tensor_tensor", out=krK[s].rearrange("p (h d) -> p h d", h=2),
                                in0=knK[s].rearrange("p (h d) -> p h d", h=2), in1=gv, op=ALU.mult), [KN, "headg"], [KRR])
            yield
            sc.add("pe", I("transpose", out=bTb[:, 0:128], in_=krK[s], identity=identb[:]), [KRR, "identb"], [PB[bT]])
            yield
            sc.add("act", I("activation", out=KTt[:, idx * 128:(idx + 1) * 128], in_=bTb[:, 0:128], func=AF.Copy), [PB[bT]], [ktname])
            yield

        def interleave(gens, width, stagger=3):
            active = []
            it = iter(gens)
            since = stagger
            done = False
            while True:
                if not done and len(active) < width and since >= stagger:
                    gnew = next(it, None)
                    if gnew is None:
                        done = True
                    else:
                        active.append(gnew)
                        since = 0
                if not active:
                    if done:
                        break
                    since = stagger
                    continue
                since += 1
                for gg_ in list(active):
                    try:
                        next(gg_)
                    except StopIteration:
                        active.remove(gg_)

        def evac_copy(ob, nb):
            sc.add("dve", I("tensor_copy", out=oT[0:65, 0:nb * 128], in_=oacc[ob][0:65, 0:nb * 128]), [OA[ob]], ["oT"])

        def evac_finish(nb, h, ydst, yname, sink):
            b6v = bank6[:, 0:4 * 65].rearrange("p (b d) -> p b d", d=65)
            for bi in range(nb):
                sc.add("pe", I("transpose", out=b6v[:, bi, :], in_=oT[0:65, bi * 128:(bi + 1) * 128],
                               identity=identf[0:65, 0:65]), ["oT", "identf"], ["pb6"])
            if sink:
                sc.add("dve", I("tensor_scalar", out=den[:, 0:nb], in0=b6v[:, 0:nb, 64], scalar1=esink[:, h:h + 1],
                                scalar2=None, op0=ALU.add), ["pb6", "esink"], ["den"])
                sc.add("dve", I("reciprocal", out=rden[:, 0:nb], in_=den[:, 0:nb]), ["den"], ["rden"])
            else:
                sc.add("dve", I("reciprocal", out=rden[:, 0:nb], in_=b6v[:, 0:nb, 64]), ["pb6"], ["rden"])
            sc.add("dve", I("tensor_tensor", out=ydst[:, 0:nb, h * 64:(h + 1) * 64], in0=b6v[:, 0:nb, 0:64],
                            in1=rden[:, 0:nb].unsqueeze(2).to_broadcast([128, nb, 64]), op=ALU.mult), ["pb6", "rden"], [yname])

        def evac_O(ob, nb, h, ydst, yname, sink):
            evac_copy(ob, nb)
            evac_finish(nb, h, ydst, yname, sink)

        def proj_gen(g, tile, qslot):
            n = g["name"]
            for bi, eb in enumerate(tile):
                sl = cnt["xt"] % 2
                cnt["xt"] += 1
                X = f"xtK{sl}"
                dma(ropet[:], dr["rext" + n][eb * 128:(eb + 1) * 128, :], [], ["ropet"], "ropet")
                dma(xt[sl][:], dr["xext" + n][eb * 128:(eb + 1) * 128, :], [], [X], X)
                yield
                sc.add("pool", I("tensor_tensor", out=sqj[:], in0=xt[sl][:], in1=xt[sl][:], op=ALU.mult), [X], ["sqj"])
                yield
                sc.add("dve", I("tensor_reduce", out=ssx[:], in_=sqj[:], axis=AX.X, op=ALU.add), ["sqj"], ["ssx"])
                yield
                rstd_ops(ssx[:], rsx[:], D, "ssx", "rsx")
                yield
                sc.add("dve", I("tensor_scalar", out=xs0[:], in0=xt[sl][:], scalar1=rsx[:, 0:1], scalar2=None, op0=ALU.mult),
                       [X, "rsx"], ["xs0"])
                yield
                for c in range(DC):
                    sc.add("pe", I("transpose", out=bank7b[:, c * 128:(c + 1) * 128], in_=xs0[:, c * 128:(c + 1) * 128], identity=identb[:]),
                           ["xs0", "identb"], ["pb7"])
                yield
                sc.add("dve", I("tensor_copy", out=hT[0][:].rearrange("p c t -> p (c t)"), in_=bank7b[:, :]), ["pb7"], ["hTK0"])
                yield
                for half, col0 in ((0, 0), (1, 768)):
                    for c in range(DC):
                        sc.add("pe", I("matmul", bank7[:, :], lhsT=hT[0][:, c, :], rhs=w_in_s[:, c, col0:col0 + 512],
                                       start=(c == 0), stop=(c == DC - 1)), ["hTK0", "w_in_s"], ["pb7"])
                    yield
                    sc.add("dve", I("tensor_copy", out=qraw[:, half * 512:(half + 1) * 512], in_=bank7[:, :]), ["pb7"], ["qraw"])
                    yield
                sc.add("pool", I("tensor_tensor", out=sqj[:], in0=qraw[:], in1=qraw[:], op=ALU.mult), ["qraw"], ["sqj"])
                yield
                sc.add("dve", I("tensor_reduce", out=ssq[:], in_=sqj[:].rearrange("p (h d) -> p h d", d=64), axis=AX.X, op=ALU.add),
                       ["sqj"], ["ssq"])
                yield
                rstd_ops(ssq[:], rsq[:], 64, "ssq", "rsq", bias2=-math.log(8.0))
                yield
                sc.add("dve", I("tensor_tensor", out=qraw[:].rearrange("p (h d) -> p h d", d=64),
                                in0=qraw[:].rearrange("p (h d) -> p h d", d=64),
                                in1=rsq[:].unsqueeze(2).to_broadcast([128, 16, 64]), op=ALU.mult), ["qraw", "rsq"], ["qraw"])
                sc.add("dve", I("tensor_tensor", out=qrb[:].rearrange("p (h d) -> p h d", d=64),
                                in0=qraw[:, 512:1024].rearrange("p (h d) -> p h d", d=64),
                                in1=headg[:, 2, :].unsqueeze(1).to_broadcast([128, 8, 64]), op=ALU.mult), ["qraw", "headg"], ["qrb"])
                yield
                for j in range(4):
                    sc.add("pe", I("transpose", out=bank7b[:, j * 128:(j + 1) * 128], in_=qrb[:, j * 128:(j + 1) * 128], identity=identb[:]),
                           ["qrb", "identb"], ["pb7"])
                sc.add("dve", I("tensor_tensor", out=qraw[:, 0:512].rearrange("p (h d) -> p h d", d=64),
                                in0=qraw[:, 0:512].rearrange("p (h d) -> p h d", d=64),
                                in1=headg[:, 0, :].unsqueeze(1).to_broadcast([128, 8, 64]), op=ALU.mult), ["qraw", "headg"], ["qraw"])
                qrav = qra[:].rearrange("p (h a f e) -> p h a f e", h=8, a=2, f=2)
                rope_ops(qraw[:, 0:512], 8, ropet[:], "ropet", lambda half: qrav[:, :, :, half, :], "qraw", "qra")
                yield
                sc.add("dve", I("tensor_copy", out=QbT[:, :, bi * 128:(bi + 1) * 128],
                                in_=bank7b[:, 0:512].rearrange("p (j t) -> p j t", j=4)), ["pb7"], ["QbT"])
                yield
                for j in range(4):
                    sc.add("pe", I("transpose", out=bank7b[:, j * 128:(j + 1) * 128], in_=qra[:, j * 128:(j + 1) * 128], identity=identb[:]),
                           ["qra", "identb"], ["pb7"])
                yield
                sc.add("dve", I("tensor_copy", out=QaT[qslot][:, :, bi * 128:(bi + 1) * 128],
                                in_=bank7b[:, 0:512].rearrange("p (j t) -> p j t", j=4)), ["pb7"], [f"QaT{qslot}"])
                yield

        def tail0(nb):
            for grp, ysrc, yn in ((0, ya, "ya"), (1, yb, "yb")):
                for bi in range(nb):
                    sc.add("act", I("activation", out=oT[:, 0:512], in_=ysrc[:, bi, :], func=AF.Square,
                                    accum_out=ssy[:, grp * 4 + bi:grp * 4 + bi + 1]), [yn], ["oT", "ssy"])
            rstd_ops(ssy[:], rsy[:], 512, "ssy", "rsy")
            for grp, ysrc, yn in ((0, ya, "ya"), (1, yb, "yb")):
                for bi in range(nb):
                    sc.add("dve", I("tensor_scalar", out=ybf[:, bi, grp * 512:(grp + 1) * 512], in0=ysrc[:, bi, :],
                                    scalar1=rsy[:, grp * 4 + bi:grp * 4 + bi + 1], scalar2=None, op0=ALU.mult), [yn, "rsy"], ["ybf"])

        def tail_gen(g, tile):
            n = g["name"]
            for bi, eb in enumerate(tile):
                dma(x1t[:, 0:D], dr["xext" + n][eb * 128:(eb + 1) * 128, :], [], ["x1t"], "x1t_in")
                for c in range(DC):
                    sc.add("pe", I("transpose", out=bank7b[:, c * 128:(c + 1) * 128], in_=ybf[:, bi, c * 128:(c + 1) * 128], identity=identb[:]),
                           ["ybf", "identb"], ["pb7"])
                yield
                sc.add("dve", I("tensor_copy", out=yT[:].rearrange("p c t -> p (c t)"), in_=bank7b[:, :]), ["pb7"], ["yT"])
                yield
                for half in range(2):
                    for c in range(DC):
                        sc.add("pe", I("matmul", bank7[:, :], lhsT=yT[:, c, :], rhs=w_out_s[:, c, half * 512:(half + 1) * 512],
                                       start=(c == 0), stop=(c == DC - 1)), ["yT", "w_out_s"], ["pb7"])
                    yield
                    sc.add("dve", I("tensor_tensor", out=x1t[:, half * 512:(half + 1) * 512], in0=bank7[:, :],
                                    in1=x1t[:, half * 512:(half + 1) * 512], op=ALU.add), ["pb7", "x1t"], ["x1t"])
                    yield
                sc.add("pool", I("tensor_tensor", out=sqj[:], in0=x1t[:, 0:D], in1=x1t[:, 0:D], op=ALU.mult), ["x1t"], ["sqj"])
                yield
                sc.add("dve", I("tensor_reduce", out=ss1[:], in_=sqj[:], axis=AX.X, op=ALU.add), ["sqj"], ["ss1"])
                yield
                rstd_ops(ss1[:], rs1[:], D, "ss1", "rs1")
                yield
                sc.add("dve", I("tensor_tensor", out=x1t[:, D:D + 1], in0=rs1[:], in1=bval[:, eb:eb + 1], op=ALU.mult), ["rs1", "bval"], ["x1t"])
                sc.add("dve", I("memset", x1t[:, D + 1:XW], 0.0), [], ["x1t"])
                yield
                dkey = f"x1{n}_{eb}"
                dma(dr["x1" + n][(eb - 1) * 128:eb * 128, :], x1t[:], ["x1t"], [dkey], "x1t_out")
                yield

        def window_attn(g, tile):
            nb = len(tile)
            halves = [list(range(0, min(2, nb)))] + ([list(range(2, nb))] if nb > 2 else [])
            steps = [(h, hv) for h in range(8) for hv in halves]
            pend = None
            wdef = []

            def pv(h, hv, slot, ob):
                two = h // 4
                for bi_l, bi in enumerate(hv):
                    eb = tile[bi]
                    for jj in range(3):
                        ke = eb - 1 + jj
                        sc.add("pe", I("matmul", oacc[ob][0:65, bi * 128:(bi + 1) * 128], lhsT=Vb[:, ke, two, :],
                                       rhs=pT[slot][:, bi_l, jj * 128:(jj + 1) * 128], start=(jj == 0), stop=(jj == 2)),
                               ["Vb", f"pT{slot}"], [OA[ob]])

            for si, (h, hv) in enumerate(steps):
                j, two = h % 4, h // 4
                r0 = two * 64
                slot = cnt["st"] % 2
                cnt["st"] += 1
                if hv is halves[0]:
                    cnt["oa"] += 1
                ob = cnt["oa"] % 2
                for bi_l, bi in enumerate(hv):
                    eb = tile[bi]
                    for jj in range(3):
                        ke = eb - 1 + jj
                        sc.add("pe", I("matmul", st_slot[slot][:, bi_l, jj * 128:(jj + 1) * 128], lhsT=KbT[r0:r0 + 64, ke * 128:(ke + 1) * 128],
                                       rhs=QbT[r0:r0 + 64, j, bi * 128:(bi + 1) * 128], start=True, stop=True),
                               ["KbT", "QbT"], ST[slot])
                nl = len(hv)
                sc.add("dve", I("tensor_tensor", out=st_slot[slot][:, 0:nl, 0:384], in0=st_slot[slot][:, 0:nl, 0:384],
                                in1=biasb[:, h, :].unsqueeze(1).to_broadcast([128, nl, 384]), op=ALU.add), ST[slot] + ["biasb"], ST[slot])
                sc.add("act", I("activation", out=pT[slot][:, 0:nl, 0:384], in_=st_slot[slot][:, 0:nl, 0:384], func=AF.Exp),
                       ST[slot], [f"pT{slot}"])
                if wdef:
                    wdef.pop(0)()
                if pend is not None:
                    pv(*pend[:4])
                    if pend[4]:
                        evac_copy(pend[3], nb)
                        wdef.append(lambda hh=pend[0]: evac_finish(nb, hh, yb, "yb", True))
                pend = (h, hv, slot, ob, hv is halves[-1])
            while wdef:
                wdef.pop(0)()
            pv(*pend[:4])
            evac_O(pend[3], nb, pend[0], yb, "yb", True)

        def global_attn(g, tile, qslot, bg, nyield):
            nb = len(tile)
            Tq = nb * 128
            nkb = g["S"] // 128
            bgstep = max(1, (4 * nkb) // (nyield + 6))
            it = 0
            deferred = []
            for j in range(4):
                obs = (0, 1)
                pend = []

                def pv(pkb, pslot):
                    for two in range(2):
                        sc.add("pe", I("matmul", oacc[obs[two]][0:65, 0:Tq], lhsT=Vaug[:, pkb, two, :], rhs=pT[pslot][:, two, 0:Tq],
                                       start=(pkb == 0), stop=(pkb == nkb - 1)), ["Vaug", f"pT{pslot}"], [OA[obs[two]]])

                for kb in range(nkb):
                    slot = cnt["st"] % 2
                    cnt["st"] += 1
                    pslot = cnt["pt"] % 3
                    cnt["pt"] += 1
                    for two in range(2):
                        r0 = two * 64
                        sc.add("pe", I("matmul", st_slot[slot][:, two, 0:Tq], lhsT=KT[r0:r0 + 64, kb * 128:(kb + 1) * 128],
                                       rhs=QaT[qslot][r0:r0 + 64, j, 0:Tq], start=True, stop=True), ["KT", f"QaT{qslot}"], ST[slot])
                    sc.add("act", I("activation", out=pT[pslot][:, :, 0:Tq], in_=st_slot[slot][:, :, 0:Tq], func=AF.Exp),
                           ST[slot], [f"pT{pslot}"])
                    pend.append((kb, pslot))
                    if kb in (1, 3) and deferred:
                        deferred.pop(0)()
                    if len(pend) > 2:
                        pv(*pend.pop(0))
                    it += 1
                    if it % bgstep == 0 and os.environ.get("K_BG", "1") == "1":
                        next(bg, None)
                while pend:
                    pv(*pend.pop(0))
                for d_ in deferred:
                    d_()
                deferred.clear()
                evac_copy(obs[0], nb)
                deferred.append(lambda j=j: (evac_finish(nb, j, ya, "ya", False), evac_copy(obs[1], nb)))
                deferred.append(lambda j=j: evac_finish(nb, 4 + j, ya, "ya", False))
            for d_ in deferred:
                d_()
            deferred.clear()
            for _ in bg:
                pass

        def chain(*gens):
            for gen in gens:
                if gen is None:
                    continue
                for _ in gen:
                    yield

        sc.add("pool", I("memset", ssy[:], 1.0), [], ["ssy"])
        sc.add("pool", I("memset", Vaug[:].rearrange("p b k d -> p (b k d)"), 1.0), [], ["Vaug"])
        for g in groups:
            n = g["name"]
            S, NB = g["S"], g["NB"]
            NE = NB + 4
            fence()
            dma(kval[:, 0:NE], dr["kval" + n][:, :], [], ["kval"], "kval")
            dma(bval[:, 0:NE], dr["bval" + n][:, :], [], ["bval"], "bval")
            for kvh in range(2):
                sc.add("pool", I("tensor_copy", out=Vb[:, 0:NE, kvh, 64], in_=kval[:, 0:NE]), ["kval"], ["Vb"])
            gens = []
            ctr = 0
            for b in range(S // 128):
                gens.append(kv_gen(dr["xseq" + n][b * 128:(b + 1) * 128, :], dr["rseq" + n][b * 128:(b + 1) * 128, :],
                                   512, 1, KT, Vaug, b, "KT", "Vaug", ctr % KW))
                ctr += 1
            for eb in range(NE):
                gens.append(kv_gen(dr["xext" + n][eb * 128:(eb + 1) * 128, :], None, 1280, 3, KbT, Vb, eb, "KbT", "Vb", ctr % KW))
                ctr += 1
            interleave(gens, KW)
            fence()
            blocks = list(range(1, NB + 3))
            tiles = [blocks[i:i + 4] for i in range(0, len(blocks), 4)]
            for _ in proj_gen(g, tiles[0], 0):
                pass
            prev_tail = None
            for ti, tile in enumerate(tiles):
                qslot = ti % 2
                window_attn(g, tile)
                if g is groups[0] and ti == 0:
                    load_w_out()
                nxt = proj_gen(g, tiles[ti + 1], (ti + 1) % 2) if ti + 1 < len(tiles) else None
                ny = (11 * 4 if prev_tail is not None else 0) + (20 * 4 if nxt is not None else 0)
                global_attn(g, tile, qslot, chain(prev_tail, nxt), ny)
                tail0(len(tile))
                prev_tail = tail_gen(g, tile)
            for _ in prev_tail:
                pass
        allx1 = [f"x1{g['name']}_{eb}" for g in groups for eb in range(1, g["NB"] + 3)]
        sc.add("sp", I("nop"), allx1, [])
        sc.add("act", I("nop"), allx1, [])
        sc.emit(nc, "A")

    with ExitStack() as es:
        def sb(name, shape, dt=F32):
            return es.enter_context(nc.sbuf_tensor("sB_" + name, list(shape), dt))

        ps = es.enter_context(nc.psum_tensor("psB", [128, 8, 512], F32))
        identf = sb("identfB", [128, 128])
        identb = sb("identbB", [128, 128], BF16)
        gcols = sb("gcolsB", [128, 24])
        convc = sb("convc", [128, 4, 44])
        w_up_s = sb("w_up_s", [128, DC, 2 * DFF], BF16)
        w_down_s = sb("w_down_s", [128, FC, D], BF16)
        SW = 1408
        NSTG = 4
        stg = [sb(f"stg{i}", [128, SW]) for i in range(NSTG)]
        TBK = 2
        x1s = [sb(f"x1s{i}", [128, XW]) for i in range(2 * TBK)]
        halo = [sb(f"halo{i}", [2, XW]) for i in range(2)]
        xsB = [sb(f"xsB{i}", [128, D], BF16) for i in range(2)]
        xsh = sb("xsh", [2, D], BF16)
        h2T = sb("h2T", [128, DC, 258], BF16)
        t1g = [sb(f"t1g{i}", [128, 256]) for i in range(2)]
        t1u = [sb(f"t1u{i}", [128, 256]) for i in range(2)]
        t2g, t2u, t3g, t3u = t1g, t1u, t1g, t1u
        gg = [sb(f"gg{i}", [128, 256]) for i in range(2)]
        aT = [sb(f"aT{i}", [128, 256], BF16) for i in range(3)]
        yout = [sb(f"yout{i}", [128, D]) for i in range(2)]

        sc = Sched()

        def dma(out, in_, reads, writes, key):
            sc.add("sp", I("dma_start", out=out, in_=in_), reads=reads, writes=writes, dma=key)

        dma(identf[:], ident_d[:, :], [], ["identf"], "identf")
        sc.add("dve", I("tensor_copy", out=identb[:], in_=identf[:]), ["identf"], ["identb"])
        dma(gcols[:], gcols_d[:, :], [], ["gcols"], "gcols")
        dma(convc[:].rearrange("p a c -> p (a c)"), convc_d[:, :], [], ["convc"], "convc")
        w_up_v = w_up.rearrange("(c p) n -> p c n", p=128)
        w_down_v = w_down.rearrange("(c p) n -> p c n", p=128)
        k = 0
        cengs = ["dve", "act", "dve", "act"]
        for c in range(DC):
            for q in range(4):
                sl = k % NSTG
                k += 1
                dma(stg[sl][:], w_up_v[:, c, q * SW:(q + 1) * SW], [], [f"stg{sl}"], f"stg{sl}")
                if cengs[sl] == "act":
                    sc.add("act", I("activation", out=w_up_s[:, c, q * SW:(q + 1) * SW], in_=stg[sl][:],
                                                                          func=AF.Copy, scale=gcols[:, 16 + c:17 + c]),
                           [f"stg{sl}", "gcols"], [f"w_up_s{sl}"])
                else:
                    sc.add(cengs[sl], I("tensor_scalar", out=w_up_s[:, c, q * SW:(q + 1) * SW], in0=stg[sl][:],
                                                                                 scalar1=gcols[:, 16 + c:17 + c], scalar2=None, op0=ALU.mult),
                           [f"stg{sl}", "gcols"], [f"w_up_s{sl}"])
        for c in range(FC):
            sl = k % NSTG
            k += 1
            dma(stg[sl][:, 0:D], w_down_v[:, c, :], [], [f"stg{sl}"], f"stg{sl}")
            if cengs[sl] == "act":
                sc.add("act", I("activation", out=w_down_s[:, c, :], in_=stg[sl][:, 0:D], func=AF.Copy),
                       [f"stg{sl}"], [f"w_down_s{sl}"])
            else:
                sc.add(cengs[sl], I("tensor_copy", out=w_down_s[:, c, :], in_=stg[sl][:, 0:D]),
                       [f"stg{sl}"], [f"w_down_s{sl}"])
        WUP = [f"w_up_s{i}" for i in range(NSTG)]
        WDN = [f"w_down_s{i}" for i in range(NSTG)]

        ytiles = []
        tcount = 0
        for g in groups:
            n = g["name"]
            NB = g["NB"]
            x1d = dr["x1" + n]
            t0 = 0
            while t0 < NB:
                nbt = min(TBK, NB - t0)
                T = nbt * 128
                par = tcount % 2
                tcount += 1
                r0 = (1 + t0) * 128
                xb = [x1s[par * TBK + i] for i in range(nbt)]
                xbn = [f"x1s{par * TBK + i}" for i in range(nbt)]
                hl = halo[par]
                HL = f"halo{par}"
                for i in range(nbt):
                    dma(xb[i][:], x1d[r0 + i * 128:r0 + (i + 1) * 128, :], [], [xbn[i]], xbn[i])
                dma(hl[0:1, :], x1d[r0 - 1:r0, :], [], [HL + "a"], HL + "a")
                dma(hl[1:2, :], x1d[r0 + T:r0 + T + 1, :], [], [HL + "b"], HL + "b")
                for i in range(nbt):
                    xi = i % 2
                    sc.add("dve", I("tensor_scalar", out=xsB[xi][:], in0=xb[i][:, 0:D], scalar1=xb[i][:, D:D + 1],
                                                                        scalar2=None, op0=ALU.mult), [xbn[i]], [f"xsB{xi}"])
                    for c in range(DC):
                        sc.add("pe", I("transpose", out=ps[:, 0, :].bitcast(BF16)[:, c * 128:(c + 1) * 128],
                                                                       in_=xsB[xi][:, c * 128:(c + 1) * 128], identity=identb[:]),
                               [f"xsB{xi}", "identb"], ["pb0"])
                    sc.add("dve", I("tensor_copy", out=h2T[:, :, 1 + i * 128:1 + (i + 1) * 128],
                                                               in_=ps[:, 0, :].bitcast(BF16)[:, :].rearrange("p (c t) -> p c t", c=DC)),
                           ["pb0"], ["h2T"])
                sc.add("dve", I("tensor_scalar", out=xsh[:], in0=hl[:, 0:D], scalar1=hl[:, D:D + 1], scalar2=None, op0=ALU.mult),
                       [HL + "a", HL + "b"], ["xsh"])
                for c in range(DC):
                    sc.add("pe", I("transpose", out=ps[:, 1, :].bitcast(BF16)[:, c * 2:(c + 1) * 2],
                                                            in_=xsh[0:2, c * 128:(c + 1) * 128], identity=identb[0:2, 0:2]),
                           ["xsh", "identb"], ["pb1"])
                hv = ps[:, 1, :].bitcast(BF16)[:, 0:16].rearrange("p (c t) -> p c t", c=DC)
                sc.add("dve", I("tensor_copy", out=h2T[:, :, 0:1], in_=hv[:, :, 0:1]), ["pb1"], ["h2T"])
                sc.add("dve", I("tensor_copy", out=h2T[:, :, T + 1:T + 2], in_=hv[:, :, 1:2]), ["pb1"], ["h2T"])

                def up(c, T=T):
                    sl = c % 2
                    for (bank, col0, nm) in ((2 * sl, c * 128, f"pbG{sl}"), (2 * sl + 1, DFF + c * 128, f"pbU{sl}")):
                        for kk in range(DC):
                            sc.add("pe", I("matmul",
                                ps[:, bank, 0:T + 2], lhsT=w_up_s[:, kk, col0:col0 + 128], rhs=h2T[:, kk, 0:T + 2],
                                start=(kk == 0), stop=(kk == DC - 1)), ["h2T"] + WUP, [f"pb{bank}"])

                def conv(c, T=T):
                    sl = c % 2
                    bG, bU = 2 * sl, 2 * sl + 1
                    w0 = convc[:, 0, c:c + 1]
                    w1 = convc[:, 1, c:c + 1]
                    w2 = convc[:, 2, c:c + 1]
                    bb = convc[:, 3, c:c + 1]
                    w0u = convc[:, 0, FC + c:FC + c + 1]
                    w1u = convc[:, 1, FC + c:FC + c + 1]
                    w2u = convc[:, 2, FC + c:FC + c + 1]
                    bbu = convc[:, 3, FC + c:FC + c + 1]
                    sc.add("act", I("activation", out=t1g[sl][:, 0:T], in_=ps[:, bG, 1:T + 1], func=AF.Identity, scale=w1, bias=bb),
                           [f"pb{bG}", "convc"], [f"t1g{sl}"])
                    sc.add("act", I("activation", out=t1u[sl][:, 0:T], in_=ps[:, bU, 1:T + 1], func=AF.Identity, scale=w1u, bias=bbu),
                           [f"pb{bU}", "convc"], [f"t1u{sl}"])
                    sc.add("dve", I("scalar_tensor_tensor", out=t2g[sl][:, 0:T], in0=ps[:, bG, 0:T], scalar=w0, in1=t1g[sl][:, 0:T],
                                                                   op0=ALU.mult, op1=ALU.add), [f"pb{bG}", f"t1g{sl}", "convc"], [f"t1g{sl}"])
                    sc.add("dve", I("scalar_tensor_tensor", out=t3g[sl][:, 0:T], in0=ps[:, bG, 2:T + 2], scalar=w2, in1=t2g[sl][:, 0:T],
                                                                   op0=ALU.mult, op1=ALU.add), [f"pb{bG}", f"t1g{sl}", "convc"], [f"t1g{sl}"])
                    sc.add("dve", I("scalar_tensor_tensor", out=t2u[sl][:, 0:T], in0=ps[:, bU, 0:T], scalar=w0u, in1=t1u[sl][:, 0:T],
                                                                   op0=ALU.mult, op1=ALU.add), [f"pb{bU}", f"t1u{sl}", "convc"], [f"t1u{sl}"])
                    sc.add("dve", I("scalar_tensor_tensor", out=t3u[sl][:, 0:T], in0=ps[:, bU, 2:T + 2], scalar=w2u, in1=t2u[sl][:, 0:T],
                                                                   op0=ALU.mult, op1=ALU.add), [f"pb{bU}", f"t1u{sl}", "convc"], [f"t1u{sl}"])
                    sc.add("act", I("activation", out=gg[sl][:, 0:T], in_=t3g[sl][:, 0:T], func=AF.Gelu_apprx_tanh),
                           [f"t1g{sl}"], [f"gg{sl}"])
                    sc.add("dve", I("tensor_tensor", out=aT[c % 3][:, 0:T], in0=gg[sl][:, 0:T], in1=t3u[sl][:, 0:T], op=ALU.mult),
                           [f"gg{sl}", f"t1u{sl}"], [f"aT{c % 3}"])

                def down(c, nbt=nbt):
                    sl = c % 2
                    for tb in range(nbt):
                        for half in range(2):
                            bank = 4 + tb * 2 + half
                            sc.add("pe", I("matmul",
                                ps[:, bank, :], lhsT=aT[c % 3][:, tb * 128:(tb + 1) * 128], rhs=w_down_s[:, c, half * 512:(half + 1) * 512],
                                start=(c == 0), stop=(c == FC - 1)), [f"aT{c % 3}"] + WDN, [f"pb{bank}"])

                up(0)
                up(1)
                conv(0)
                for c in range(FC):
                    if c + 1 < FC:
                        conv(c + 1)
                    if c + 2 < FC:
                        up(c + 2)
                    down(c)
                for tb in range(nbt):
                    yo = yout[tb]
                    YO = f"yout{tb}"
                    for half in range(2):
                        bank = 4 + tb * 2 + half
                        sc.add("dve", I("tensor_tensor",
                            out=yo[:, half * 512:(half + 1) * 512], in0=ps[:, bank, :], in1=xb[tb][:, half * 512:(half + 1) * 512],
                            op=ALU.add), [f"pb{bank}", xbn[tb]], [YO])
                    ykey = f"y{n}_{t0 + tb}"
                    ytiles.append(ykey)
                    dma(dr["y" + n][(t0 + tb) * 128:(t0 + tb + 1) * 128, :], yo[:], [YO], [ykey], YO + "_out")
                t0 += nbt
        sc.add("sp", I("nop", ), ytiles, [])
        sc.add("act", I("nop", ), ytiles, [])
        sc.emit(nc, "B")
    return nc


def _rope_table(pos, S):
    pos = np.clip(pos, 0, S - 1)
    row = (pos // 64).astype(np.float32)
    col = (pos % 64).astype(np.float32)
    inv = (np.float32(10000.0) ** (-np.arange(0, 32, 2, dtype=np.float32) / np.float32(32))).astype(np.float32)
    ang = np.concatenate([row[:, None] * inv[None, :], col[:, None] * inv[None, :]], axis=1).astype(np.float32)
    return np.concatenate([np.cos(ang), np.sin(ang)], axis=1).astype(np.float32)


def _bias_table():
    slopes = np.exp2(-8.0 * np.arange(1, 9, dtype=np.float32) / 8.0).astype(np.float32)
    s = np.arange(128)[:, None]
    q = np.arange(128)[None, :]
    out = np.zeros((128, 8, 3, 128), np.float32)
    for jj in range(3):
        dist = np.abs(q - (s + (jj - 1) * 128)).astype(np.float32)
        valid = dist <= 128
        for h in range(8):
            out[:, h, jj, :] = np.where(valid, -slopes[h] * dist, NEG)
    return out.reshape(128, 8 * 384)


def prepare_core_inputs(core, groups_full, x_by_group, shared):
    m = dict(shared)
    for g in groups_full:
        n, S, NB = g["name"], g["S"], g["NB"]
        x = x_by_group[n]
        b, qtr = core // 4, core % 4
        start = qtr * NB * 128
        NE = NB + 4
        xe = np.zeros((NE * 128, D), np.float32)
        lo, hi = start - 256, start + NB * 128 + 256
        slo, shi = max(lo, 0), min(hi, S)
        xe[slo - lo:shi - lo] = x[b, slo:shi]
        m["xseq" + n] = np.ascontiguousarray(x[b])
        m["xext" + n] = xe
        m["rseq" + n] = g["rseq"]
        m["rext" + n] = _rope_table(np.arange(lo, hi), S)
        blk0 = lo // 128
        val = np.array([1.0 if 0 <= blk0 + e < S // 128 else 0.0 for e in range(NE)], np.float32)
        m["kval" + n] = np.ascontiguousarray(np.broadcast_to(val[None, :], (128, NE)))
        m["bval" + n] = m["kval" + n]
    return m


def make_shared(inp):
    def col(v, nch):
        return np.asarray(v, np.float32).reshape(nch, 128).T

    gcols = np.concatenate([col(inp["norm_mix_g"][0], 8),
                            col(np.concatenate([inp["out_norm_a_g"][0], inp["out_norm_b_g"][0]]), 8),
                            col(inp["norm_ffn_g"][0], 8)], axis=1)
    cw = np.asarray(inp["conv_w"][0], np.float32)
    cb = np.asarray(inp["conv_b"][0], np.float32)
    convc = np.stack([col(cw[0], 44), col(cw[1], 44), col(cw[2], 44), col(cb, 44)], axis=1).reshape(128, 4 * 44)
    hg = np.concatenate([inp["qnorm_a_g"][0], inp["knorm_a_g"][0], inp["qnorm_b_g"][0], inp["knorm_b_g"][0]]).astype(np.float32)
    shared = dict(
        w_in=np.ascontiguousarray(inp["w_in"][0], dtype=np.float32),
        w_out=np.ascontiguousarray(inp["w_out"][0], dtype=np.float32),
        w_up=np.ascontiguousarray(inp["w_up"][0], dtype=np.float32),
        w_down=np.ascontiguousarray(inp["w_down"][0], dtype=np.float32),
        gcols=np.ascontiguousarray(gcols, dtype=np.float32),
        convc=np.ascontiguousarray(convc, dtype=np.float32),
        headg=np.ascontiguousarray(np.broadcast_to(hg[None, :], (128, 256))),
        sinkr=np.ascontiguousarray(np.broadcast_to(np.asarray(inp["sink_b"][0], np.float32)[None, :], (128, 8))),
        biasT=_bias_table(),
        ident=np.eye(128, dtype=np.float32),
    )
    return shared


def run(inp, SP, SS, runner):
    inp = {k: np.asarray(v) for k, v in inp.items()}
    groups = [dict(name="P", S=SP, NB=SP // 512), dict(name="S", S=SS, NB=SS // 512)]
    for g in groups:
        g["rseq"] = _rope_table(np.arange(g["S"]), g["S"])
    nc = build(groups)
    shared = make_shared(inp)
    xg = {"P": np.asarray(inp["x_prompt"], np.float32), "S": np.asarray(inp["x_sample"], np.float32)}
    in_maps = [prepare_core_inputs(c, groups, xg, shared) for c in range(8)]
    results = runner(nc, in_maps)
    outs = []
    for g, key in zip(groups, ("x_prompt", "x_sample")):
        n, S, NB = g["name"], g["S"], g["NB"]
        y = np.zeros((2, S, D), np.float32)
        for c in range(8):
            b, qtr = c // 4, c % 4
            y[b, qtr * NB * 128:(qtr + 1) * NB * 128] = results[c]["y" + n]
        outs.append(y)
    return tuple(outs)


def kernel(**inputs):
    def runner(nc, in_maps):
        res = run_bass_kernel_spmd(nc, in_maps, core_ids=list(range(8)))
        return res.results
    return run(inputs, 16384, 8192, runner)
```
